# Optimizing a Trainium2 kernel written in Bass

```python
import math
import jax, jax.numpy as jnp
from jax import lax
import numpy as np

D_MODEL = 1024
BATCH = 8
SEQ = 4096
DEPTH = 4

N_A = DEPTH // 2
N_B = DEPTH - N_A
RET_HEADS = 4
RET_QK_DIM = D_MODEL // RET_HEADS
RET_V_DIM = 2 * RET_QK_DIM
RET_V_TOTAL = RET_HEADS * RET_V_DIM
RET_CHUNK = 128
A_IN_WIDTH = 2 * D_MODEL + 2 * RET_V_TOTAL
SWA_HEAD_DIM = 64
SWA_Q_HEADS = D_MODEL // SWA_HEAD_DIM
SWA_KV_HEADS = SWA_Q_HEADS // 8
SWA_GROUP = SWA_Q_HEADS // SWA_KV_HEADS
SWA_Q_WIDTH = SWA_Q_HEADS * SWA_HEAD_DIM
KV_WIDTH = SWA_KV_HEADS * SWA_HEAD_DIM
WINDOW = 128
BLOCK = 128
REL_BUCKETS = 32
REL_MAX_DIST = 128
EPS = 1e-6

kernel_name = "yoco_retention_swa_sink_hybrid"


def rms(x):
    xf = x.astype(jnp.float32)
    return (xf * lax.rsqrt(jnp.mean(xf * xf, axis=-1, keepdims=True) + EPS)).astype(x.dtype)


def rmsnorm(x, g):
    return (rms(x) * g).astype(x.dtype)


def xpos_tables(seq, dim):
    angle = 1.0 / (10000.0 ** jnp.linspace(0.0, 1.0, dim // 2, dtype=jnp.float32))
    angle = jnp.repeat(angle, 2)
    pos = jnp.arange(seq, dtype=jnp.float32)[:, None]
    return jnp.sin(pos * angle), jnp.cos(pos * angle)


def theta_shift(x, sin, cos):
    x1 = x[..., ::2]
    x2 = x[..., 1::2]
    rot = jnp.stack([-x2, x1], axis=-1).reshape(x.shape)
    return x * cos[:, None, :].astype(x.dtype) + rot * sin[:, None, :].astype(x.dtype)


def retention(q, k, v):
    b, s, h, dk = q.shape
    dv = v.shape[-1]
    c = RET_CHUNK
    n = s // c
    dt = q.dtype
    log_gamma = jnp.log(1.0 - 2.0 ** (-5.0 - jnp.arange(h, dtype=jnp.float32)))
    idx = jnp.arange(c, dtype=jnp.float32)
    diff = idx[:, None] - idx[None, :]
    intra_decay = jnp.where(diff[None] >= 0,
                            jnp.exp(jnp.maximum(diff, 0.0)[None] * log_gamma[:, None, None]),
                            0.0).astype(dt)
    q_decay = jnp.exp((idx + 1.0)[:, None] * log_gamma[None, :]).astype(dt)
    k_decay = jnp.exp((c - 1.0 - idx)[:, None] * log_gamma[None, :]).astype(dt)
    chunk_decay = jnp.exp(c * log_gamma).astype(dt)
    k = k * jnp.asarray(dk ** -0.5, dt)

    def to_chunks(t):
        return t.reshape(b, n, c, h, t.shape[-1]).transpose(1, 0, 2, 3, 4)

    def step(state, inp):
        qc, kc, vc = inp
        scores = jnp.einsum('bihd,bjhd->bhij', qc, kc) * intra_decay[None]
        intra = jnp.einsum('bhij,bjhe->bihe', scores, vc)
        inter = jnp.einsum('bihd,bhde->bihe', qc, state) * q_decay[None, :, :, None]
        new_state = (state * chunk_decay[None, :, None, None]
                     + jnp.einsum('bjhd,bjhe->bhde', kc * k_decay[None, :, :, None], vc))
        return new_state, intra + inter

    state0 = jnp.zeros((b, h, dk, dv), dt)
    _, out = lax.scan(step, state0, (to_chunks(q), to_chunks(k), to_chunks(v)))
    return out.transpose(1, 0, 2, 3, 4).reshape(b, s, h, dv)


def retention_layer(x, g_norm, w_in, w_out, sin, cos):
    b, s, _ = x.shape
    proj = rmsnorm(x, g_norm) @ w_in
    q, k, v, gate = jnp.split(proj, [D_MODEL, 2 * D_MODEL, 2 * D_MODEL + RET_V_TOTAL], axis=-1)
    q = theta_shift(q.reshape(b, s, RET_HEADS, RET_QK_DIM), sin, cos)
    k = theta_shift(k.reshape(b, s, RET_HEADS, RET_QK_DIM), sin, cos)
    v = v.reshape(b, s, RET_HEADS, RET_V_DIM)
    o = rms(retention(q, k, v))
    o = o.reshape(b, s, RET_V_TOTAL) * jax.nn.silu(gate)
    return x + o @ w_out


def shared_kv(x, g_norm, w_kv, g_k):
    b, s, _ = x.shape
    nb = s // BLOCK
    kv = rmsnorm(x, g_norm) @ w_kv
    k, v = jnp.split(kv, [KV_WIDTH], axis=-1)
    k = rmsnorm(k.reshape(b, s, SWA_KV_HEADS, SWA_HEAD_DIM), g_k)
    v = v.reshape(b, s, SWA_KV_HEADS, SWA_HEAD_DIM)

    def band(t):
        tb = t.reshape(b, nb, BLOCK, SWA_KV_HEADS, SWA_HEAD_DIM)
        prev = jnp.concatenate([jnp.zeros_like(tb[:, :1]), tb[:, :-1]], axis=1)
        return jnp.concatenate([prev, tb], axis=2)

    return band(k), band(v)


def t5_causal_bucket(dist):
    max_exact = REL_BUCKETS // 2
    dist_f = jnp.maximum(dist, 1).astype(jnp.float32)
    large = max_exact + (jnp.log(dist_f / max_exact) / math.log(REL_MAX_DIST / max_exact)
                         * (REL_BUCKETS - max_exact)).astype(jnp.int32)
    large = jnp.minimum(large, REL_BUCKETS - 1)
    return jnp.where(dist < max_exact, dist, large)


def band_bias_and_mask(rel_bias, nb):
    i = jnp.arange(BLOCK)[:, None]
    j = jnp.arange(2 * BLOCK)[None, :]
    dist = i + BLOCK - j
    bucket = t5_causal_bucket(jnp.maximum(dist, 0))
    bias = rel_bias.astype(jnp.float32)[bucket]
    bias = bias.transpose(2, 0, 1).reshape(SWA_KV_HEADS, SWA_GROUP, BLOCK, 2 * BLOCK)
    in_window = (dist >= 0) & (dist < WINDOW)
    key_pos = jnp.arange(nb)[:, None, None] * BLOCK - BLOCK + j[None]
    mask = in_window[None] & (key_pos >= 0)
    return bias, mask


def swa_layer(x, g_norm, w_in, g_q, sinks, w_out, k_band, v_band, bias, mask):
    b, s, _ = x.shape
    nb = s // BLOCK
    proj = rmsnorm(x, g_norm) @ w_in
    q, gate = jnp.split(proj, [SWA_Q_WIDTH], axis=-1)
    q = rmsnorm(q.reshape(b, nb, BLOCK, SWA_KV_HEADS, SWA_GROUP, SWA_HEAD_DIM), g_q)
    scores = jnp.einsum('bnqkgd,bnjkd->bnkgqj', q, k_band).astype(jnp.float32)
    scores = scores * (SWA_HEAD_DIM ** -0.5) + bias[None, None]
    scores = jnp.where(mask[None, :, None, None], scores, -jnp.inf)
    sink = sinks.astype(jnp.float32).reshape(1, 1, SWA_KV_HEADS, SWA_GROUP, 1, 1)
    m = jnp.maximum(jnp.max(scores, axis=-1, keepdims=True), sink)
    p = jnp.exp(scores - m)
    p = p / (jnp.sum(p, axis=-1, keepdims=True) + jnp.exp(sink - m))
    o = jnp.einsum('bnkgqj,bnjkd->bnqkgd', p.astype(v_band.dtype), v_band)
    o = o.reshape(b, s, SWA_Q_WIDTH) * jax.nn.silu(gate)
    return x + o @ w_out


def setup_inputs(seed: int = 0) -> dict:
    key = jax.random.key(seed)
    ks = jax.random.split(key, 13)

    def nrm(k, shape, scale):
        return jax.random.normal(k, shape, jnp.float32) * scale

    return {
        "x": nrm(ks[0], (BATCH, SEQ, D_MODEL), 1.0),
        "a_norm_g": 1.0 + nrm(ks[1], (N_A, D_MODEL), 0.02),
        "a_w_in": nrm(ks[2], (N_A, D_MODEL, A_IN_WIDTH), D_MODEL ** -0.5),
        "a_w_out": nrm(ks[3], (N_A, RET_V_TOTAL, D_MODEL), RET_V_TOTAL ** -0.5),
        "kv_norm_g": 1.0 + nrm(ks[4], (D_MODEL,), 0.02),
        "w_kv": nrm(ks[5], (D_MODEL, 2 * KV_WIDTH), D_MODEL ** -0.5),
        "k_norm_g": 1.0 + nrm(ks[6], (SWA_HEAD_DIM,), 0.02),
        "rel_bias": nrm(ks[7], (REL_BUCKETS, SWA_Q_HEADS), 0.5),
        "b_norm_g": 1.0 + nrm(ks[8], (N_B, D_MODEL), 0.02),
        "b_w_in": nrm(ks[9], (N_B, D_MODEL, 2 * SWA_Q_WIDTH), D_MODEL ** -0.5),
        "b_q_norm_g": 1.0 + nrm(ks[10], (N_B, SWA_HEAD_DIM), 0.02),
        "b_sinks": nrm(ks[11], (N_B, SWA_Q_HEADS), 1.0),
        "b_w_out": nrm(ks[12], (N_B, SWA_Q_WIDTH, D_MODEL), SWA_Q_WIDTH ** -0.5),
    }


def reference(x, a_norm_g, a_w_in, a_w_out, kv_norm_g, w_kv, k_norm_g, rel_bias,
              b_norm_g, b_w_in, b_q_norm_g, b_sinks, b_w_out):
    s = x.shape[1]
    sin, cos = xpos_tables(s, RET_QK_DIM)
    bias, mask = band_bias_and_mask(rel_bias, s // BLOCK)
    k_band = None
    v_band = None
    for layer in range(DEPTH):
        if layer < N_A:
            x = retention_layer(x, a_norm_g[layer], a_w_in[layer], a_w_out[layer], sin, cos)
        else:
            j = layer - N_A
            if j == 0:
                k_band, v_band = shared_kv(x, kv_norm_g, w_kv, k_norm_g)
            x = swa_layer(x, b_norm_g[j], b_w_in[j], b_q_norm_g[j], b_sinks[j], b_w_out[j],
                          k_band, v_band, bias, mask)
    return x
```

```python
import math
from contextlib import ExitStack

import numpy as np
import concourse.bass as bass
import concourse.mybir as mybir
from concourse.bass_utils import run_bass_kernel_spmd

F32 = mybir.dt.float32
BF16 = mybir.dt.bfloat16
AF = mybir.ActivationFunctionType
ALU = mybir.AluOpType
AX = mybir.AxisListType

D = 1024
EPS = 1e-6
UNIT_SIZES = [16384] * 8 + [16384, 11264, 16384, 8192]
UNIT_OFF = [int(v) for v in np.cumsum([0] + UNIT_SIZES)]
WTOT = UNIT_OFF[-1]
GAMMA = [1.0 - 2.0 ** (-5.0 - h) for h in range(4)]
CHUNK_DECAY = [g ** 128 for g in GAMMA]

C_IDENT = 0
C_BONES = 128
C_DT = 256
C_QD = 768
C_KD = 1280
C_MASK = 1284
C_G = 1540
C_GQ = 1580
C_GK = 1582
C_SINK = 1583
C_GQR = 1599
C_GKR = 1727
C_EPS = 1791
NA = 1792

COMPUTE = ("pe", "act", "dve", "pool")
QUEUES = ("sync",)


class Buf:
    __slots__ = ("name", "w", "r")

    def __init__(self, name):
        self.name = name
        self.w = None
        self.r = {}


class Sched:
    def __init__(self, nc):
        self.nc = nc
        self.streams = {e: [] for e in COMPUTE + QUEUES}
        self.known = {e: {} for e in COMPUTE + QUEUES}
        self.dmacnt = {}
        self.needed = {e: set() for e in COMPUTE}

    def _deps(self, eng, reads, writes):
        waits = {}
        kn = self.known[eng]

        def need(ev, raw):
            if ev is None:
                return
            k, v = ev
            if k == eng and not raw:
                return
            if kn.get(k, 0) >= v:
                return
            if waits.get(k, 0) < v:
                waits[k] = v

        for b in reads:
            need(b.w, True)
        for b in writes:
            need(b.w, False)
            for k, v in b.r.items():
                need((k, v), False)
        for k, v in waits.items():
            kn[k] = v
            if k in self.needed:
                self.needed[k].add(v)
        return list(waits.items())

    def _record(self, ev, reads, writes):
        k, v = ev
        for b in reads:
            if b.r.get(k, 0) < v:
                b.r[k] = v
        for b in writes:
            b.w = ev
            b.r = {}

    def op(self, eng, fn, reads=(), writes=()):
        pr = [b for b in reads if b.name.startswith("psum")]
        if pr:
            reads = [b for b in reads if not b.name.startswith("psum")]
            writes = list(writes) + pr
        waits = self._deps(eng, reads, writes)
        idx = len(self.streams[eng]) + 1
        ev = (eng, idx)
        self.streams[eng].append((fn, waits, ev, 1))
        self._record(ev, reads, writes)

    def dma(self, eng, fns, reads=(), writes=(), sem="dma"):
        if callable(fns):
            fns = [fns]
        waits = self._deps(eng, reads, writes)
        self.dmacnt[sem] = self.dmacnt.get(sem, 0) + 16 * len(fns)
        ev = (sem, self.dmacnt[sem])
        for i, fn in enumerate(fns):
            self.streams[eng].append((fn, waits if i == 0 else [], (sem, 16), 0))
        self._record(ev, reads, writes)

    def wait_for(self, eng, bufs):
        waits = self._deps(eng, bufs, bufs)
        self.streams[eng].append((None, waits, None, 0))

    def barrier(self, bufs_to_reset=()):
        pos = {e: self._last_idx(e) for e in COMPUTE}
        for e in COMPUTE + QUEUES:
            waits = []
            for k, v in pos.items():
                if k == e or v == 0:
                    continue
                if self.known[e].get(k, 0) >= v:
                    continue
                self.known[e][k] = v
                self.needed[k].add(v)
                waits.append((k, v))
            if waits:
                self.streams[e].append((None, waits, None, 0))
        for b in bufs_to_reset:
            b.w = None
            b.r = {}

    def _last_idx(self, e):
        for fn, waits, ev, kind in reversed(self.streams[e]):
            if kind == 1:
                return ev[1]
        return 0

    def op_index_fix(self):
        pass

    def emit(self):
        nc = self.nc
        rank = {}
        for e in COMPUTE:
            ids = sorted(self.needed[e])
            rank[e] = {v: i + 1 for i, v in enumerate(ids)}
        with ExitStack() as es:
            sems = {}
            for k in list(COMPUTE) + list(self.dmacnt.keys()):
                sems[k] = es.enter_context(nc.semaphore("s_" + k))
            block = es.enter_context(nc.Block())

            def run(eng, e):
                for fn, waits, ev, kind in self.streams[eng]:
                    for k, v in waits:
                        val = rank[k][v] if k in rank else v
                        e.wait_ge(sems[k], val)
                    if fn is None:
                        continue
                    ins = fn(e)
                    if kind == 1:
                        if ev[1] in rank[ev[0]]:
                            ins.then_inc(sems[ev[0]], 1)
                    else:
                        ins.then_inc(sems[ev[0]], 16)

            @block.tensor
            def _(e):
                run("pe", e)

            @block.scalar
            def _(e):
                run("act", e)

            @block.vector
            def _(e):
                run("dve", e)

            @block.gpsimd
            def _(e):
                run("pool", e)

            @block.sync
            def _(e):
                run("sync", e)


def build_program(S_len, n_layers=4, debug=False):
    NG = S_len // 512
    nc = bass.Bass("TRN2", target_bir_lowering=False)
    x_d = nc.dram_tensor("x", [S_len, D], F32, kind="ExternalInput").ap()
    y_d = nc.dram_tensor("y", [S_len, D], F32, kind="ExternalOutput").ap()
    wall_d = nc.dram_tensor("wall", [128, WTOT], F32, kind="ExternalInput").ap()
    cst_d = nc.dram_tensor("cst", [128, NA], F32, kind="ExternalInput").ap()
    bias_d = nc.dram_tensor("biasT", [128, 4096], F32, kind="ExternalInput").ap()
    cos_d = nc.dram_tensor("cosT", [128, S_len], F32, kind="ExternalInput").ap()
    sin_d = nc.dram_tensor("sinT", [128, S_len], F32, kind="ExternalInput").ap()
    scr_d = nc.dram_tensor("scr", [128, WTOT], BF16, kind="Internal").ap()
    if debug:
        dbg_kz = nc.dram_tensor("dbg_kz", [128, 4096], BF16, kind="ExternalOutput").ap()
        dbg_vz = nc.dram_tensor("dbg_vz", [128, 4096], BF16, kind="ExternalOutput").ap()
        dbg_q = nc.dram_tensor("dbg_q", [128, 4096], BF16, kind="ExternalOutput").ap()
        dbg_sg = nc.dram_tensor("dbg_sg", [128, 4096], BF16, kind="ExternalOutput").ap()

    es = ExitStack()
    with es:
        def sb(name, shape, dt):
            return es.enter_context(nc.sbuf_tensor("sb_" + name, shape, dt))

        def psum(name, shape, dt):
            return es.enter_context(nc.psum_tensor(name, shape, dt))

        S = Sched(nc)
        bufs = {}

        def B(name):
            if name not in bufs:
                bufs[name] = Buf(name)
            return bufs[name]

        cst = sb("cst", [128, NA], F32)
        ident = sb("ident", [128, 128], BF16)
        bones = sb("bones", [128, 128], BF16)
        onesz = sb("onesz", [128, 2, 128], BF16)
        expB = sb("expB", [128, 2, 2, 1024], BF16)
        state_f = sb("state_f", [128, 2, 4, 2, 512], F32)
        state_b = sb("state_b", [128, 2, 2, 512], BF16)
        kz = sb("kz", [128, 2, 2, 1024], BF16)
        vz = sb("vz", [128, 8, 2, 2, 128], BF16)
        xg = sb("xg", [128, 4, 1024], F32)
        wslot = [sb("wslot0", [128, 16384], BF16), sb("wslot1", [128, 16384], BF16)]
        small = sb("small", [128, 64], F32)
        ARENA = 29184
        arena = sb("arena", [128, ARENA], BF16)
        PS = [psum("ps%d" % i, [128, 1024], F32) for i in range(4)]

        def pbank(i):
            return PS[i // 2][:, (i % 2) * 512:(i % 2) * 512 + 512]

        def pbank_bf(i):
            return pbank(i).bitcast(BF16)

        def PB(i):
            return [B("psum%d" % i)]

        rr = [0]

        def next_bank():
            rr[0] = (rr[0] + 1) % 8
            return rr[0]

        class Carver:
            def __init__(self):
                self.pos = 0

            def bf(self, n):
                a = self.pos
                self.pos += n
                assert self.pos <= ARENA, self.pos
                return arena[:, a:a + n]

            def f32(self, n):
                a = self.pos
                self.pos += 2 * n
                assert self.pos <= ARENA, self.pos
                return arena[:, a:a + 2 * n].bitcast(F32)

        arena_bufs = []

        def AB(name):
            b = B(name)
            if b not in arena_bufs:
                arena_bufs.append(b)
            return b

        SM_SS = 0
        SM_LN = 4
        SM_RS = 8
        SM_SSO = 12
        SM_LNO = 14
        SM_RSO = 16
        SM_MQ2 = 20
        SM_MK2 = 22
        SM_NMQ = 24
        SM_SE = 32

        S.dma("sync", lambda e: e.dma_start(out=cst[:], in_=cst_d[:, :]), writes=[B("cst")], sem="cst")
        cv = Carver()
        btmp = cv.f32(4096)
        S.dma("sync", lambda e: e.dma_start(out=btmp, in_=bias_d[:, :]), writes=[AB("btmp")], sem="btmp")
        S.op("dve", lambda e: e.tensor_copy(out=ident[:], in_=cst[:, C_IDENT:C_IDENT + 128]), reads=[B("cst")], writes=[B("ident")])
        S.op("dve", lambda e: e.tensor_copy(out=bones[:], in_=cst[:, C_BONES:C_BONES + 128]), reads=[B("cst")], writes=[B("bones")])
        S.op("pool", lambda e: e.memset(onesz[:], 0.0), writes=[B("onesz")])
        S.op("pool", lambda e: e.memset(onesz[:, 0, 0:64], 1.0), writes=[B("onesz")])
        S.op("pool", lambda e: e.memset(onesz[:, 1, 64:128], 1.0), writes=[B("onesz")])
        S.op("pool", lambda e: e.memset(kz[:], 0.0), writes=[B("kz")])
        S.op("pool", lambda e: e.memset(vz[:], 0.0), writes=[B("vz")])
        S.op("pool", lambda e: e.memset(state_f[:], 0.0), writes=[B("state_f%d_%d" % (L, h)) for L in range(2) for h in range(4)])
        gq2 = cv.f32(128)
        gk2 = cv.f32(64)
        S.op("dve", lambda e: e.tensor_tensor(out=gq2, in0=cst[:, C_GQR:C_GQR + 128], in1=cst[:, C_GQR:C_GQR + 128], op=ALU.mult), reads=[B("cst")], writes=[AB("gq2")])
        S.op("dve", lambda e: e.tensor_tensor(out=gk2, in0=cst[:, C_GKR:C_GKR + 64], in1=cst[:, C_GKR:C_GKR + 64], op=ALU.mult), reads=[B("cst")], writes=[AB("gk2")])
        S.op("dve", lambda e: e.tensor_reduce(out=small[:, SM_MQ2:SM_MQ2 + 2], in_=gq2.rearrange("p (a b) -> p a b", a=2), axis=AX.X, op=ALU.max), reads=[AB("gq2")], writes=[B("mq2")])
        S.op("dve", lambda e: e.tensor_reduce(out=small[:, SM_MK2:SM_MK2 + 1], in_=gk2, axis=AX.X, op=ALU.max), reads=[AB("gk2")], writes=[B("mk2")])
        S.op("dve", lambda e: e.tensor_scalar(out=small[:, SM_NMQ:SM_NMQ + 2], in0=small[:, SM_MQ2:SM_MQ2 + 2], scalar1=small[:, SM_MK2:SM_MK2 + 1], scalar2=-4.0, op0=ALU.add, op1=ALU.mult), reads=[B("mq2"), B("mk2")], writes=[B("nmq")])
        for j in range(2):
            S.op("act", lambda e, j=j: e.activation(out=small[:, SM_SE + 8 * j:SM_SE + 8 * j + 8], in_=cst[:, C_SINK + 8 * j:C_SINK + 8 * j + 8], func=AF.Exp, bias=small[:, SM_NMQ + j:SM_NMQ + j + 1]), reads=[B("cst"), B("nmq")], writes=[B("sinkexp")])
        S.op("act", lambda e: e.activation(out=btmp, in_=btmp, func=AF.Exp), reads=[AB("btmp")], writes=[AB("btmp")])
        for blk in range(2):
            S.op("dve", lambda e, blk=blk: e.tensor_tensor(
                out=expB[:, blk, :, :].rearrange("p g (c i) -> p (g c) i", i=128),
                in0=btmp[:, blk * 2048:(blk + 1) * 2048].rearrange("p (c i) -> p c i", i=128),
                in1=cst[:, C_MASK + blk * 128:C_MASK + blk * 128 + 128].unsqueeze(1).to_broadcast([128, 16, 128]),
                op=ALU.mult), reads=[AB("btmp"), B("cst")], writes=[B("expB")])

        stage = [cv.f32(4096), cv.f32(4096)]
        pieces_of_unit = []
        for u in range(12):
            pcs = []
            if u < 8:
                L = u // 4
                for pp in range(4):
                    pcs.append((pp * 3072, 3072, [(k * 1536, 1536, L, 2 * pp + k) for k in range(2)]))
                pcs.append((12288, 4096, [(0, 4096, None, None)]))
            elif u in (8, 10):
                j = (u - 8) // 2
                for pp in range(4):
                    pcs.append((pp * 4096, 4096, [(k * 2048, 2048, 3 + j, 2 * pp + k) for k in range(2)]))
            else:
                pcs.append((0, 4096, [(0, 4096, None, None)]))
                pcs.append((4096, 4096, [(0, 4096, None, None)]))
                if u == 9:
                    pcs.append((8192, 3072, [(k * 384, 384, 2, k) for k in range(8)]))
            pieces_of_unit.append(pcs)
        pcnt = 0
        ccnt = 0
        for u in range(12):
            slot = u % 2
            wb = B("wslot%d" % slot)
            for (c0, n, subs) in pieces_of_unit[u]:
                sidx = pcnt % 2
                pcnt += 1
                st = stage[sidx]
                stb = AB("stage%d" % sidx)
                S.dma("sync", lambda e, st=st, c0=c0, n=n, u=u: e.dma_start(out=st[:, 0:n], in_=wall_d[:, UNIT_OFF[u] + c0:UNIT_OFF[u] + c0 + n]), writes=[stb], sem="pp%d" % sidx)
                for (o, m, gi, kt) in subs:
                    dst = wslot[slot][:, c0 + o:c0 + o + m]
                    src = st[:, o:o + m]
                    if gi is None:
                        half = m // 2
                        S.op("pool", lambda e, dst=dst, src=src, half=half: e.tensor_copy(out=dst[:, 0:half], in_=src[:, 0:half]), reads=[stb], writes=[wb])
                        S.op("dve", lambda e, dst=dst, src=src, half=half, m=m: e.tensor_copy(out=dst[:, half:m], in_=src[:, half:m]), reads=[stb], writes=[wb])
                    else:
                        gcol = C_G + gi * 8 + kt
                        if ccnt % 2 == 0:
                            S.op("dve", lambda e, dst=dst, src=src, gcol=gcol: e.tensor_scalar(out=dst, in0=src, scalar1=cst[:, gcol:gcol + 1], scalar2=None, op0=ALU.mult), reads=[stb, B("cst")], writes=[wb])
                        else:
                            S.op("act", lambda e, dst=dst, src=src, gcol=gcol: e.activation(out=dst, in_=src, func=AF.Copy, scale=cst[:, gcol:gcol + 1]), reads=[stb, B("cst")], writes=[wb])
                        ccnt += 1
            sz = UNIT_SIZES[u]
            S.dma("sync", lambda e, u=u, sz=sz, slot=slot: e.dma_start(out=scr_d[:, UNIT_OFF[u]:UNIT_OFF[u] + sz], in_=wslot[slot][:, 0:sz]), reads=[wb], writes=[B("scr%d" % u)], sem="scr%d" % u)

        S.barrier(arena_bufs)

        units_per_group = list(range(0, 4 * min(n_layers, 2))) + ([8, 9] if n_layers >= 3 else []) + ([10, 11] if n_layers >= 4 else [])
        useq = [(g, u) for g in range(NG) for u in units_per_group]
        loaded = [0]

        def load_unit(si):
            if si >= len(useq):
                return
            g, u = useq[si]
            slot = si % 2
            sz = UNIT_SIZES[u]
            half = sz // 2
            S.dma("sync", [lambda e, u=u, slot=slot, half=half: e.dma_start(out=wslot[slot][:, 0:half], in_=scr_d[:, UNIT_OFF[u]:UNIT_OFF[u] + half]),
                           lambda e, u=u, slot=slot, half=half, sz=sz: e.dma_start(out=wslot[slot][:, half:sz], in_=scr_d[:, UNIT_OFF[u] + half:UNIT_OFF[u] + sz])],
                  reads=[B("scr%d" % u)], writes=[B("wslot%d" % slot)], sem="w%d" % slot)

        seqpos = [0]
        issued = [0]

        def prefetch_upto(si):
            while issued[0] <= si:
                load_unit(issued[0])
                issued[0] += 1

        def take_unit():
            si = seqpos[0]
            prefetch_upto(si)
            seqpos[0] += 1
            return si

        def begin_unit():
            si = take_unit()
            prefetch_upto(si + 1)
            return si % 2

        def norm_phase(xnT, xn, junk):
            for c in range(4):
                xb = B("x%d" % c)
                S.op("act", lambda e, c=c: e.activation(out=junk, in_=xg[:, c, :], func=AF.Square, accum_out=small[:, SM_SS + c:SM_SS + c + 1]), reads=[xb], writes=[AB("junk"), B("ss%d" % c)])
                S.op("act", lambda e, c=c: e.activation(out=small[:, SM_LN + c:SM_LN + c + 1], in_=small[:, SM_SS + c:SM_SS + c + 1], func=AF.Ln, scale=1.0 / D, bias=cst[:, C_EPS:C_EPS + 1]), reads=[B("ss%d" % c), B("cst")], writes=[B("ln%d" % c)])
                S.op("act", lambda e, c=c: e.activation(out=small[:, SM_RS + c:SM_RS + c + 1], in_=small[:, SM_LN + c:SM_LN + c + 1], func=AF.Exp, scale=-0.5), reads=[B("ln%d" % c)], writes=[B("rs%d" % c)])
                xs = xn[c % 2]
                S.op("dve", lambda e, c=c, xs=xs: e.tensor_scalar(out=xs, in0=xg[:, c, :], scalar1=small[:, SM_RS + c:SM_RS + c + 1], scalar2=None, op0=ALU.mult), reads=[xb, B("rs%d" % c)], writes=[AB("xn%d" % (c % 2))])
                bk = next_bank()
                pv = pbank_bf(bk)
                for kt in range(8):
                    S.op("pe", lambda e, kt=kt, pv=pv, xs=xs: e.transpose(out=pv[:, kt * 128:(kt + 1) * 128], in_=xs[:, kt * 128:(kt + 1) * 128], identity=ident[:]), reads=[AB("xn%d" % (c % 2)), B("ident")], writes=[*PB(bk)])
                eng = "act" if c % 2 == 0 else "dve"
                if eng == "act":
                    S.op("act", lambda e, c=c, pv=pv: e.activation(out=xnT[:, :, c * 128:(c + 1) * 128], in_=pv.rearrange("p (k t) -> p k t", k=8), func=AF.Copy), reads=[*PB(bk)], writes=[AB("xnT")])
                else:
                    S.op("dve", lambda e, c=c, pv=pv: e.tensor_copy(out=xnT[:, :, c * 128:(c + 1) * 128], in_=pv.rearrange("p (k t) -> p k t", k=8)), reads=[*PB(bk)], writes=[AB("xnT")])

        def load_x_chunk(g, c):
            r0 = (g * 4 + c) * 128
            S.dma("sync", lambda e, c=c, r0=r0: e.dma_start(out=xg[:, c, :], in_=x_d[r0:r0 + 128, :]), writes=[B("x%d" % c)], sem="x%d" % c)

        def store_x_chunk(g, c):
            r0 = (g * 4 + c) * 128
            S.dma("sync", lambda e, c=c, r0=r0: e.dma_start(out=y_d[r0:r0 + 128, :], in_=xg[:, c, :]), reads=[B("x%d" % c)], writes=[B("ydram%d" % c)], sem="st%d" % c)

        for c in range(4):
            load_x_chunk(0, c)

        lh_counter = [0]

        for g in range(NG):
            cv = Carver()
            xn = [cv.bf(1024), cv.bf(1024)]
            junk = cv.bf(1024)
            xnT = cv.bf(4096).rearrange("p (k t) -> p k t", k=8)
            qT = cv.bf(1024).rearrange("p (d t) -> p d t", d=2)
            qdT = cv.bf(1024).rearrange("p (d t) -> p d t", d=2)
            kT = cv.bf(1024).rearrange("p (d t) -> p d t", d=2)
            rt = [cv.f32(512) for _ in range(4)]
            cs = cv.f32(1024).rearrange("p (a t) -> p a t", a=2)
            sg = cv.f32(2048).rearrange("p (c n) -> p c n", c=4)
            k_tm = [cv.bf(256) for _ in range(2)]
            v_sb = [cv.bf(512) for _ in range(2)]
            sT = [cv.bf(128) for _ in range(2)]
            og = [cv.bf(512) for _ in range(2)]
            ogT = [cv.bf(512).rearrange("p (e t) -> p e t", e=4) for _ in range(2)]

            S.dma("sync", [lambda e, g=g: e.dma_start(out=cs[:, 0, :], in_=cos_d[:, g * 512:(g + 1) * 512]),
                           lambda e, g=g: e.dma_start(out=cs[:, 1, :], in_=sin_d[:, g * 512:(g + 1) * 512])], writes=[AB("cs")], sem="cs")

            for L in range(min(n_layers, 2)):
                norm_phase(xnT, xn, junk)
                for h in range(4):
                    slot = begin_unit()
                    W = wslot[slot]
                    wb = B("wslot%d" % slot)
                    Win = W[:, 0:12288].rearrange("p (k c) -> p k c", k=8)
                    Wout = W[:, 12288:16384].rearrange("p (e n) -> p e n", e=4)
                    sbs = lh_counter[0] % 2
                    lh_counter[0] += 1
                    stf = B("state_f%d_%d" % (L, h))
                    stb_ = B("state_b%d" % sbs)
                    if g > 0:
                        S.op("pool", lambda e, L=L, h=h, sbs=sbs: e.tensor_copy(out=state_b[:, sbs, :, :], in_=state_f[:, L, h, :, :]), reads=[stf], writes=[stb_])
                    for tile in range(4):
                        for kt in range(8):
                            S.op("pe", lambda e, tile=tile, kt=kt, Win=Win: e.matmul(pbank(tile), lhsT=Win[:, kt, tile * 128:(tile + 1) * 128], rhs=xnT[:, kt, :], start=(kt == 0), stop=(kt == 7)), reads=[wb, AB("xnT")], writes=[*PB(tile)])
                    for c in range(4):
                        for kt in range(8):
                            S.op("pe", lambda e, c=c, kt=kt, Win=Win: e.matmul(pbank(4 + c), lhsT=xnT[:, kt, c * 128:(c + 1) * 128], rhs=Win[:, kt, 1024:1536], start=(kt == 0), stop=(kt == 7)), reads=[wb, AB("xnT")], writes=[*PB(4 + c)])
                        S.op("act", lambda e, c=c: e.activation(out=sg[:, c, :], in_=pbank(4 + c), func=AF.Silu), reads=[*PB(4 + c)], writes=[AB("sg%d" % c)])
                    for (pe_, po_, dst, nm) in ((0, 1, qT, "qT"), (2, 3, kT, "kT")):
                        S.op("dve", lambda e, pe_=pe_: e.tensor_tensor(out=rt[0], in0=pbank(pe_), in1=cs[:, 0, :], op=ALU.mult), reads=[*PB(pe_), AB("cs")], writes=[AB("rt0")])
                        S.op("dve", lambda e, po_=po_: e.tensor_tensor(out=rt[1], in0=pbank(po_), in1=cs[:, 1, :], op=ALU.mult), reads=[*PB(po_), AB("cs")], writes=[AB("rt1")])
                        S.op("pool", lambda e, dst=dst: e.tensor_tensor(out=dst[:, 0, :], in0=rt[0], in1=rt[1], op=ALU.subtract), reads=[AB("rt0"), AB("rt1")], writes=[AB(nm)])
                        S.op("dve", lambda e, po_=po_: e.tensor_tensor(out=rt[2], in0=pbank(po_), in1=cs[:, 0, :], op=ALU.mult), reads=[*PB(po_), AB("cs")], writes=[AB("rt2")])
                        S.op("dve", lambda e, pe_=pe_: e.tensor_tensor(out=rt[3], in0=pbank(pe_), in1=cs[:, 1, :], op=ALU.mult), reads=[*PB(pe_), AB("cs")], writes=[AB("rt3")])
                        S.op("pool", lambda e, dst=dst: e.tensor_tensor(out=dst[:, 1, :], in0=rt[2], in1=rt[3], op=ALU.add), reads=[AB("rt2"), AB("rt3")], writes=[AB(nm)])
                    for dt in range(2):
                        S.op("pool", lambda e, dt=dt, h=h: e.tensor_tensor(
                            out=qdT[:, dt, :].rearrange("p (c i) -> p c i", c=4),
                            in0=qT[:, dt, :].rearrange("p (c i) -> p c i", c=4),
                            in1=cst[:, C_QD + h * 128:C_QD + h * 128 + 128].unsqueeze(1).to_broadcast([128, 4, 128]),
                            op=ALU.mult), reads=[AB("qT"), B("cst")], writes=[AB("qdT")])
                    for c in range(4):
                        gc = g * 4 + c
                        cb = c % 2
                        csl = slice(c * 128, (c + 1) * 128)
                        MISC = 2
                        ktr = pbank_bf(3)[:, 0:256]
                        scps = pbank(MISC)[:, 128:256]
                        ogtr = pbank_bf(MISC)[:, 512:1024]
                        if gc < S_len // 128 - 1:
                            for dt in range(2):
                                S.op("pe", lambda e, dt=dt, csl=csl, ktr=ktr: e.transpose(out=ktr[:, dt * 128:(dt + 1) * 128], in_=kT[:, dt, csl], identity=ident[:]), reads=[AB("kT"), B("ident")], writes=[*PB(3)])
                            S.op("act", lambda e, cb=cb, h=h, ktr=ktr: e.activation(out=k_tm[cb], in_=ktr, func=AF.Copy, scale=cst[:, C_KD + h:C_KD + h + 1]), reads=[*PB(3), B("cst")], writes=[AB("k_tm%d" % cb)])
                        vbank = 0
                        for kt in range(8):
                            S.op("pe", lambda e, kt=kt, csl=csl, vbank=vbank, Win=Win: e.matmul(pbank(vbank), lhsT=xnT[:, kt, csl], rhs=Win[:, kt, 512:1024], start=(kt == 0), stop=(kt == 7)), reads=[wb, AB("xnT")], writes=[*PB(vbank)])
                        S.op("act", lambda e, cb=cb, vbank=vbank: e.activation(out=v_sb[cb], in_=pbank(vbank), func=AF.Copy), reads=[*PB(vbank)], writes=[AB("v_sb%d" % cb)])
                        for dt in range(2):
                            S.op("pe", lambda e, dt=dt, csl=csl, scps=scps: e.matmul(scps, lhsT=kT[:, dt, csl], rhs=qT[:, dt, csl], start=(dt == 0), stop=(dt == 1)), reads=[AB("kT"), AB("qT")], writes=[*PB(2)])
                        S.op("dve", lambda e, cb=cb, h=h, scps=scps: e.tensor_tensor(out=sT[cb], in0=scps, in1=cst[:, C_DT + h * 128:C_DT + h * 128 + 128], op=ALU.mult), reads=[*PB(2), B("cst")], writes=[AB("sT%d" % cb)])
                        OB = 1
                        has_inter = gc > 0
                        S.op("pe", lambda e, cb=cb, has_inter=has_inter: e.matmul(pbank(OB), lhsT=sT[cb], rhs=v_sb[cb], start=True, stop=(not has_inter)), reads=[AB("sT%d" % cb), AB("v_sb%d" % cb)], writes=[*PB(OB)])
                        if has_inter:
                            for dt in range(2):
                                S.op("pe", lambda e, dt=dt, csl=csl, sbs=sbs: e.matmul(pbank(OB), lhsT=qdT[:, dt, csl], rhs=state_b[:, sbs, dt, :], start=False, stop=(dt == 1)), reads=[AB("qdT"), stb_], writes=[*PB(OB)])
                        if gc < S_len // 128 - 1:
                            for dt in range(2):
                                S.op("pe", lambda e, dt=dt, cb=cb: e.matmul(pbank(4 + dt), lhsT=k_tm[cb][:, dt * 128:(dt + 1) * 128], rhs=v_sb[cb], start=True, stop=True), reads=[AB("k_tm%d" % cb), AB("v_sb%d" % cb)], writes=[*PB(4 + dt)])
                                S.op("dve", lambda e, dt=dt, L=L, h=h: e.scalar_tensor_tensor(out=state_f[:, L, h, dt, :], in0=state_f[:, L, h, dt, :], scalar=CHUNK_DECAY[h], in1=pbank(4 + dt), op0=ALU.mult, op1=ALU.add), reads=[stf, *PB(4 + dt)], writes=[stf])
                            S.op("pool", lambda e, L=L, h=h, sbs=sbs: e.tensor_copy(out=state_b[:, sbs, :, :], in_=state_f[:, L, h, :, :]), reads=[stf], writes=[stb_])
                        S.op("act", lambda e, cb=cb: e.activation(out=junk[:, 0:512], in_=pbank(OB), func=AF.Square, accum_out=small[:, SM_SSO + cb:SM_SSO + cb + 1]), reads=[*PB(OB)], writes=[AB("junk"), B("sso%d" % cb)])
                        S.op("act", lambda e, cb=cb: e.activation(out=small[:, SM_LNO + cb:SM_LNO + cb + 1], in_=small[:, SM_SSO + cb:SM_SSO + cb + 1], func=AF.Ln, scale=1.0 / 512, bias=cst[:, C_EPS:C_EPS + 1]), reads=[B("sso%d" % cb), B("cst")], writes=[B("lno%d" % cb)])
                        S.op("act", lambda e, cb=cb: e.activation(out=small[:, SM_RSO + cb:SM_RSO + cb + 1], in_=small[:, SM_LNO + cb:SM_LNO + cb + 1], func=AF.Exp, scale=-0.5), reads=[B("lno%d" % cb)], writes=[B("rso%d" % cb)])
                        S.op("dve", lambda e, cb=cb, c=c: e.scalar_tensor_tensor(out=og[cb], in0=pbank(OB), scalar=small[:, SM_RSO + cb:SM_RSO + cb + 1], in1=sg[:, c, :], op0=ALU.mult, op1=ALU.mult), reads=[*PB(OB), B("rso%d" % cb), AB("sg%d" % c)], writes=[AB("og%d" % cb)])
                        for et in range(4):
                            S.op("pe", lambda e, et=et, cb=cb, ogtr=ogtr: e.transpose(out=ogtr[:, et * 128:(et + 1) * 128], in_=og[cb][:, et * 128:(et + 1) * 128], identity=ident[:]), reads=[AB("og%d" % cb), B("ident")], writes=[*PB(2)])
                        S.op("act", lambda e, cb=cb, ogtr=ogtr: e.activation(out=ogT[cb], in_=ogtr.rearrange("p (e t) -> p e t", e=4), func=AF.Copy), reads=[*PB(2)], writes=[AB("ogT%d" % cb)])
                        for nh in range(2):
                            for et in range(4):
                                S.op("pe", lambda e, nh=nh, et=et, cb=cb, Wout=Wout: e.matmul(pbank(6 + nh), lhsT=ogT[cb][:, et, :], rhs=Wout[:, et, nh * 512:(nh + 1) * 512], start=(et == 0), stop=(et == 3)), reads=[AB("ogT%d" % cb), wb], writes=[*PB(6 + nh)])
                            S.op("dve", lambda e, nh=nh, c=c: e.tensor_tensor(out=xg[:, c, nh * 512:(nh + 1) * 512], in0=pbank(6 + nh), in1=xg[:, c, nh * 512:(nh + 1) * 512], op=ALU.add), reads=[*PB(6 + nh), B("x%d" % c)], writes=[B("x%d" % c)])
                        if n_layers <= 2 and L == n_layers - 1 and h == 3:
                            store_x_chunk(g, c)
                            if g + 1 < NG:
                                load_x_chunk(g + 1, c)

            if n_layers <= 2:
                continue
            S.barrier(arena_bufs)

            cv = Carver()
            xn = [cv.bf(1024), cv.bf(1024)]
            junk = cv.bf(1024)
            xnT = cv.bf(4096).rearrange("p (k t) -> p k t", k=8)
            qTs = cv.bf(4096).rearrange("p (t n) -> p t n", t=8)
            sq = cv.bf(512)
            lnr = cv.f32(512)
            rstd = cv.f32(512)
            sgT = cv.bf(4096).rearrange("p (t n) -> p t n", t=8)
            ebuf = cv.f32(1024)
            pT = [cv.bf(2048).rearrange("p (b n) -> p b n", b=2) for _ in range(2)]
            dens = [cv.f32(512) for _ in range(2)]
            rg = [cv.f32(512) for _ in range(2)]
            ogTs = cv.bf(1024).rearrange("p (t i) -> p t i", t=8)
            r = g % 2

            def qknorm(bk, nm):
                S.op("act", lambda e: e.activation(out=sq, in_=pbank(bk), func=AF.Square), reads=[*PB(bk)], writes=[AB("sq")])
                b2 = next_bank()
                if b2 == bk:
                    b2 = next_bank()
                S.op("pe", lambda e, b2=b2: e.matmul(pbank(b2), lhsT=bones[:], rhs=sq, start=True, stop=True), reads=[AB("sq"), B("bones")], writes=[*PB(b2)])
                S.op("act", lambda e, b2=b2: e.activation(out=lnr, in_=pbank(b2), func=AF.Ln, scale=1.0 / 64, bias=cst[:, C_EPS:C_EPS + 1]), reads=[*PB(b2), B("cst")], writes=[AB("lnr")])
                S.op("act", lambda e: e.activation(out=rstd, in_=lnr, func=AF.Exp, scale=-0.5), reads=[AB("lnr")], writes=[AB("rstd")])

            for j in range(n_layers - 2):
                siA = take_unit()
                prefetch_upto(siA + 1)
                siB = take_unit()
                slotA = siA % 2
                slotB = siB % 2
                WA = wslot[slotA]
                WBt = wslot[slotB]
                wa = B("wslot%d" % slotA)
                wbb = B("wslot%d" % slotB)
                Wq = WA[:, :].rearrange("p (k c) -> p k c", k=8)
                Wo = WBt[:, 0:8192].rearrange("p (k n) -> p k n", k=8)
                norm_phase(xnT, xn, junk)
                if j == 0:
                    Wkv = WBt[:, 8192:11264].rearrange("p (k c) -> p k c", k=8)
                    for gg in range(2):
                        bk = next_bank()
                        for kt in range(8):
                            S.op("pe", lambda e, gg=gg, kt=kt, bk=bk, Wkv=Wkv: e.matmul(pbank(bk), lhsT=Wkv[:, kt, gg * 128:(gg + 1) * 128], rhs=xnT[:, kt, :], start=(kt == 0), stop=(kt == 7)), reads=[wbb, AB("xnT")], writes=[*PB(bk)])
                        qknorm(bk, "k")
                        for par in range(2):
                            psl = slice(par * 64, par * 64 + 64)
                            S.op("dve", lambda e, gg=gg, par=par, psl=psl, bk=bk, r=r: e.scalar_tensor_tensor(
                                out=kz[psl, par, gg, r * 512:(r + 1) * 512], in0=pbank(bk)[psl, :], scalar=cst[psl, C_GK:C_GK + 1], in1=rstd[psl, :],
                                op0=ALU.mult, op1=ALU.mult), reads=[*PB(bk), AB("rstd"), B("cst")], writes=[B("kz")])
                    for c in range(4):
                        bk = next_bank()
                        vslot = r * 4 + c
                        for kt in range(8):
                            S.op("pe", lambda e, c=c, kt=kt, bk=bk, Wkv=Wkv: e.matmul(pbank(bk)[:, 0:128], lhsT=xnT[:, kt, c * 128:(c + 1) * 128], rhs=Wkv[:, kt, 256:384], start=(kt == 0), stop=(kt == 7)), reads=[wbb, AB("xnT")], writes=[*PB(bk)])
                        S.op("act", lambda e, bk=bk, vslot=vslot: e.activation(out=vz[:, vslot, :, 0, 0:64], in_=pbank(bk)[:, 0:128].rearrange("p (g d) -> p g d", g=2), func=AF.Copy), reads=[*PB(bk)], writes=[B("vz")])
                        S.op("dve", lambda e, bk=bk, vslot=vslot: e.tensor_copy(out=vz[:, vslot, :, 1, 64:128], in_=pbank(bk)[:, 0:128].rearrange("p (g d) -> p g d", g=2)), reads=[*PB(bk)], writes=[B("vz")])
                for t in range(8):
                    bk = next_bank()
                    for kt in range(8):
                        S.op("pe", lambda e, t=t, kt=kt, bk=bk, Wq=Wq: e.matmul(pbank(bk), lhsT=Wq[:, kt, t * 128:(t + 1) * 128], rhs=xnT[:, kt, :], start=(kt == 0), stop=(kt == 7)), reads=[wa, AB("xnT")], writes=[*PB(bk)])
                    qknorm(bk, "q")
                    S.op("dve", lambda e, t=t, bk=bk, j=j: e.scalar_tensor_tensor(out=qTs[:, t, :], in0=pbank(bk), scalar=cst[:, C_GQ + j:C_GQ + j + 1], in1=rstd, op0=ALU.mult, op1=ALU.mult), reads=[*PB(bk), AB("rstd"), B("cst")], writes=[AB("qTs")])
                for t in range(8):
                    bk = next_bank()
                    for kt in range(8):
                        S.op("pe", lambda e, t=t, kt=kt, bk=bk, Wq=Wq: e.matmul(pbank(bk), lhsT=Wq[:, kt, 1024 + t * 128:1024 + (t + 1) * 128], rhs=xnT[:, kt, :], start=(kt == 0), stop=(kt == 7)), reads=[wa, AB("xnT")], writes=[*PB(bk)])
                    S.op("act", lambda e, t=t, bk=bk: e.activation(out=sgT[:, t, :], in_=pbank(bk), func=AF.Silu), reads=[*PB(bk)], writes=[AB("sgT")])
                prefetch_upto(siA + 2)
                scnt = 0
                for c in range(4):
                    gc = g * 4 + c
                    csl = slice(c * 128, (c + 1) * 128)
                    cur_tok = r * 512 + c * 128
                    prev_tok = (cur_tok - 128) % 1024
                    cur_slot = r * 4 + c
                    prev_slot = (cur_slot - 1) % 8
                    blks = [(0, prev_tok, prev_slot), (1, cur_tok, cur_slot)] if gc > 0 else [(1, cur_tok, cur_slot)]
                    for gg in range(2):
                        ps_ = gg
                        for bi, (blk, btok, bslot) in enumerate(blks):
                            dbl = scnt % 2
                            scnt += 1
                            for par in range(2):
                                bk = dbl * 2 + par
                                S.op("pe", lambda e, par=par, gg=gg, btok=btok, bk=bk, csl=csl: e.matmul(pbank(bk), lhsT=kz[:, par, gg, btok:btok + 128], rhs=qTs[:, 4 * gg:4 * gg + 4, csl], start=True, stop=True), reads=[B("kz"), AB("qTs")], writes=[*PB(bk)])
                            for par in range(2):
                                S.op("act", lambda e, dbl=dbl, j=j, par=par: e.activation(out=ebuf[:, par * 512:(par + 1) * 512], in_=pbank(dbl * 2 + par), func=AF.Exp, scale=0.125, bias=small[:, SM_NMQ + j:SM_NMQ + j + 1]), reads=[*PB(dbl * 2 + par), B("nmq")], writes=[AB("ebuf")])
                            eng = "dve" if (scnt % 2 == 0) else "pool"
                            S.op(eng, lambda e, ps_=ps_, bi=bi, blk=blk, gg=gg: e.tensor_tensor(out=pT[ps_][:, bi, :], in0=ebuf, in1=expB[:, blk, gg, :], op=ALU.mult), reads=[AB("ebuf"), B("expB")], writes=[AB("pT%d" % ps_)])
                        nmm = 2 * len(blks)
                        k_ = 0
                        for par in range(2):
                            for bi, (blk, btok, bslot) in enumerate(blks):
                                S.op("pe", lambda e, par=par, bi=bi, bslot=bslot, gg=gg, ps_=ps_, k_=k_, nmm=nmm: e.matmul(pbank(4), lhsT=vz[:, bslot, gg, par, :], rhs=pT[ps_][:, bi, par * 512:(par + 1) * 512], start=(k_ == 0), stop=(k_ == nmm - 1)), reads=[B("vz"), AB("pT%d" % ps_)], writes=[*PB(4)])
                                k_ += 1
                        k_ = 0
                        for par in range(2):
                            for bi, (blk, btok, bslot) in enumerate(blks):
                                S.op("pe", lambda e, par=par, bi=bi, ps_=ps_, k_=k_, nmm=nmm: e.matmul(pbank(5), lhsT=onesz[:, par, :], rhs=pT[ps_][:, bi, par * 512:(par + 1) * 512], start=(k_ == 0), stop=(k_ == nmm - 1)), reads=[B("onesz"), AB("pT%d" % ps_)], writes=[*PB(5)])
                                k_ += 1
                        S.op("dve", lambda e, gg=gg, j=j: e.tensor_tensor(
                            out=dens[gg].rearrange("p (t i) -> p t i", t=4), in0=pbank(5).rearrange("p (t i) -> p t i", t=4),
                            in1=small[:, SM_SE + 8 * j + 4 * gg:SM_SE + 8 * j + 4 * gg + 4].unsqueeze(2).to_broadcast([128, 4, 128]), op=ALU.add),
                            reads=[*PB(5), B("sinkexp")], writes=[AB("dens%d" % gg)])
                        S.op("dve", lambda e, gg=gg: e.reciprocal(out=dens[gg], in_=dens[gg]), reads=[AB("dens%d" % gg)], writes=[AB("dens%d" % gg)])
                        S.op("pool", lambda e, gg=gg, csl=csl: e.tensor_tensor(out=rg[gg].rearrange("p (t i) -> p t i", t=4), in0=dens[gg].rearrange("p (t i) -> p t i", t=4), in1=sgT[:, 4 * gg:4 * gg + 4, csl], op=ALU.mult), reads=[AB("dens%d" % gg), AB("sgT")], writes=[AB("rg%d" % gg)])
                        S.op("dve", lambda e, gg=gg: e.tensor_tensor(out=ogTs[:, 4 * gg:4 * gg + 4, :], in0=pbank(4).rearrange("p (t i) -> p t i", t=4), in1=rg[gg].rearrange("p (t i) -> p t i", t=4), op=ALU.mult), reads=[*PB(4), AB("rg%d" % gg)], writes=[AB("ogTs")])
                    for nh in range(2):
                        for t in range(8):
                            S.op("pe", lambda e, nh=nh, t=t, Wo=Wo: e.matmul(pbank(6 + nh), lhsT=ogTs[:, t, :], rhs=Wo[:, t, nh * 512:(nh + 1) * 512], start=(t == 0), stop=(t == 7)), reads=[AB("ogTs"), wbb], writes=[*PB(6 + nh)])
                        S.op("dve", lambda e, nh=nh, c=c: e.tensor_tensor(out=xg[:, c, nh * 512:(nh + 1) * 512], in0=pbank(6 + nh), in1=xg[:, c, nh * 512:(nh + 1) * 512], op=ALU.add), reads=[*PB(6 + nh), B("x%d" % c)], writes=[B("x%d" % c)])
                    if j == n_layers - 3:
                        store_x_chunk(g, c)
                        if g + 1 < NG:
                            load_x_chunk(g + 1, c)
            S.barrier(arena_bufs)

        if debug:
            S.dma("sync", lambda e: e.dma_start(out=dbg_kz[:, :], in_=kz[:].rearrange("p a b c -> p (a b c)")), reads=[B("kz")], writes=[B("dbg1")], sem="dbg")
            S.dma("sync", lambda e: e.dma_start(out=dbg_vz[:, :], in_=vz[:].rearrange("p a b c d -> p (a b c d)")), reads=[B("vz")], writes=[B("dbg2")], sem="dbg")
            S.dma("sync", lambda e: e.dma_start(out=dbg_q[:, :], in_=qTs.rearrange("p a b -> p (a b)")), reads=[AB("qTs")], writes=[B("dbg3")], sem="dbg")
            S.dma("sync", lambda e: e.dma_start(out=dbg_sg[:, :], in_=sgT.rearrange("p a b -> p (a b)")), reads=[AB("sgT")], writes=[B("dbg4")], sem="dbg")
            S.wait_for("sync", [B("dbg1"), B("dbg2"), B("dbg3"), B("dbg4")])
        S.wait_for("sync", [B("ydram%d" % c) for c in range(4)])
        S.emit()
    return nc


def _t5_bucket(dist):
    max_exact = 16
    dist_f = np.maximum(dist, 1).astype(np.float32)
    large = max_exact + (np.log(dist_f / np.float32(max_exact)) / np.float32(math.log(128 / max_exact)) * np.float32(32 - max_exact)).astype(np.int32)
    large = np.minimum(large, 31)
    return np.where(dist < max_exact, dist, large)


def host_prep(inputs, S_len):
    f32 = np.float32
    a_w_in = np.asarray(inputs["a_w_in"], f32)
    a_w_out = np.asarray(inputs["a_w_out"], f32)
    w_kv = np.asarray(inputs["w_kv"], f32)
    b_w_in = np.asarray(inputs["b_w_in"], f32)
    b_w_out = np.asarray(inputs["b_w_out"], f32)
    wall = np.empty((128, WTOT), f32)

    def pk(w):
        n = w.shape[1]
        return w.reshape(8, 128, n).transpose(1, 0, 2).reshape(128, 8 * n)

    for L in range(2):
        for h in range(4):
            u = L * 4 + h
            W = a_w_in[L]
            q = W[:, h * 256:(h + 1) * 256]
            k = W[:, 1024 + h * 256:1024 + (h + 1) * 256]
            v = W[:, 2048 + h * 512:2048 + (h + 1) * 512]
            gt = W[:, 4096 + h * 512:4096 + (h + 1) * 512]
            blk = np.concatenate([q[:, 0::2], q[:, 1::2], k[:, 0::2], k[:, 1::2], v, gt], axis=1)
            wall[:, UNIT_OFF[u]:UNIT_OFF[u] + 12288] = pk(blk)
            wo = a_w_out[L][h * 512:(h + 1) * 512, :]
            wall[:, UNIT_OFF[u] + 12288:UNIT_OFF[u] + 16384] = wo.reshape(4, 128, 1024).transpose(1, 0, 2).reshape(128, 4096)
    for j in range(2):
        ua = 8 + 2 * j
        ub = 9 + 2 * j
        wall[:, UNIT_OFF[ua]:UNIT_OFF[ua] + 16384] = pk(b_w_in[j])
        wall[:, UNIT_OFF[ub]:UNIT_OFF[ub] + 8192] = pk(b_w_out[j])
        if j == 0:
            kk = w_kv[:, 0:128]
            vv = w_kv[:, 128:256]
            blk = np.concatenate([kk[:, 0:64], kk[:, 0:64], kk[:, 64:128], kk[:, 64:128], vv], axis=1)
            wall[:, UNIT_OFF[ub] + 8192:UNIT_OFF[ub] + 11264] = pk(blk)

    cst = np.zeros((128, NA), f32)
    cst[:, C_IDENT:C_IDENT + 128] = np.eye(128, dtype=f32)
    bo = np.zeros((128, 128), f32)
    bo[0:64, 0:64] = 1.0
    bo[64:128, 64:128] = 1.0
    cst[:, C_BONES:C_BONES + 128] = bo
    idx = np.arange(128)
    for h in range(4):
        lg = math.log(GAMMA[h])
        diff = idx[None, :] - idx[:, None]
        dt_ = np.where(diff >= 0, np.exp(np.maximum(diff, 0) * lg), 0.0) / 16.0
        cst[:, C_DT + h * 128:C_DT + (h + 1) * 128] = dt_.astype(f32)
        cst[:, C_QD + h * 128:C_QD + (h + 1) * 128] = np.exp((idx + 1.0) * lg).astype(f32)[None, :]
        cst[:, C_KD + h] = (np.exp((127.0 - idx) * lg) / 16.0).astype(f32)
    jj = idx[:, None]
    ii = idx[None, :]
    cst[:, C_MASK:C_MASK + 128] = (jj > ii).astype(f32)
    cst[:, C_MASK + 128:C_MASK + 256] = (jj <= ii).astype(f32)
    gains = [inputs["a_norm_g"][0], inputs["a_norm_g"][1], inputs["kv_norm_g"], inputs["b_norm_g"][0], inputs["b_norm_g"][1]]
    for n, gv in enumerate(gains):
        cst[:, C_G + n * 8:C_G + n * 8 + 8] = np.asarray(gv, f32).reshape(8, 128).T
    p64 = idx % 64
    par = idx // 64
    for j in range(2):
        gq = np.asarray(inputs["b_q_norm_g"][j], f32)
        cst[:, C_GQ + j] = gq[p64]
        cst[:, C_GQR + j * 64:C_GQR + (j + 1) * 64] = gq[None, :]
        sk = np.asarray(inputs["b_sinks"][j], f32)
        for gg in range(2):
            for t in range(4):
                cst[:, C_SINK + j * 8 + gg * 4 + t] = sk[8 * gg + 2 * t + par]
    gk = np.asarray(inputs["k_norm_g"], f32)
    cst[:, C_GK] = gk[p64]
    cst[:, C_GKR:C_GKR + 64] = gk[None, :]
    cst[:, C_EPS] = EPS

    rel_bias = np.asarray(inputs["rel_bias"], f32)
    biasT = np.zeros((128, 2, 2, 2, 4, 128), f32)
    for blk in range(2):
        dist = (ii + 128 - jj) if blk == 0 else (ii - jj)
        bucket = _t5_bucket(np.maximum(dist, 0))
        for gg in range(2):
            for pr in range(2):
                for t in range(4):
                    hh = 8 * gg + 2 * t + pr
                    biasT[:, blk, gg, pr, t, :] = rel_bias[bucket, hh]
    biasT = biasT.reshape(128, 4096)

    angle = (1.0 / (np.float32(10000.0) ** np.linspace(0.0, 1.0, 128, dtype=f32))).astype(f32)
    pos = np.arange(S_len, dtype=f32)
    ang = (angle[:, None] * pos[None, :]).astype(f32)
    cosT = np.cos(ang).astype(f32)
    sinT = np.sin(ang).astype(f32)
    return {"wall": wall, "cst": cst, "biasT": biasT, "cosT": cosT, "sinT": sinT}


_CACHE = {}


def kernel(**inputs):
    x = np.asarray(inputs["x"], np.float32)
    nb, S_len, _ = x.shape
    shared = host_prep(inputs, S_len)
    key = (S_len, 4)
    if key not in _CACHE:
        _CACHE[key] = build_program(S_len, 4)
    nc = _CACHE[key]
    in_maps = []
    for b in range(nb):
        m = dict(shared)
        m["x"] = np.ascontiguousarray(x[b])
        in_maps.append(m)
    res = run_bass_kernel_spmd(nc, in_maps, core_ids=list(range(nb)))
    out = np.stack([np.asarray(r["y"], np.float32) for r in res.results], axis=0)
    return out
```

```python
import math
from contextlib import ExitStack

import numpy as np
import concourse.bass as bass
import concourse.mybir as mybir
from concourse.bass_utils import run_bass_kernel_spmd

F32 = mybir.dt.float32
BF16 = mybir.dt.bfloat16
AF = mybir.ActivationFunctionType
ALU = mybir.AluOpType
AX = mybir.AxisListType

D = 1024
EPS = 1e-6
UNIT_SIZES = [16384] * 8 + [16384, 11264, 16384, 8192]
UNIT_OFF = [int(v) for v in np.cumsum([0] + UNIT_SIZES)]
WTOT = UNIT_OFF[-1]
GAMMA = [1.0 - 2.0 ** (-5.0 - h) for h in range(4)]
CHUNK_DECAY = [g ** 128 for g in GAMMA]

C_IDENT = 0
C_BONES = 128
C_DT = 256
C_QD = 768
C_KD = 1280
C_MASK = 1284
C_G = 1540
C_GQ = 1580
C_GK = 1582
C_SINK = 1583
C_GQR = 1599
C_GKR = 1727
C_EPS = 1791
NA = 1792

COMPUTE = ("pe", "act", "dve", "pool")
QUEUES = ("sync",)


class Buf:
    __slots__ = ("name", "w", "r")

    def __init__(self, name):
        self.name = name
        self.w = None
        self.r = {}


class Sched:
    def __init__(self, nc):
        self.nc = nc
        self.streams = {e: [] for e in COMPUTE + QUEUES}
        self.known = {e: {} for e in COMPUTE + QUEUES}
        self.dmacnt = {}
        self.needed = {e: set() for e in COMPUTE}

    def _deps(self, eng, reads, writes):
        waits = {}
        kn = self.known[eng]

        def need(ev, raw):
            if ev is None:
                return
            k, v = ev
            if k == eng and not raw:
                return
            if kn.get(k, 0) >= v:
                return
            if waits.get(k, 0) < v:
                waits[k] = v

        for b in reads:
            need(b.w, True)
        for b in writes:
            need(b.w, False)
            for k, v in b.r.items():
                need((k, v), False)
        for k, v in waits.items():
            kn[k] = v
            if k in self.needed:
                self.needed[k].add(v)
        return list(waits.items())

    def _record(self, ev, reads, writes):
        k, v = ev
        for b in reads:
            if b.r.get(k, 0) < v:
                b.r[k] = v
        for b in writes:
            b.w = ev
            b.r = {}

    def op(self, eng, fn, reads=(), writes=()):
        pr = [b for b in reads if b.name.startswith("psum")]
        if pr:
            reads = [b for b in reads if not b.name.startswith("psum")]
            writes = list(writes) + pr
        waits = self._deps(eng, reads, writes)
        idx = len(self.streams[eng]) + 1
        ev = (eng, idx)
        self.streams[eng].append((fn, waits, ev, 1))
        self._record(ev, reads, writes)

    def dma(self, eng, fns, reads=(), writes=(), sem="dma"):
        if callable(fns):
            fns = [fns]
        waits = self._deps(eng, reads, writes)
        self.dmacnt[sem] = self.dmacnt.get(sem, 0) + 16 * len(fns)
        ev = (sem, self.dmacnt[sem])
        for i, fn in enumerate(fns):
            self.streams[eng].append((fn, waits if i == 0 else [], (sem, 16), 0))
        self._record(ev, reads, writes)

    def wait_for(self, eng, bufs):
        waits = self._deps(eng, bufs, bufs)
        self.streams[eng].append((None, waits, None, 0))

    def barrier(self, bufs_to_reset=()):
        pos = {e: self._last_idx(e) for e in COMPUTE}
        for e in COMPUTE + QUEUES:
            waits = []
            for k, v in pos.items():
                if k == e or v == 0:
                    continue
                if self.known[e].get(k, 0) >= v:
                    continue
                self.known[e][k] = v
                self.needed[k].add(v)
                waits.append((k, v))
            if waits:
                self.streams[e].append((None, waits, None, 0))
        for b in bufs_to_reset:
            b.w = None
            b.r = {}

    def _last_idx(self, e):
        for fn, waits, ev, kind in reversed(self.streams[e]):
            if kind == 1:
                return ev[1]
        return 0

    def op_index_fix(self):
        pass

    def emit(self):
        nc = self.nc
        rank = {}
        for e in COMPUTE:
            ids = sorted(self.needed[e])
            rank[e] = {v: i + 1 for i, v in enumerate(ids)}
        with ExitStack() as es:
            sems = {}
            for k in list(COMPUTE) + list(self.dmacnt.keys()):
                sems[k] = es.enter_context(nc.semaphore("s_" + k))
            block = es.enter_context(nc.Block())

            def run(eng, e):
                for fn, waits, ev, kind in self.streams[eng]:
                    for k, v in waits:
                        val = rank[k][v] if k in rank else v
                        e.wait_ge(sems[k], val)
                    if fn is None:
                        continue
                    ins = fn(e)
                    if kind == 1:
                        if ev[1] in rank[ev[0]]:
                            ins.then_inc(sems[ev[0]], 1)
                    else:
                        ins.then_inc(sems[ev[0]], 16)

            @block.tensor
            def _(e):
                run("pe", e)

            @block.scalar
            def _(e):
                run("act", e)

            @block.vector
            def _(e):
                run("dve", e)

            @block.gpsimd
            def _(e):
                run("pool", e)

            @block.sync
            def _(e):
                run("sync", e)


def build_program(S_len, n_layers=4, debug=False):
    NG = S_len // 512
    nc = bass.Bass("TRN2", target_bir_lowering=False)
    x_d = nc.dram_tensor("x", [S_len, D], F32, kind="ExternalInput").ap()
    y_d = nc.dram_tensor("y", [S_len, D], F32, kind="ExternalOutput").ap()
    wall_d = nc.dram_tensor("wall", [128, WTOT], F32, kind="ExternalInput").ap()
    cst_d = nc.dram_tensor("cst", [128, NA], F32, kind="ExternalInput").ap()
    bias_d = nc.dram_tensor("biasT", [128, 4096], F32, kind="ExternalInput").ap()
    cos_d = nc.dram_tensor("cosT", [128, S_len], F32, kind="ExternalInput").ap()
    sin_d = nc.dram_tensor("sinT", [128, S_len], F32, kind="ExternalInput").ap()
    scr_d = nc.dram_tensor("scr", [128, WTOT], BF16, kind="Internal").ap()
    if debug:
        dbg_kz = nc.dram_tensor("dbg_kz", [128, 4096], BF16, kind="ExternalOutput").ap()
        dbg_vz = nc.dram_tensor("dbg_vz", [128, 4096], BF16, kind="ExternalOutput").ap()
        dbg_q = nc.dram_tensor("dbg_q", [128, 4096], BF16, kind="ExternalOutput").ap()
        dbg_sg = nc.dram_tensor("dbg_sg", [128, 4096], BF16, kind="ExternalOutput").ap()

    es = ExitStack()
    with es:
        def sb(name, shape, dt):
            return es.enter_context(nc.sbuf_tensor("sb_" + name, shape, dt))

        def psum(name, shape, dt):
            return es.enter_context(nc.psum_tensor(name, shape, dt))

        S = Sched(nc)
        bufs = {}

        def B(name):
            if name not in bufs:
                bufs[name] = Buf(name)
            return bufs[name]

        cst = sb("cst", [128, NA], F32)
        ident = sb("ident", [128, 128], BF16)
        bones = sb("bones", [128, 128], BF16)
        onesz = sb("onesz", [128, 2, 128], BF16)
        expB = sb("expB", [128, 2, 2, 1024], BF16)
        state_f = sb("state_f", [128, 2, 4, 2, 512], F32)
        state_b = sb("state_b", [128, 2, 2, 512], BF16)
        kz = sb("kz", [128, 2, 2, 1024], BF16)
        vz = sb("vz", [128, 8, 2, 2, 128], BF16)
        xg = sb("xg", [128, 4, 1024], F32)
        wslot = [sb("wslot0", [128, 16384], BF16), sb("wslot1", [128, 16384], BF16)]
        small = sb("small", [128, 64], F32)
        ARENA = 29184
        arena = sb("arena", [128, ARENA], BF16)
        PS = [psum("ps%d" % i, [128, 1024], F32) for i in range(4)]

        def pbank(i):
            return PS[i // 2][:, (i % 2) * 512:(i % 2) * 512 + 512]

        def pbank_bf(i):
            return pbank(i).bitcast(BF16)

        def PB(i):
            return [B("psum%d" % i)]

        rr = [0]

        def next_bank():
            rr[0] = (rr[0] + 1) % 8
            return rr[0]

        class Carver:
            def __init__(self):
                self.pos = 0

            def bf(self, n):
                a = self.pos
                self.pos += n
                assert self.pos <= ARENA, self.pos
                return arena[:, a:a + n]

            def f32(self, n):
                a = self.pos
                self.pos += 2 * n
                assert self.pos <= ARENA, self.pos
                return arena[:, a:a + 2 * n].bitcast(F32)

        arena_bufs = []

        def AB(name):
            b = B(name)
            if b not in arena_bufs:
                arena_bufs.append(b)
            return b

        SM_SS = 0
        SM_LN = 4
        SM_RS = 8
        SM_SSO = 12
        SM_LNO = 14
        SM_RSO = 16
        SM_MQ2 = 20
        SM_MK2 = 22
        SM_NMQ = 24
        SM_SE = 32

        S.dma("sync", lambda e: e.dma_start(out=cst[:], in_=cst_d[:, :]), writes=[B("cst")], sem="cst")
        cv = Carver()
        btmp = cv.f32(4096)
        S.dma("sync", lambda e: e.dma_start(out=btmp, in_=bias_d[:, :]), writes=[AB("btmp")], sem="btmp")
        S.op("dve", lambda e: e.tensor_copy(out=ident[:], in_=cst[:, C_IDENT:C_IDENT + 128]), reads=[B("cst")], writes=[B("ident")])
        S.op("dve", lambda e: e.tensor_copy(out=bones[:], in_=cst[:, C_BONES:C_BONES + 128]), reads=[B("cst")], writes=[B("bones")])
        S.op("pool", lambda e: e.memset(onesz[:], 0.0), writes=[B("onesz")])
        S.op("pool", lambda e: e.memset(onesz[:, 0, 0:64], 1.0), writes=[B("onesz")])
        S.op("pool", lambda e: e.memset(onesz[:, 1, 64:128], 1.0), writes=[B("onesz")])
        S.op("pool", lambda e: e.memset(kz[:], 0.0), writes=[B("kz")])
        S.op("pool", lambda e: e.memset(vz[:], 0.0), writes=[B("vz")])
        S.op("pool", lambda e: e.memset(state_f[:], 0.0), writes=[B("state_f%d_%d" % (L, h)) for L in range(2) for h in range(4)])
        gq2 = cv.f32(128)
        gk2 = cv.f32(64)
        S.op("dve", lambda e: e.tensor_tensor(out=gq2, in0=cst[:, C_GQR:C_GQR + 128], in1=cst[:, C_GQR:C_GQR + 128], op=ALU.mult), reads=[B("cst")], writes=[AB("gq2")])
        S.op("dve", lambda e: e.tensor_tensor(out=gk2, in0=cst[:, C_GKR:C_GKR + 64], in1=cst[:, C_GKR:C_GKR + 64], op=ALU.mult), reads=[B("cst")], writes=[AB("gk2")])
        S.op("dve", lambda e: e.tensor_reduce(out=small[:, SM_MQ2:SM_MQ2 + 2], in_=gq2.rearrange("p (a b) -> p a b", a=2), axis=AX.X, op=ALU.max), reads=[AB("gq2")], writes=[B("mq2")])
        S.op("dve", lambda e: e.tensor_reduce(out=small[:, SM_MK2:SM_MK2 + 1], in_=gk2, axis=AX.X, op=ALU.max), reads=[AB("gk2")], writes=[B("mk2")])
        S.op("dve", lambda e: e.tensor_scalar(out=small[:, SM_NMQ:SM_NMQ + 2], in0=small[:, SM_MQ2:SM_MQ2 + 2], scalar1=small[:, SM_MK2:SM_MK2 + 1], scalar2=-4.0, op0=ALU.add, op1=ALU.mult), reads=[B("mq2"), B("mk2")], writes=[B("nmq")])
        for j in range(2):
            S.op("act", lambda e, j=j: e.activation(out=small[:, SM_SE + 8 * j:SM_SE + 8 * j + 8], in_=cst[:, C_SINK + 8 * j:C_SINK + 8 * j + 8], func=AF.Exp, bias=small[:, SM_NMQ + j:SM_NMQ + j + 1]), reads=[B("cst"), B("nmq")], writes=[B("sinkexp")])
        S.op("act", lambda e: e.activation(out=btmp, in_=btmp, func=AF.Exp), reads=[AB("btmp")], writes=[AB("btmp")])
        for blk in range(2):
            S.op("dve", lambda e, blk=blk: e.tensor_tensor(
                out=expB[:, blk, :, :].rearrange("p g (c i) -> p (g c) i", i=128),
                in0=btmp[:, blk * 2048:(blk + 1) * 2048].rearrange("p (c i) -> p c i", i=128),
                in1=cst[:, C_MASK + blk * 128:C_MASK + blk * 128 + 128].unsqueeze(1).to_broadcast([128, 16, 128]),
                op=ALU.mult), reads=[AB("btmp"), B("cst")], writes=[B("expB")])

        stage = [cv.f32(4096), cv.f32(4096)]
        pieces_of_unit = []
        for u in range(12):
            pcs = []
            if u < 8:
                L = u // 4
                for pp in range(4):
                    pcs.append((pp * 3072, 3072, [(k * 1536, 1536, L, 2 * pp + k) for k in range(2)]))
                pcs.append((12288, 4096, [(0, 4096, None, None)]))
            elif u in (8, 10):
                j = (u - 8) // 2
                for pp in range(4):
                    pcs.append((pp * 4096, 4096, [(k * 2048, 2048, 3 + j, 2 * pp + k) for k in range(2)]))
            else:
                pcs.append((0, 4096, [(0, 4096, None, None)]))
                pcs.append((4096, 4096, [(0, 4096, None, None)]))
                if u == 9:
                    pcs.append((8192, 3072, [(k * 384, 384, 2, k) for k in range(8)]))
            pieces_of_unit.append(pcs)
        pcnt = 0
        ccnt = 0
        for u in range(12):
            slot = u % 2
            wb = B("wslot%d" % slot)
            for (c0, n, subs) in pieces_of_unit[u]:
                sidx = pcnt % 2
                pcnt += 1
                st = stage[sidx]
                stb = AB("stage%d" % sidx)
                S.dma("sync", lambda e, st=st, c0=c0, n=n, u=u: e.dma_start(out=st[:, 0:n], in_=wall_d[:, UNIT_OFF[u] + c0:UNIT_OFF[u] + c0 + n]), writes=[stb], sem="pp%d" % sidx)
                for (o, m, gi, kt) in subs:
                    dst = wslot[slot][:, c0 + o:c0 + o + m]
                    src = st[:, o:o + m]
                    if gi is None:
                        half = m // 2
                        S.op("pool", lambda e, dst=dst, src=src, half=half: e.tensor_copy(out=dst[:, 0:half], in_=src[:, 0:half]), reads=[stb], writes=[wb])
                        S.op("dve", lambda e, dst=dst, src=src, half=half, m=m: e.tensor_copy(out=dst[:, half:m], in_=src[:, half:m]), reads=[stb], writes=[wb])
                    else:
                        gcol = C_G + gi * 8 + kt
                        if ccnt % 2 == 0:
                            S.op("dve", lambda e, dst=dst, src=src, gcol=gcol: e.tensor_scalar(out=dst, in0=src, scalar1=cst[:, gcol:gcol + 1], scalar2=None, op0=ALU.mult), reads=[stb, B("cst")], writes=[wb])
                        else:
                            S.op("act", lambda e, dst=dst, src=src, gcol=gcol: e.activation(out=dst, in_=src, func=AF.Copy, scale=cst[:, gcol:gcol + 1]), reads=[stb, B("cst")], writes=[wb])
                        ccnt += 1
            sz = UNIT_SIZES[u]
            S.dma("sync", lambda e, u=u, sz=sz, slot=slot: e.dma_start(out=scr_d[:, UNIT_OFF[u]:UNIT_OFF[u] + sz], in_=wslot[slot][:, 0:sz]), reads=[wb], writes=[B("scr%d" % u)], sem="scr%d" % u)

        S.barrier(arena_bufs)

        units_per_group = list(range(0, 4 * min(n_layers, 2))) + ([8, 9] if n_layers >= 3 else []) + ([10, 11] if n_layers >= 4 else [])
        useq = [(g, u) for g in range(NG) for u in units_per_group]
        loaded = [0]

        def load_unit(si):
            if si >= len(useq):
                return
            g, u = useq[si]
            slot = si % 2
            sz = UNIT_SIZES[u]
            half = sz // 2
            S.dma("sync", [lambda e, u=u, slot=slot, half=half: e.dma_start(out=wslot[slot][:, 0:half], in_=scr_d[:, UNIT_OFF[u]:UNIT_OFF[u] + half]),
                           lambda e, u=u, slot=slot, half=half, sz=sz: e.dma_start(out=wslot[slot][:, half:sz], in_=scr_d[:, UNIT_OFF[u] + half:UNIT_OFF[u] + sz])],
                  reads=[B("scr%d" % u)], writes=[B("wslot%d" % slot)], sem="w%d" % slot)

        seqpos = [0]
        issued = [0]

        def prefetch_upto(si):
            while issued[0] <= si:
                load_unit(issued[0])
                issued[0] += 1

        def take_unit():
            si = seqpos[0]
            prefetch_upto(si)
            seqpos[0] += 1
            return si

        def begin_unit():
            si = take_unit()
            prefetch_upto(si + 1)
            return si % 2

        def norm_phase(xnT, xn, junk, junkname="junk"):
            for c in range(4):
                xb = B("x%d" % c)
                S.op("act", lambda e, c=c: e.activation(out=junk, in_=xg[:, c, :], func=AF.Square, accum_out=small[:, SM_SS + c:SM_SS + c + 1]), reads=[xb], writes=[AB(junkname), B("ss%d" % c)])
                S.op("act", lambda e, c=c: e.activation(out=small[:, SM_LN + c:SM_LN + c + 1], in_=small[:, SM_SS + c:SM_SS + c + 1], func=AF.Ln, scale=1.0 / D, bias=cst[:, C_EPS:C_EPS + 1]), reads=[B("ss%d" % c), B("cst")], writes=[B("ln%d" % c)])
                S.op("act", lambda e, c=c: e.activation(out=small[:, SM_RS + c:SM_RS + c + 1], in_=small[:, SM_LN + c:SM_LN + c + 1], func=AF.Exp, scale=-0.5), reads=[B("ln%d" % c)], writes=[B("rs%d" % c)])
                xs = xn[c % 2]
                S.op("dve", lambda e, c=c, xs=xs: e.tensor_scalar(out=xs, in0=xg[:, c, :], scalar1=small[:, SM_RS + c:SM_RS + c + 1], scalar2=None, op0=ALU.mult), reads=[xb, B("rs%d" % c)], writes=[AB("xn%d" % (c % 2))])
                bk = next_bank()
                pv = pbank_bf(bk)
                for kt in range(8):
                    S.op("pe", lambda e, kt=kt, pv=pv, xs=xs: e.transpose(out=pv[:, kt * 128:(kt + 1) * 128], in_=xs[:, kt * 128:(kt + 1) * 128], identity=ident[:]), reads=[AB("xn%d" % (c % 2)), B("ident")], writes=[*PB(bk)])
                eng = "act" if c % 2 == 0 else "dve"
                if eng == "act":
                    S.op("act", lambda e, c=c, pv=pv: e.activation(out=xnT[:, :, c * 128:(c + 1) * 128], in_=pv.rearrange("p (k t) -> p k t", k=8), func=AF.Copy), reads=[*PB(bk)], writes=[AB("xnT")])
                else:
                    S.op("dve", lambda e, c=c, pv=pv: e.tensor_copy(out=xnT[:, :, c * 128:(c + 1) * 128], in_=pv.rearrange("p (k t) -> p k t", k=8)), reads=[*PB(bk)], writes=[AB("xnT")])

        def load_x_chunk(g, c):
            r0 = (g * 4 + c) * 128
            S.dma("sync", lambda e, c=c, r0=r0: e.dma_start(out=xg[:, c, :], in_=x_d[r0:r0 + 128, :]), writes=[B("x%d" % c)], sem="x%d" % c)

        def store_x_chunk(g, c):
            r0 = (g * 4 + c) * 128
            S.dma("sync", lambda e, c=c, r0=r0: e.dma_start(out=y_d[r0:r0 + 128, :], in_=xg[:, c, :]), reads=[B("x%d" % c)], writes=[B("ydram%d" % c)], sem="st%d" % c)

        for c in range(4):
            load_x_chunk(0, c)

        lh_counter = [0]

        for g in range(NG):
            cv = Carver()
            xn = [cv.bf(1024), cv.bf(1024)]
            junk = cv.bf(1024)
            xnT = cv.bf(4096).rearrange("p (k t) -> p k t", k=8)
            qT = cv.bf(1024).rearrange("p (d t) -> p d t", d=2)
            qdT = cv.bf(1024).rearrange("p (d t) -> p d t", d=2)
            kT = cv.bf(1024).rearrange("p (d t) -> p d t", d=2)
            rt = [cv.f32(512) for _ in range(4)]
            cs = cv.f32(1024).rearrange("p (a t) -> p a t", a=2)
            sg = cv.f32(2048).rearrange("p (c n) -> p c n", c=4)
            k_tm = [cv.bf(256) for _ in range(2)]
            v_sb = [cv.bf(512) for _ in range(2)]
            sT = [cv.bf(128) for _ in range(2)]
            og = [cv.bf(512) for _ in range(2)]
            ogT = [cv.bf(512).rearrange("p (e t) -> p e t", e=4) for _ in range(2)]

            S.dma("sync", [lambda e, g=g: e.dma_start(out=cs[:, 0, :], in_=cos_d[:, g * 512:(g + 1) * 512]),
                           lambda e, g=g: e.dma_start(out=cs[:, 1, :], in_=sin_d[:, g * 512:(g + 1) * 512])], writes=[AB("cs")], sem="cs")

            for L in range(min(n_layers, 2)):
                norm_phase(xnT, xn, junk)
                for h in range(4):
                    slot = begin_unit()
                    W = wslot[slot]
                    wb = B("wslot%d" % slot)
                    Win = W[:, 0:12288].rearrange("p (k c) -> p k c", k=8)
                    Wout = W[:, 12288:16384].rearrange("p (e n) -> p e n", e=4)
                    sbs = lh_counter[0] % 2
                    lh_counter[0] += 1
                    stf = B("state_f%d_%d" % (L, h))
                    stb_ = B("state_b%d" % sbs)
                    if g > 0:
                        S.op("pool", lambda e, L=L, h=h, sbs=sbs: e.tensor_copy(out=state_b[:, sbs, :, :], in_=state_f[:, L, h, :, :]), reads=[stf], writes=[stb_])
                    for tile in range(4):
                        for kt in range(8):
                            S.op("pe", lambda e, tile=tile, kt=kt, Win=Win: e.matmul(pbank(tile), lhsT=Win[:, kt, tile * 128:(tile + 1) * 128], rhs=xnT[:, kt, :], start=(kt == 0), stop=(kt == 7)), reads=[wb, AB("xnT")], writes=[*PB(tile)])
                    for c in range(4):
                        for kt in range(8):
                            S.op("pe", lambda e, c=c, kt=kt, Win=Win: e.matmul(pbank(4 + c), lhsT=xnT[:, kt, c * 128:(c + 1) * 128], rhs=Win[:, kt, 1024:1536], start=(kt == 0), stop=(kt == 7)), reads=[wb, AB("xnT")], writes=[*PB(4 + c)])
                        S.op("act", lambda e, c=c: e.activation(out=sg[:, c, :], in_=pbank(4 + c), func=AF.Silu), reads=[*PB(4 + c)], writes=[AB("sg%d" % c)])
                    for (pe_, po_, dst, nm) in ((0, 1, qT, "qT"), (2, 3, kT, "kT")):
                        S.op("dve", lambda e, pe_=pe_: e.tensor_tensor(out=rt[0], in0=pbank(pe_), in1=cs[:, 0, :], op=ALU.mult), reads=[*PB(pe_), AB("cs")], writes=[AB("rt0")])
                        S.op("dve", lambda e, po_=po_: e.tensor_tensor(out=rt[1], in0=pbank(po_), in1=cs[:, 1, :], op=ALU.mult), reads=[*PB(po_), AB("cs")], writes=[AB("rt1")])
                        S.op("pool", lambda e, dst=dst: e.tensor_tensor(out=dst[:, 0, :], in0=rt[0], in1=rt[1], op=ALU.subtract), reads=[AB("rt0"), AB("rt1")], writes=[AB(nm)])
                        S.op("dve", lambda e, po_=po_: e.tensor_tensor(out=rt[2], in0=pbank(po_), in1=cs[:, 0, :], op=ALU.mult), reads=[*PB(po_), AB("cs")], writes=[AB("rt2")])
                        S.op("dve", lambda e, pe_=pe_: e.tensor_tensor(out=rt[3], in0=pbank(pe_), in1=cs[:, 1, :], op=ALU.mult), reads=[*PB(pe_), AB("cs")], writes=[AB("rt3")])
                        S.op("pool", lambda e, dst=dst: e.tensor_tensor(out=dst[:, 1, :], in0=rt[2], in1=rt[3], op=ALU.add), reads=[AB("rt2"), AB("rt3")], writes=[AB(nm)])
                    for dt in range(2):
                        S.op("pool", lambda e, dt=dt, h=h: e.tensor_tensor(
                            out=qdT[:, dt, :].rearrange("p (c i) -> p c i", c=4),
                            in0=qT[:, dt, :].rearrange("p (c i) -> p c i", c=4),
                            in1=cst[:, C_QD + h * 128:C_QD + h * 128 + 128].unsqueeze(1).to_broadcast([128, 4, 128]),
                            op=ALU.mult), reads=[AB("qT"), B("cst")], writes=[AB("qdT")])
                    ktr = pbank_bf(3)[:, 0:256]
                    scps = pbank(2)[:, 128:256]
                    ogtr = pbank_bf(2)[:, 512:1024]
                    OB = 1
                    last_gc = S_len // 128 - 1

                    def stageA(c, L=L, h=h, Win=Win, wb=wb):
                        gc = g * 4 + c
                        cb = c % 2
                        csl = slice(c * 128, (c + 1) * 128)
                        if gc < last_gc:
                            for dt in range(2):
                                S.op("pe", lambda e, dt=dt, csl=csl: e.transpose(out=ktr[:, dt * 128:(dt + 1) * 128], in_=kT[:, dt, csl], identity=ident[:]), reads=[AB("kT"), B("ident")], writes=[*PB(3)])
                            S.op("act", lambda e, cb=cb, h=h: e.activation(out=k_tm[cb], in_=ktr, func=AF.Copy, scale=cst[:, C_KD + h:C_KD + h + 1]), reads=[*PB(3), B("cst")], writes=[AB("k_tm%d" % cb)])
                        for kt in range(8):
                            S.op("pe", lambda e, kt=kt, csl=csl, Win=Win: e.matmul(pbank(0), lhsT=xnT[:, kt, csl], rhs=Win[:, kt, 512:1024], start=(kt == 0), stop=(kt == 7)), reads=[wb, AB("xnT")], writes=[*PB(0)])
                        S.op("act", lambda e, cb=cb: e.activation(out=v_sb[cb], in_=pbank(0), func=AF.Copy), reads=[*PB(0)], writes=[AB("v_sb%d" % cb)])
                        for dt in range(2):
                            S.op("pe", lambda e, dt=dt, csl=csl: e.matmul(scps, lhsT=kT[:, dt, csl], rhs=qT[:, dt, csl], start=(dt == 0), stop=(dt == 1)), reads=[AB("kT"), AB("qT")], writes=[*PB(2)])
                        S.op("dve", lambda e, cb=cb, h=h: e.tensor_tensor(out=sT[cb], in0=scps, in1=cst[:, C_DT + h * 128:C_DT + h * 128 + 128], op=ALU.mult), reads=[*PB(2), B("cst")], writes=[AB("sT%d" % cb)])

                    def stageB(c, L=L, h=h, sbs=sbs, stf=stf, stb_=stb_):
                        gc = g * 4 + c
                        cb = c % 2
                        csl = slice(c * 128, (c + 1) * 128)
                        has_inter = gc > 0
                        S.op("pe", lambda e, cb=cb, has_inter=has_inter: e.matmul(pbank(OB), lhsT=sT[cb], rhs=v_sb[cb], start=True, stop=(not has_inter)), reads=[AB("sT%d" % cb), AB("v_sb%d" % cb)], writes=[*PB(OB)])
                        if has_inter:
                            for dt in range(2):
                                S.op("pe", lambda e, dt=dt, csl=csl, sbs=sbs: e.matmul(pbank(OB), lhsT=qdT[:, dt, csl], rhs=state_b[:, sbs, dt, :], start=False, stop=(dt == 1)), reads=[AB("qdT"), stb_], writes=[*PB(OB)])
                        if gc < last_gc:
                            for dt in range(2):
                                S.op("pe", lambda e, dt=dt, cb=cb: e.matmul(pbank(4 + dt), lhsT=k_tm[cb][:, dt * 128:(dt + 1) * 128], rhs=v_sb[cb], start=True, stop=True), reads=[AB("k_tm%d" % cb), AB("v_sb%d" % cb)], writes=[*PB(4 + dt)])
                                S.op("dve", lambda e, dt=dt, L=L, h=h: e.scalar_tensor_tensor(out=state_f[:, L, h, dt, :], in0=state_f[:, L, h, dt, :], scalar=CHUNK_DECAY[h], in1=pbank(4 + dt), op0=ALU.mult, op1=ALU.add), reads=[stf, *PB(4 + dt)], writes=[stf])
                            S.op("pool", lambda e, L=L, h=h, sbs=sbs: e.tensor_copy(out=state_b[:, sbs, :, :], in_=state_f[:, L, h, :, :]), reads=[stf], writes=[stb_])
                        S.op("act", lambda e, cb=cb: e.activation(out=junk[:, 0:512], in_=pbank(OB), func=AF.Square, accum_out=small[:, SM_SSO + cb:SM_SSO + cb + 1]), reads=[*PB(OB)], writes=[AB("junk"), B("sso%d" % cb)])
                        S.op("act", lambda e, cb=cb: e.activation(out=small[:, SM_LNO + cb:SM_LNO + cb + 1], in_=small[:, SM_SSO + cb:SM_SSO + cb + 1], func=AF.Ln, scale=1.0 / 512, bias=cst[:, C_EPS:C_EPS + 1]), reads=[B("sso%d" % cb), B("cst")], writes=[B("lno%d" % cb)])
                        S.op("act", lambda e, cb=cb: e.activation(out=small[:, SM_RSO + cb:SM_RSO + cb + 1], in_=small[:, SM_LNO + cb:SM_LNO + cb + 1], func=AF.Exp, scale=-0.5), reads=[B("lno%d" % cb)], writes=[B("rso%d" % cb)])
                        S.op("dve", lambda e, cb=cb, c=c: e.scalar_tensor_tensor(out=og[cb], in0=pbank(OB), scalar=small[:, SM_RSO + cb:SM_RSO + cb + 1], in1=sg[:, c, :], op0=ALU.mult, op1=ALU.mult), reads=[*PB(OB), B("rso%d" % cb), AB("sg%d" % c)], writes=[AB("og%d" % cb)])

                    def stageC(c, L=L, h=h, Wout=Wout, wb=wb):
                        cb = c % 2
                        for et in range(4):
                            S.op("pe", lambda e, et=et, cb=cb: e.transpose(out=ogtr[:, et * 128:(et + 1) * 128], in_=og[cb][:, et * 128:(et + 1) * 128], identity=ident[:]), reads=[AB("og%d" % cb), B("ident")], writes=[*PB(2)])
                        S.op("act", lambda e, cb=cb: e.activation(out=ogT[cb], in_=ogtr.rearrange("p (e t) -> p e t", e=4), func=AF.Copy), reads=[*PB(2)], writes=[AB("ogT%d" % cb)])
                        for nh in range(2):
                            for et in range(4):
                                S.op("pe", lambda e, nh=nh, et=et, cb=cb, Wout=Wout: e.matmul(pbank(6 + nh), lhsT=ogT[cb][:, et, :], rhs=Wout[:, et, nh * 512:(nh + 1) * 512], start=(et == 0), stop=(et == 3)), reads=[AB("ogT%d" % cb), wb], writes=[*PB(6 + nh)])
                            S.op("dve", lambda e, nh=nh, c=c: e.tensor_tensor(out=xg[:, c, nh * 512:(nh + 1) * 512], in0=pbank(6 + nh), in1=xg[:, c, nh * 512:(nh + 1) * 512], op=ALU.add), reads=[*PB(6 + nh), B("x%d" % c)], writes=[B("x%d" % c)])
                        if n_layers <= 2 and L == n_layers - 1 and h == 3:
                            store_x_chunk(g, c)
                            if g + 1 < NG:
                                load_x_chunk(g + 1, c)

                    for c in range(4):
                        stageA(c)
                        if c >= 1:
                            stageB(c - 1)
                        if c >= 2:
                            stageC(c - 2)
                    stageB(3)
                    stageC(2)
                    stageC(3)

            if n_layers <= 2:
                continue
            S.barrier(arena_bufs)

            cv = Carver()
            xnS = [cv.bf(1024), cv.bf(1024)]
            xnTS = cv.bf(4096).rearrange("p (k t) -> p k t", k=8)
            qTs = cv.bf(4096).rearrange("p (t n) -> p t n", t=8)
            sq = cv.bf(512)
            lnr = cv.f32(512)
            rstd = cv.f32(512)
            sgT = cv.bf(4096).rearrange("p (t n) -> p t n", t=8)
            e0_bf = cv.bf(2048)
            e1_bf = cv.bf(2048)
            ebufs = [e0_bf.bitcast(F32), e1_bf.bitcast(F32)]
            junkS = e0_bf[:, 0:1024]
            pT = [cv.bf(2048).rearrange("p (b n) -> p b n", b=2) for _ in range(2)]
            dens = [cv.f32(512) for _ in range(2)]
            ogTs2 = [cv.bf(1024).rearrange("p (t i) -> p t i", t=8) for _ in range(2)]
            r = g % 2

            def qknorm(bk, nm):
                S.op("act", lambda e: e.activation(out=sq, in_=pbank(bk), func=AF.Square), reads=[*PB(bk)], writes=[AB("sq")])
                b2 = next_bank()
                if b2 == bk:
                    b2 = next_bank()
                S.op("pe", lambda e, b2=b2: e.matmul(pbank(b2), lhsT=bones[:], rhs=sq, start=True, stop=True), reads=[AB("sq"), B("bones")], writes=[*PB(b2)])
                S.op("act", lambda e, b2=b2: e.activation(out=lnr, in_=pbank(b2), func=AF.Ln, scale=1.0 / 64, bias=cst[:, C_EPS:C_EPS + 1]), reads=[*PB(b2), B("cst")], writes=[AB("lnr")])
                S.op("act", lambda e: e.activation(out=rstd, in_=lnr, func=AF.Exp, scale=-0.5), reads=[AB("lnr")], writes=[AB("rstd")])

            for j in range(n_layers - 2):
                siA = take_unit()
                prefetch_upto(siA + 1)
                siB = take_unit()
                slotA = siA % 2
                slotB = siB % 2
                WA = wslot[slotA]
                WBt = wslot[slotB]
                wa = B("wslot%d" % slotA)
                wbb = B("wslot%d" % slotB)
                Wq = WA[:, :].rearrange("p (k c) -> p k c", k=8)
                Wo = WBt[:, 0:8192].rearrange("p (k n) -> p k n", k=8)
                norm_phase(xnTS, xnS, junkS, "ebuf0")
                if j == 0:
                    Wkv = WBt[:, 8192:11264].rearrange("p (k c) -> p k c", k=8)
                    for gg in range(2):
                        bk = next_bank()
                        for kt in range(8):
                            S.op("pe", lambda e, gg=gg, kt=kt, bk=bk, Wkv=Wkv: e.matmul(pbank(bk), lhsT=Wkv[:, kt, gg * 128:(gg + 1) * 128], rhs=xnTS[:, kt, :], start=(kt == 0), stop=(kt == 7)), reads=[wbb, AB("xnT")], writes=[*PB(bk)])
                        qknorm(bk, "k")
                        for par in range(2):
                            psl = slice(par * 64, par * 64 + 64)
                            S.op("dve", lambda e, gg=gg, par=par, psl=psl, bk=bk, r=r: e.scalar_tensor_tensor(
                                out=kz[psl, par, gg, r * 512:(r + 1) * 512], in0=pbank(bk)[psl, :], scalar=cst[psl, C_GK:C_GK + 1], in1=rstd[psl, :],
                                op0=ALU.mult, op1=ALU.mult), reads=[*PB(bk), AB("rstd"), B("cst")], writes=[B("kz")])
                    for c in range(4):
                        bk = next_bank()
                        vslot = r * 4 + c
                        for kt in range(8):
                            S.op("pe", lambda e, c=c, kt=kt, bk=bk, Wkv=Wkv: e.matmul(pbank(bk)[:, 0:128], lhsT=xnTS[:, kt, c * 128:(c + 1) * 128], rhs=Wkv[:, kt, 256:384], start=(kt == 0), stop=(kt == 7)), reads=[wbb, AB("xnT")], writes=[*PB(bk)])
                        S.op("act", lambda e, bk=bk, vslot=vslot: e.activation(out=vz[:, vslot, :, 0, 0:64], in_=pbank(bk)[:, 0:128].rearrange("p (g d) -> p g d", g=2), func=AF.Copy), reads=[*PB(bk)], writes=[B("vz")])
                        S.op("dve", lambda e, bk=bk, vslot=vslot: e.tensor_copy(out=vz[:, vslot, :, 1, 64:128], in_=pbank(bk)[:, 0:128].rearrange("p (g d) -> p g d", g=2)), reads=[*PB(bk)], writes=[B("vz")])
                for t in range(8):
                    bk = next_bank()
                    for kt in range(8):
                        S.op("pe", lambda e, t=t, kt=kt, bk=bk, Wq=Wq: e.matmul(pbank(bk), lhsT=Wq[:, kt, t * 128:(t + 1) * 128], rhs=xnTS[:, kt, :], start=(kt == 0), stop=(kt == 7)), reads=[wa, AB("xnT")], writes=[*PB(bk)])
                    qknorm(bk, "q")
                    S.op("dve", lambda e, t=t, bk=bk, j=j: e.scalar_tensor_tensor(out=qTs[:, t, :], in0=pbank(bk), scalar=cst[:, C_GQ + j:C_GQ + j + 1], in1=rstd, op0=ALU.mult, op1=ALU.mult), reads=[*PB(bk), AB("rstd"), B("cst")], writes=[AB("qTs")])
                for t in range(8):
                    bk = next_bank()
                    for kt in range(8):
                        S.op("pe", lambda e, t=t, kt=kt, bk=bk, Wq=Wq: e.matmul(pbank(bk), lhsT=Wq[:, kt, 1024 + t * 128:1024 + (t + 1) * 128], rhs=xnTS[:, kt, :], start=(kt == 0), stop=(kt == 7)), reads=[wa, AB("xnT")], writes=[*PB(bk)])
                    S.op("act", lambda e, t=t, bk=bk: e.activation(out=sgT[:, t, :], in_=pbank(bk), func=AF.Silu), reads=[*PB(bk)], writes=[AB("sgT")])
                prefetch_upto(siA + 2)
                scnt = [0]

                def unit_info(c):
                    gc = g * 4 + c
                    cur_tok = r * 512 + c * 128
                    prev_tok = (cur_tok - 128) % 1024
                    cur_slot = r * 4 + c
                    prev_slot = (cur_slot - 1) % 8
                    return [(0, prev_tok, prev_slot), (1, cur_tok, cur_slot)] if gc > 0 else [(1, cur_tok, cur_slot)]

                def SA(c, gg, j=j):
                    csl = slice(c * 128, (c + 1) * 128)
                    for bi, (blk, btok, bslot) in enumerate(unit_info(c)):
                        dbl = scnt[0] % 2
                        scnt[0] += 1
                        eb = ebufs[dbl]
                        for par in range(2):
                            bk = dbl * 2 + par
                            S.op("pe", lambda e, par=par, gg=gg, btok=btok, bk=bk, csl=csl: e.matmul(pbank(bk), lhsT=kz[:, par, gg, btok:btok + 128], rhs=qTs[:, 4 * gg:4 * gg + 4, csl], start=True, stop=True), reads=[B("kz"), AB("qTs")], writes=[*PB(bk)])
                        for par in range(2):
                            S.op("act", lambda e, dbl=dbl, j=j, par=par, eb=eb: e.activation(out=eb[:, par * 512:(par + 1) * 512], in_=pbank(dbl * 2 + par), func=AF.Exp, scale=0.125, bias=small[:, SM_NMQ + j:SM_NMQ + j + 1]), reads=[*PB(dbl * 2 + par), B("nmq")], writes=[AB("ebuf%d" % dbl)])
                        eng = "dve" if (scnt[0] % 2 == 0) else "pool"
                        S.op(eng, lambda e, gg=gg, bi=bi, blk=blk, eb=eb: e.tensor_tensor(out=pT[gg][:, bi, :], in0=eb, in1=expB[:, blk, gg, :], op=ALU.mult), reads=[AB("ebuf%d" % dbl), B("expB")], writes=[AB("pT%d" % gg)])

                def SB(c, gg, j=j):
                    csl = slice(c * 128, (c + 1) * 128)
                    blks = unit_info(c)
                    og_ = ogTs2[c % 2]
                    nmm = 2 * len(blks)
                    k_ = 0
                    for par in range(2):
                        for bi, (blk, btok, bslot) in enumerate(blks):
                            S.op("pe", lambda e, par=par, bi=bi, bslot=bslot, gg=gg, k_=k_, nmm=nmm: e.matmul(pbank(4), lhsT=vz[:, bslot, gg, par, :], rhs=pT[gg][:, bi, par * 512:(par + 1) * 512], start=(k_ == 0), stop=(k_ == nmm - 1)), reads=[B("vz"), AB("pT%d" % gg)], writes=[*PB(4)])
                            k_ += 1
                    k_ = 0
                    for par in range(2):
                        for bi, (blk, btok, bslot) in enumerate(blks):
                            S.op("pe", lambda e, par=par, bi=bi, gg=gg, k_=k_, nmm=nmm: e.matmul(pbank(5), lhsT=onesz[:, par, :], rhs=pT[gg][:, bi, par * 512:(par + 1) * 512], start=(k_ == 0), stop=(k_ == nmm - 1)), reads=[B("onesz"), AB("pT%d" % gg)], writes=[*PB(5)])
                            k_ += 1
                    S.op("dve", lambda e, gg=gg, j=j: e.tensor_tensor(
                        out=dens[gg].rearrange("p (t i) -> p t i", t=4), in0=pbank(5).rearrange("p (t i) -> p t i", t=4),
                        in1=small[:, SM_SE + 8 * j + 4 * gg:SM_SE + 8 * j + 4 * gg + 4].unsqueeze(2).to_broadcast([128, 4, 128]), op=ALU.add),
                        reads=[*PB(5), B("sinkexp")], writes=[AB("dens%d" % gg)])
                    S.op("dve", lambda e, gg=gg: e.reciprocal(out=dens[gg], in_=dens[gg]), reads=[AB("dens%d" % gg)], writes=[AB("dens%d" % gg)])
                    S.op("pool", lambda e, gg=gg, csl=csl: e.tensor_tensor(out=dens[gg].rearrange("p (t i) -> p t i", t=4), in0=dens[gg].rearrange("p (t i) -> p t i", t=4), in1=sgT[:, 4 * gg:4 * gg + 4, csl], op=ALU.mult), reads=[AB("dens%d" % gg), AB("sgT")], writes=[AB("dens%d" % gg)])
                    S.op("dve", lambda e, gg=gg, og_=og_: e.tensor_tensor(out=og_[:, 4 * gg:4 * gg + 4, :], in0=pbank(4).rearrange("p (t i) -> p t i", t=4), in1=dens[gg].rearrange("p (t i) -> p t i", t=4), op=ALU.mult), reads=[*PB(4), AB("dens%d" % gg)], writes=[AB("ogTs%d" % (c % 2))])

                def SC(c, j=j, Wo=Wo, wbb=wbb):
                    og_ = ogTs2[c % 2]
                    for nh in range(2):
                        for t in range(8):
                            S.op("pe", lambda e, nh=nh, t=t, Wo=Wo, og_=og_: e.matmul(pbank(6 + nh), lhsT=og_[:, t, :], rhs=Wo[:, t, nh * 512:(nh + 1) * 512], start=(t == 0), stop=(t == 7)), reads=[AB("ogTs%d" % (c % 2)), wbb], writes=[*PB(6 + nh)])
                        S.op("dve", lambda e, nh=nh, c=c: e.tensor_tensor(out=xg[:, c, nh * 512:(nh + 1) * 512], in0=pbank(6 + nh), in1=xg[:, c, nh * 512:(nh + 1) * 512], op=ALU.add), reads=[*PB(6 + nh), B("x%d" % c)], writes=[B("x%d" % c)])
                    if j == n_layers - 3:
                        store_x_chunk(g, c)
                        if g + 1 < NG:
                            load_x_chunk(g + 1, c)

                units = [(c, gg) for c in range(4) for gg in range(2)]
                pendC = []
                for i, (c, gg) in enumerate(units):
                    SA(c, gg)
                    if i >= 1:
                        pc, pg = units[i - 1]
                        if pendC:
                            SC(pendC.pop(0))
                        SB(pc, pg)
                        if pg == 1:
                            pendC.append(pc)
                if pendC:
                    SC(pendC.pop(0))
                SB(*units[-1])
                SC(units[-1][0])
            S.barrier(arena_bufs)

        if debug:
            S.dma("sync", lambda e: e.dma_start(out=dbg_kz[:, :], in_=kz[:].rearrange("p a b c -> p (a b c)")), reads=[B("kz")], writes=[B("dbg1")], sem="dbg")
            S.dma("sync", lambda e: e.dma_start(out=dbg_vz[:, :], in_=vz[:].rearrange("p a b c d -> p (a b c d)")), reads=[B("vz")], writes=[B("dbg2")], sem="dbg")
            S.dma("sync", lambda e: e.dma_start(out=dbg_q[:, :], in_=qTs.rearrange("p a b -> p (a b)")), reads=[AB("qTs")], writes=[B("dbg3")], sem="dbg")
            S.dma("sync", lambda e: e.dma_start(out=dbg_sg[:, :], in_=sgT.rearrange("p a b -> p (a b)")), reads=[AB("sgT")], writes=[B("dbg4")], sem="dbg")
            S.wait_for("sync", [B("dbg1"), B("dbg2"), B("dbg3"), B("dbg4")])
        S.wait_for("sync", [B("ydram%d" % c) for c in range(4)])
        S.emit()
    return nc


def _t5_bucket(dist):
    max_exact = 16
    dist_f = np.maximum(dist, 1).astype(np.float32)
    large = max_exact + (np.log(dist_f / np.float32(max_exact)) / np.float32(math.log(128 / max_exact)) * np.float32(32 - max_exact)).astype(np.int32)
    large = np.minimum(large, 31)
    return np.where(dist < max_exact, dist, large)


def host_prep(inputs, S_len):
    f32 = np.float32
    a_w_in = np.asarray(inputs["a_w_in"], f32)
    a_w_out = np.asarray(inputs["a_w_out"], f32)
    w_kv = np.asarray(inputs["w_kv"], f32)
    b_w_in = np.asarray(inputs["b_w_in"], f32)
    b_w_out = np.asarray(inputs["b_w_out"], f32)
    wall = np.empty((128, WTOT), f32)

    def pk(w):
        n = w.shape[1]
        return w.reshape(8, 128, n).transpose(1, 0, 2).reshape(128, 8 * n)

    for L in range(2):
        for h in range(4):
            u = L * 4 + h
            W = a_w_in[L]
            q = W[:, h * 256:(h + 1) * 256]
            k = W[:, 1024 + h * 256:1024 + (h + 1) * 256]
            v = W[:, 2048 + h * 512:2048 + (h + 1) * 512]
            gt = W[:, 4096 + h * 512:4096 + (h + 1) * 512]
            blk = np.concatenate([q[:, 0::2], q[:, 1::2], k[:, 0::2], k[:, 1::2], v, gt], axis=1)
            wall[:, UNIT_OFF[u]:UNIT_OFF[u] + 12288] = pk(blk)
            wo = a_w_out[L][h * 512:(h + 1) * 512, :]
            wall[:, UNIT_OFF[u] + 12288:UNIT_OFF[u] + 16384] = wo.reshape(4, 128, 1024).transpose(1, 0, 2).reshape(128, 4096)
    for j in range(2):
        ua = 8 + 2 * j
        ub = 9 + 2 * j
        wall[:, UNIT_OFF[ua]:UNIT_OFF[ua] + 16384] = pk(b_w_in[j])
        wall[:, UNIT_OFF[ub]:UNIT_OFF[ub] + 8192] = pk(b_w_out[j])
        if j == 0:
            kk = w_kv[:, 0:128]
            vv = w_kv[:, 128:256]
            blk = np.concatenate([kk[:, 0:64], kk[:, 0:64], kk[:, 64:128], kk[:, 64:128], vv], axis=1)
            wall[:, UNIT_OFF[ub] + 8192:UNIT_OFF[ub] + 11264] = pk(blk)

    cst = np.zeros((128, NA), f32)
    cst[:, C_IDENT:C_IDENT + 128] = np.eye(128, dtype=f32)
    bo = np.zeros((128, 128), f32)
    bo[0:64, 0:64] = 1.0
    bo[64:128, 64:128] = 1.0
    cst[:, C_BONES:C_BONES + 128] = bo
    idx = np.arange(128)
    for h in range(4):
        lg = math.log(GAMMA[h])
        diff = idx[None, :] - idx[:, None]
        dt_ = np.where(diff >= 0, np.exp(np.maximum(diff, 0) * lg), 0.0) / 16.0
        cst[:, C_DT + h * 128:C_DT + (h + 1) * 128] = dt_.astype(f32)
        cst[:, C_QD + h * 128:C_QD + (h + 1) * 128] = np.exp((idx + 1.0) * lg).astype(f32)[None, :]
        cst[:, C_KD + h] = (np.exp((127.0 - idx) * lg) / 16.0).astype(f32)
    jj = idx[:, None]
    ii = idx[None, :]
    cst[:, C_MASK:C_MASK + 128] = (jj > ii).astype(f32)
    cst[:, C_MASK + 128:C_MASK + 256] = (jj <= ii).astype(f32)
    gains = [inputs["a_norm_g"][0], inputs["a_norm_g"][1], inputs["kv_norm_g"], inputs["b_norm_g"][0], inputs["b_norm_g"][1]]
    for n, gv in enumerate(gains):
        cst[:, C_G + n * 8:C_G + n * 8 + 8] = np.asarray(gv, f32).reshape(8, 128).T
    p64 = idx % 64
    par = idx // 64
    for j in range(2):
        gq = np.asarray(inputs["b_q_norm_g"][j], f32)
        cst[:, C_GQ + j] = gq[p64]
        cst[:, C_GQR + j * 64:C_GQR + (j + 1) * 64] = gq[None, :]
        sk = np.asarray(inputs["b_sinks"][j], f32)
        for gg in range(2):
            for t in range(4):
                cst[:, C_SINK + j * 8 + gg * 4 + t] = sk[8 * gg + 2 * t + par]
    gk = np.asarray(inputs["k_norm_g"], f32)
    cst[:, C_GK] = gk[p64]
    cst[:, C_GKR:C_GKR + 64] = gk[None, :]
    cst[:, C_EPS] = EPS

    rel_bias = np.asarray(inputs["rel_bias"], f32)
    biasT = np.zeros((128, 2, 2, 2, 4, 128), f32)
    for blk in range(2):
        dist = (ii + 128 - jj) if blk == 0 else (ii - jj)
        bucket = _t5_bucket(np.maximum(dist, 0))
        for gg in range(2):
            for pr in range(2):
                for t in range(4):
                    hh = 8 * gg + 2 * t + pr
                    biasT[:, blk, gg, pr, t, :] = rel_bias[bucket, hh]
    biasT = biasT.reshape(128, 4096)

    angle = (1.0 / (np.float32(10000.0) ** np.linspace(0.0, 1.0, 128, dtype=f32))).astype(f32)
    pos = np.arange(S_len, dtype=f32)
    ang = (angle[:, None] * pos[None, :]).astype(f32)
    cosT = np.cos(ang).astype(f32)
    sinT = np.sin(ang).astype(f32)
    return {"wall": wall, "cst": cst, "biasT": biasT, "cosT": cosT, "sinT": sinT}


_CACHE = {}


def kernel(**inputs):
    x = np.asarray(inputs["x"], np.float32)
    nb, S_len, _ = x.shape
    shared = host_prep(inputs, S_len)
    key = (S_len, 4)
    if key not in _CACHE:
        _CACHE[key] = build_program(S_len, 4)
    nc = _CACHE[key]
    in_maps = []
    for b in range(nb):
        m = dict(shared)
        m["x"] = np.ascontiguousarray(x[b])
        in_maps.append(m)
    res = run_bass_kernel_spmd(nc, in_maps, core_ids=list(range(nb)))
    out = np.stack([np.asarray(r["y"], np.float32) for r in res.results], axis=0)
    return out
```

```python
import math
from contextlib import ExitStack

import numpy as np
import concourse.bass as bass
import concourse.mybir as mybir
from concourse.bass_utils import run_bass_kernel_spmd

F32 = mybir.dt.float32
BF16 = mybir.dt.bfloat16
AF = mybir.ActivationFunctionType
ALU = mybir.AluOpType
AX = mybir.AxisListType

D = 1024
EPS = 1e-6
UNIT_SIZES = [16384] * 8 + [16384, 11264, 16384, 8192]
UNIT_OFF = [int(v) for v in np.cumsum([0] + UNIT_SIZES)]
WTOT = UNIT_OFF[-1]
GAMMA = [1.0 - 2.0 ** (-5.0 - h) for h in range(4)]
CHUNK_DECAY = [g ** 128 for g in GAMMA]

C_IDENT = 0
C_BONES = 128
C_DT = 256
C_QD = 768
C_KD = 1280
C_MASK = 1284
C_G = 1540
C_GQ = 1580
C_GK = 1582
C_SINK = 1583
C_GQR = 1599
C_GKR = 1727
C_EPS = 1791
NA = 1792

COMPUTE = ("pe", "act", "dve", "pool")
QUEUES = ("sync",)


class Buf:
    __slots__ = ("name", "w", "r")

    def __init__(self, name):
        self.name = name
        self.w = None
        self.r = {}


class Sched:
    def __init__(self, nc):
        self.nc = nc
        self.streams = {e: [] for e in COMPUTE + QUEUES}
        self.known = {e: {} for e in COMPUTE + QUEUES}
        self.dmacnt = {}
        self.needed = {e: set() for e in COMPUTE}

    def _deps(self, eng, reads, writes):
        waits = {}
        kn = self.known[eng]

        def need(ev, raw):
            if ev is None:
                return
            k, v = ev
            if k == eng and not raw:
                return
            if kn.get(k, 0) >= v:
                return
            if waits.get(k, 0) < v:
                waits[k] = v

        for b in reads:
            need(b.w, True)
        for b in writes:
            need(b.w, False)
            for k, v in b.r.items():
                need((k, v), False)
        for k, v in waits.items():
            kn[k] = v
            if k in self.needed:
                self.needed[k].add(v)
        return list(waits.items())

    def _record(self, ev, reads, writes):
        k, v = ev
        for b in reads:
            if b.r.get(k, 0) < v:
                b.r[k] = v
        for b in writes:
            b.w = ev
            b.r = {}

    def op(self, eng, fn, reads=(), writes=()):
        pr = [b for b in reads if b.name.startswith("psum")]
        if pr:
            reads = [b for b in reads if not b.name.startswith("psum")]
            writes = list(writes) + pr
        waits = self._deps(eng, reads, writes)
        idx = len(self.streams[eng]) + 1
        ev = (eng, idx)
        self.streams[eng].append((fn, waits, ev, 1))
        self._record(ev, reads, writes)

    def dma(self, eng, fns, reads=(), writes=(), sem="dma"):
        if callable(fns):
            fns = [fns]
        waits = self._deps(eng, reads, writes)
        self.dmacnt[sem] = self.dmacnt.get(sem, 0) + 16 * len(fns)
        ev = (sem, self.dmacnt[sem])
        for i, fn in enumerate(fns):
            self.streams[eng].append((fn, waits if i == 0 else [], (sem, 16), 0))
        self._record(ev, reads, writes)

    def wait_for(self, eng, bufs):
        waits = self._deps(eng, bufs, bufs)
        self.streams[eng].append((None, waits, None, 0))

    def barrier(self, bufs_to_reset=()):
        pos = {e: self._last_idx(e) for e in COMPUTE}
        for e in COMPUTE + QUEUES:
            waits = []
            for k, v in pos.items():
                if k == e or v == 0:
                    continue
                if self.known[e].get(k, 0) >= v:
                    continue
                self.known[e][k] = v
                self.needed[k].add(v)
                waits.append((k, v))
            if waits:
                self.streams[e].append((None, waits, None, 0))
        for b in bufs_to_reset:
            b.w = None
            b.r = {}

    def _last_idx(self, e):
        for fn, waits, ev, kind in reversed(self.streams[e]):
            if kind == 1:
                return ev[1]
        return 0

    def op_index_fix(self):
        pass

    def emit(self):
        nc = self.nc
        rank = {}
        for e in COMPUTE:
            ids = sorted(self.needed[e])
            rank[e] = {v: i + 1 for i, v in enumerate(ids)}
        with ExitStack() as es:
            sems = {}
            for k in list(COMPUTE) + list(self.dmacnt.keys()):
                sems[k] = es.enter_context(nc.semaphore("s_" + k))
            block = es.enter_context(nc.Block())

            def run(eng, e):
                for fn, waits, ev, kind in self.streams[eng]:
                    for k, v in waits:
                        val = rank[k][v] if k in rank else v
                        e.wait_ge(sems[k], val)
                    if fn is None:
                        continue
                    ins = fn(e)
                    if kind == 1:
                        if ev[1] in rank[ev[0]]:
                            ins.then_inc(sems[ev[0]], 1)
                    else:
                        ins.then_inc(sems[ev[0]], 16)

            @block.tensor
            def _(e):
                run("pe", e)

            @block.scalar
            def _(e):
                run("act", e)

            @block.vector
            def _(e):
                run("dve", e)

            @block.gpsimd
            def _(e):
                run("pool", e)

            @block.sync
            def _(e):
                run("sync", e)


def build_program(S_len, n_layers=4, debug=False):
    NG = S_len // 512
    nc = bass.Bass("TRN2", target_bir_lowering=False)
    x_d = nc.dram_tensor("x", [S_len, D], F32, kind="ExternalInput").ap()
    y_d = nc.dram_tensor("y", [S_len, D], F32, kind="ExternalOutput").ap()
    wall_d = nc.dram_tensor("wall", [128, WTOT], F32, kind="ExternalInput").ap()
    cst_d = nc.dram_tensor("cst", [128, NA], F32, kind="ExternalInput").ap()
    bias_d = nc.dram_tensor("biasT", [128, 4096], F32, kind="ExternalInput").ap()
    cos_d = nc.dram_tensor("cosT", [128, S_len], F32, kind="ExternalInput").ap()
    sin_d = nc.dram_tensor("sinT", [128, S_len], F32, kind="ExternalInput").ap()
    scr_d = nc.dram_tensor("scr", [128, WTOT], BF16, kind="Internal").ap()
    if debug:
        dbg_kz = nc.dram_tensor("dbg_kz", [128, 4096], BF16, kind="ExternalOutput").ap()
        dbg_vz = nc.dram_tensor("dbg_vz", [128, 4096], BF16, kind="ExternalOutput").ap()
        dbg_q = nc.dram_tensor("dbg_q", [128, 4096], BF16, kind="ExternalOutput").ap()
        dbg_sg = nc.dram_tensor("dbg_sg", [128, 4096], BF16, kind="ExternalOutput").ap()

    es = ExitStack()
    with es:
        def sb(name, shape, dt):
            return es.enter_context(nc.sbuf_tensor("sb_" + name, shape, dt))

        def psum(name, shape, dt):
            return es.enter_context(nc.psum_tensor(name, shape, dt))

        S = Sched(nc)
        bufs = {}

        def B(name):
            if name not in bufs:
                bufs[name] = Buf(name)
            return bufs[name]

        cst = sb("cst", [128, NA], F32)
        ident = sb("ident", [128, 128], BF16)
        bones = sb("bones", [128, 128], BF16)
        onesz = sb("onesz", [128, 2, 128], BF16)
        expB = sb("expB", [128, 2, 2, 1024], BF16)
        state_f = sb("state_f", [128, 2, 4, 2, 512], F32)
        state_b = sb("state_b", [128, 2, 2, 512], BF16)
        kz = sb("kz", [128, 2, 2, 1024], BF16)
        vz = sb("vz", [128, 8, 2, 2, 128], BF16)
        xg = sb("xg", [128, 4, 1024], F32)
        wslot = [sb("wslot0", [128, 16384], BF16), sb("wslot1", [128, 16384], BF16)]
        small = sb("small", [128, 64], F32)
        ARENA = 29184
        arena = sb("arena", [128, ARENA], BF16)
        PS = [psum("ps%d" % i, [128, 1024], F32) for i in range(4)]

        def pbank(i):
            return PS[i // 2][:, (i % 2) * 512:(i % 2) * 512 + 512]

        def pbank_bf(i):
            return pbank(i).bitcast(BF16)

        def PB(i):
            return [B("psum%d" % i)]

        rr = [0]

        def next_bank():
            rr[0] = (rr[0] + 1) % 8
            return rr[0]

        class Carver:
            def __init__(self):
                self.pos = 0

            def bf(self, n):
                a = self.pos
                self.pos += n
                assert self.pos <= ARENA, self.pos
                return arena[:, a:a + n]

            def f32(self, n):
                a = self.pos
                self.pos += 2 * n
                assert self.pos <= ARENA, self.pos
                return arena[:, a:a + 2 * n].bitcast(F32)

        arena_bufs = []

        def AB(name):
            b = B(name)
            if b not in arena_bufs:
                arena_bufs.append(b)
            return b

        SM_SS = 0
        SM_LN = 4
        SM_RS = 8
        SM_SSO = 12
        SM_LNO = 14
        SM_RSO = 16
        SM_MQ2 = 20
        SM_MK2 = 22
        SM_NMQ = 24
        SM_SE = 32

        S.dma("sync", lambda e: e.dma_start(out=cst[:], in_=cst_d[:, :]), writes=[B("cst")], sem="cst")
        cv = Carver()
        btmp = cv.f32(4096)
        S.dma("sync", lambda e: e.dma_start(out=btmp, in_=bias_d[:, :]), writes=[AB("btmp")], sem="btmp")
        S.op("dve", lambda e: e.tensor_copy(out=ident[:], in_=cst[:, C_IDENT:C_IDENT + 128]), reads=[B("cst")], writes=[B("ident")])
        S.op("dve", lambda e: e.tensor_copy(out=bones[:], in_=cst[:, C_BONES:C_BONES + 128]), reads=[B("cst")], writes=[B("bones")])
        S.op("pool", lambda e: e.memset(onesz[:], 0.0), writes=[B("onesz")])
        S.op("pool", lambda e: e.memset(onesz[:, 0, 0:64], 1.0), writes=[B("onesz")])
        S.op("pool", lambda e: e.memset(onesz[:, 1, 64:128], 1.0), writes=[B("onesz")])
        S.op("pool", lambda e: e.memset(kz[:], 0.0), writes=[B("kz")])
        S.op("pool", lambda e: e.memset(vz[:], 0.0), writes=[B("vz")])
        S.op("pool", lambda e: e.memset(state_f[:], 0.0), writes=[B("state_f%d_%d" % (L, h)) for L in range(2) for h in range(4)])
        gq2 = cv.f32(128)
        gk2 = cv.f32(64)
        S.op("dve", lambda e: e.tensor_tensor(out=gq2, in0=cst[:, C_GQR:C_GQR + 128], in1=cst[:, C_GQR:C_GQR + 128], op=ALU.mult), reads=[B("cst")], writes=[AB("gq2")])
        S.op("dve", lambda e: e.tensor_tensor(out=gk2, in0=cst[:, C_GKR:C_GKR + 64], in1=cst[:, C_GKR:C_GKR + 64], op=ALU.mult), reads=[B("cst")], writes=[AB("gk2")])
        S.op("dve", lambda e: e.tensor_reduce(out=small[:, SM_MQ2:SM_MQ2 + 2], in_=gq2.rearrange("p (a b) -> p a b", a=2), axis=AX.X, op=ALU.max), reads=[AB("gq2")], writes=[B("mq2")])
        S.op("dve", lambda e: e.tensor_reduce(out=small[:, SM_MK2:SM_MK2 + 1], in_=gk2, axis=AX.X, op=ALU.max), reads=[AB("gk2")], writes=[B("mk2")])
        S.op("dve", lambda e: e.tensor_scalar(out=small[:, SM_NMQ:SM_NMQ + 2], in0=small[:, SM_MQ2:SM_MQ2 + 2], scalar1=small[:, SM_MK2:SM_MK2 + 1], scalar2=-4.0, op0=ALU.add, op1=ALU.mult), reads=[B("mq2"), B("mk2")], writes=[B("nmq")])
        for j in range(2):
            S.op("act", lambda e, j=j: e.activation(out=small[:, SM_SE + 8 * j:SM_SE + 8 * j + 8], in_=cst[:, C_SINK + 8 * j:C_SINK + 8 * j + 8], func=AF.Exp, bias=small[:, SM_NMQ + j:SM_NMQ + j + 1]), reads=[B("cst"), B("nmq")], writes=[B("sinkexp")])
        S.op("act", lambda e: e.activation(out=btmp, in_=btmp, func=AF.Exp), reads=[AB("btmp")], writes=[AB("btmp")])
        for blk in range(2):
            S.op("dve", lambda e, blk=blk: e.tensor_tensor(
                out=expB[:, blk, :, :].rearrange("p g (c i) -> p (g c) i", i=128),
                in0=btmp[:, blk * 2048:(blk + 1) * 2048].rearrange("p (c i) -> p c i", i=128),
                in1=cst[:, C_MASK + blk * 128:C_MASK + blk * 128 + 128].unsqueeze(1).to_broadcast([128, 16, 128]),
                op=ALU.mult), reads=[AB("btmp"), B("cst")], writes=[B("expB")])

        stage = [cv.f32(4096), cv.f32(4096)]
        pieces_of_unit = []
        for u in range(12):
            pcs = []
            if u < 8:
                L = u // 4
                for pp in range(4):
                    pcs.append((pp * 3072, 3072, [(k * 1536, 1536, L, 2 * pp + k) for k in range(2)]))
                pcs.append((12288, 4096, [(0, 4096, None, None)]))
            elif u in (8, 10):
                j = (u - 8) // 2
                for pp in range(4):
                    pcs.append((pp * 4096, 4096, [(k * 2048, 2048, 3 + j, 2 * pp + k) for k in range(2)]))
            else:
                pcs.append((0, 4096, [(0, 4096, None, None)]))
                pcs.append((4096, 4096, [(0, 4096, None, None)]))
                if u == 9:
                    pcs.append((8192, 3072, [(k * 384, 384, 2, k) for k in range(8)]))
            pieces_of_unit.append(pcs)
        pcnt = 0
        ccnt = 0
        for u in range(12):
            slot = u % 2
            wb = B("wslot%d" % slot)
            for (c0, n, subs) in pieces_of_unit[u]:
                sidx = pcnt % 2
                pcnt += 1
                st = stage[sidx]
                stb = AB("stage%d" % sidx)
                S.dma("sync", lambda e, st=st, c0=c0, n=n, u=u: e.dma_start(out=st[:, 0:n], in_=wall_d[:, UNIT_OFF[u] + c0:UNIT_OFF[u] + c0 + n]), writes=[stb], sem="pp%d" % sidx)
                for (o, m, gi, kt) in subs:
                    dst = wslot[slot][:, c0 + o:c0 + o + m]
                    src = st[:, o:o + m]
                    if gi is None:
                        half = m // 2
                        S.op("pool", lambda e, dst=dst, src=src, half=half: e.tensor_copy(out=dst[:, 0:half], in_=src[:, 0:half]), reads=[stb], writes=[wb])
                        S.op("dve", lambda e, dst=dst, src=src, half=half, m=m: e.tensor_copy(out=dst[:, half:m], in_=src[:, half:m]), reads=[stb], writes=[wb])
                    else:
                        gcol = C_G + gi * 8 + kt
                        if ccnt % 2 == 0:
                            S.op("dve", lambda e, dst=dst, src=src, gcol=gcol: e.tensor_scalar(out=dst, in0=src, scalar1=cst[:, gcol:gcol + 1], scalar2=None, op0=ALU.mult), reads=[stb, B("cst")], writes=[wb])
                        else:
                            S.op("act", lambda e, dst=dst, src=src, gcol=gcol: e.activation(out=dst, in_=src, func=AF.Copy, scale=cst[:, gcol:gcol + 1]), reads=[stb, B("cst")], writes=[wb])
                        ccnt += 1
            sz = UNIT_SIZES[u]
            S.dma("sync", lambda e, u=u, sz=sz, slot=slot: e.dma_start(out=scr_d[:, UNIT_OFF[u]:UNIT_OFF[u] + sz], in_=wslot[slot][:, 0:sz]), reads=[wb], writes=[B("scr%d" % u)], sem="scr%d" % u)

        S.barrier(arena_bufs)

        units_per_group = list(range(0, 4 * min(n_layers, 2))) + ([8, 9] if n_layers >= 3 else []) + ([10, 11] if n_layers >= 4 else [])
        useq = [(g, u) for g in range(NG) for u in units_per_group]
        loaded = [0]

        def load_unit(si):
            if si >= len(useq):
                return
            g, u = useq[si]
            slot = si % 2
            sz = UNIT_SIZES[u]
            half = sz // 2
            S.dma("sync", [lambda e, u=u, slot=slot, half=half: e.dma_start(out=wslot[slot][:, 0:half], in_=scr_d[:, UNIT_OFF[u]:UNIT_OFF[u] + half]),
                           lambda e, u=u, slot=slot, half=half, sz=sz: e.dma_start(out=wslot[slot][:, half:sz], in_=scr_d[:, UNIT_OFF[u] + half:UNIT_OFF[u] + sz])],
                  reads=[B("scr%d" % u)], writes=[B("wslot%d" % slot)], sem="w%d" % slot)

        seqpos = [0]
        issued = [0]

        def prefetch_upto(si):
            while issued[0] <= si:
                load_unit(issued[0])
                issued[0] += 1

        def take_unit():
            si = seqpos[0]
            prefetch_upto(si)
            seqpos[0] += 1
            return si

        def begin_unit():
            si = take_unit()
            prefetch_upto(si + 1)
            return si % 2

        def norm_phase(xnT, xn, junk, junkname="junk"):
            for c in range(4):
                xb = B("x%d" % c)
                S.op("act", lambda e, c=c: e.activation(out=junk, in_=xg[:, c, :], func=AF.Square, accum_out=small[:, SM_SS + c:SM_SS + c + 1]), reads=[xb], writes=[AB(junkname), B("ss%d" % c)])
            for c in range(4):
                S.op("act", lambda e, c=c: e.activation(out=small[:, SM_LN + c:SM_LN + c + 1], in_=small[:, SM_SS + c:SM_SS + c + 1], func=AF.Ln, scale=1.0 / D, bias=cst[:, C_EPS:C_EPS + 1]), reads=[B("ss%d" % c), B("cst")], writes=[B("ln%d" % c)])
            for c in range(4):
                S.op("act", lambda e, c=c: e.activation(out=small[:, SM_RS + c:SM_RS + c + 1], in_=small[:, SM_LN + c:SM_LN + c + 1], func=AF.Exp, scale=-0.5), reads=[B("ln%d" % c)], writes=[B("rs%d" % c)])
            for c in range(4):
                xb = B("x%d" % c)
                xs = xn[c % 2]
                S.op("dve", lambda e, c=c, xs=xs: e.tensor_scalar(out=xs, in0=xg[:, c, :], scalar1=small[:, SM_RS + c:SM_RS + c + 1], scalar2=None, op0=ALU.mult), reads=[xb, B("rs%d" % c)], writes=[AB("xn%d" % (c % 2))])
                bk = next_bank()
                pv = pbank_bf(bk)
                for kt in range(8):
                    S.op("pe", lambda e, kt=kt, pv=pv, xs=xs: e.transpose(out=pv[:, kt * 128:(kt + 1) * 128], in_=xs[:, kt * 128:(kt + 1) * 128], identity=ident[:]), reads=[AB("xn%d" % (c % 2)), B("ident")], writes=[*PB(bk)])
                S.op("act", lambda e, c=c, pv=pv: e.activation(out=xnT[:, :, c * 128:(c + 1) * 128], in_=pv.rearrange("p (k t) -> p k t", k=8), func=AF.Copy), reads=[*PB(bk)], writes=[AB("xnT")])

        def load_x_chunk(g, c):
            r0 = (g * 4 + c) * 128
            S.dma("sync", lambda e, c=c, r0=r0: e.dma_start(out=xg[:, c, :], in_=x_d[r0:r0 + 128, :]), writes=[B("x%d" % c)], sem="x%d" % c)

        def store_x_chunk(g, c):
            r0 = (g * 4 + c) * 128
            S.dma("sync", lambda e, c=c, r0=r0: e.dma_start(out=y_d[r0:r0 + 128, :], in_=xg[:, c, :]), reads=[B("x%d" % c)], writes=[B("ydram%d" % c)], sem="st%d" % c)

        for c in range(4):
            load_x_chunk(0, c)

        lh_counter = [0]

        for g in range(NG):
            cv = Carver()
            xn = [cv.bf(1024), cv.bf(1024)]
            junk = cv.bf(1024)
            xnT = cv.bf(4096).rearrange("p (k t) -> p k t", k=8)
            qT = cv.bf(1024).rearrange("p (d t) -> p d t", d=2)
            qdT = cv.bf(1024).rearrange("p (d t) -> p d t", d=2)
            kT = cv.bf(1024).rearrange("p (d t) -> p d t", d=2)
            rt = [cv.f32(512) for _ in range(4)]
            cs = cv.f32(1024).rearrange("p (a t) -> p a t", a=2)
            sg = cv.f32(2048).rearrange("p (c n) -> p c n", c=4)
            k_tm = [cv.bf(256) for _ in range(2)]
            v_sb = [cv.bf(512) for _ in range(2)]
            sT = [cv.bf(128) for _ in range(2)]
            og = [cv.bf(512) for _ in range(2)]
            ogT = [cv.bf(512).rearrange("p (e t) -> p e t", e=4) for _ in range(2)]

            S.dma("sync", [lambda e, g=g: e.dma_start(out=cs[:, 0, :], in_=cos_d[:, g * 512:(g + 1) * 512]),
                           lambda e, g=g: e.dma_start(out=cs[:, 1, :], in_=sin_d[:, g * 512:(g + 1) * 512])], writes=[AB("cs")], sem="cs")

            for L in range(min(n_layers, 2)):
                norm_phase(xnT, xn, junk)
                for h in range(4):
                    slot = begin_unit()
                    W = wslot[slot]
                    wb = B("wslot%d" % slot)
                    Win = W[:, 0:12288].rearrange("p (k c) -> p k c", k=8)
                    Wout = W[:, 12288:16384].rearrange("p (e n) -> p e n", e=4)
                    sbs = lh_counter[0] % 2
                    lh_counter[0] += 1
                    stf = B("state_f%d_%d" % (L, h))
                    stb_ = B("state_b%d" % sbs)
                    if g > 0:
                        S.op("pool", lambda e, L=L, h=h, sbs=sbs: e.tensor_copy(out=state_b[:, sbs, :, :], in_=state_f[:, L, h, :, :]), reads=[stf], writes=[stb_])
                    for tile in range(4):
                        for kt in range(8):
                            S.op("pe", lambda e, tile=tile, kt=kt, Win=Win: e.matmul(pbank(tile), lhsT=Win[:, kt, tile * 128:(tile + 1) * 128], rhs=xnT[:, kt, :], start=(kt == 0), stop=(kt == 7)), reads=[wb, AB("xnT")], writes=[*PB(tile)])
                    for c in range(4):
                        for kt in range(8):
                            S.op("pe", lambda e, c=c, kt=kt, Win=Win: e.matmul(pbank(4 + c), lhsT=xnT[:, kt, c * 128:(c + 1) * 128], rhs=Win[:, kt, 1024:1536], start=(kt == 0), stop=(kt == 7)), reads=[wb, AB("xnT")], writes=[*PB(4 + c)])
                        S.op("act", lambda e, c=c: e.activation(out=sg[:, c, :], in_=pbank(4 + c), func=AF.Silu), reads=[*PB(4 + c)], writes=[AB("sg%d" % c)])
                    for (pe_, po_, dst, nm) in ((0, 1, qT, "qT"), (2, 3, kT, "kT")):
                        S.op("dve", lambda e, pe_=pe_: e.tensor_tensor(out=rt[0], in0=pbank(pe_), in1=cs[:, 0, :], op=ALU.mult), reads=[*PB(pe_), AB("cs")], writes=[AB("rt0")])
                        S.op("dve", lambda e, po_=po_: e.tensor_tensor(out=rt[1], in0=pbank(po_), in1=cs[:, 1, :], op=ALU.mult), reads=[*PB(po_), AB("cs")], writes=[AB("rt1")])
                        S.op("pool", lambda e, dst=dst: e.tensor_tensor(out=dst[:, 0, :], in0=rt[0], in1=rt[1], op=ALU.subtract), reads=[AB("rt0"), AB("rt1")], writes=[AB(nm)])
                        S.op("dve", lambda e, po_=po_: e.tensor_tensor(out=rt[2], in0=pbank(po_), in1=cs[:, 0, :], op=ALU.mult), reads=[*PB(po_), AB("cs")], writes=[AB("rt2")])
                        S.op("dve", lambda e, pe_=pe_: e.tensor_tensor(out=rt[3], in0=pbank(pe_), in1=cs[:, 1, :], op=ALU.mult), reads=[*PB(pe_), AB("cs")], writes=[AB("rt3")])
                        S.op("pool", lambda e, dst=dst: e.tensor_tensor(out=dst[:, 1, :], in0=rt[2], in1=rt[3], op=ALU.add), reads=[AB("rt2"), AB("rt3")], writes=[AB(nm)])
                    for dt in range(2):
                        S.op("pool", lambda e, dt=dt, h=h: e.tensor_tensor(
                            out=qdT[:, dt, :].rearrange("p (c i) -> p c i", c=4),
                            in0=qT[:, dt, :].rearrange("p (c i) -> p c i", c=4),
                            in1=cst[:, C_QD + h * 128:C_QD + h * 128 + 128].unsqueeze(1).to_broadcast([128, 4, 128]),
                            op=ALU.mult), reads=[AB("qT"), B("cst")], writes=[AB("qdT")])
                    ktr = pbank_bf(3)[:, 0:256]
                    scps = pbank(2)[:, 128:256]
                    ogtr = pbank_bf(2)[:, 512:1024]
                    OB = 1
                    last_gc = S_len // 128 - 1

                    def stageA(c, L=L, h=h, Win=Win, wb=wb):
                        gc = g * 4 + c
                        cb = c % 2
                        csl = slice(c * 128, (c + 1) * 128)
                        if gc < last_gc:
                            for dt in range(2):
                                S.op("pe", lambda e, dt=dt, csl=csl: e.transpose(out=ktr[:, dt * 128:(dt + 1) * 128], in_=kT[:, dt, csl], identity=ident[:]), reads=[AB("kT"), B("ident")], writes=[*PB(3)])
                            S.op("act", lambda e, cb=cb, h=h: e.activation(out=k_tm[cb], in_=ktr, func=AF.Copy, scale=cst[:, C_KD + h:C_KD + h + 1]), reads=[*PB(3), B("cst")], writes=[AB("k_tm%d" % cb)])
                        for kt in range(8):
                            S.op("pe", lambda e, kt=kt, csl=csl, Win=Win: e.matmul(pbank(0), lhsT=xnT[:, kt, csl], rhs=Win[:, kt, 512:1024], start=(kt == 0), stop=(kt == 7)), reads=[wb, AB("xnT")], writes=[*PB(0)])
                        S.op("act", lambda e, cb=cb: e.activation(out=v_sb[cb], in_=pbank(0), func=AF.Copy), reads=[*PB(0)], writes=[AB("v_sb%d" % cb)])
                        for dt in range(2):
                            S.op("pe", lambda e, dt=dt, csl=csl: e.matmul(scps, lhsT=kT[:, dt, csl], rhs=qT[:, dt, csl], start=(dt == 0), stop=(dt == 1)), reads=[AB("kT"), AB("qT")], writes=[*PB(2)])
                        S.op("dve", lambda e, cb=cb, h=h: e.tensor_tensor(out=sT[cb], in0=scps, in1=cst[:, C_DT + h * 128:C_DT + h * 128 + 128], op=ALU.mult), reads=[*PB(2), B("cst")], writes=[AB("sT%d" % cb)])

                    def stageB(c, L=L, h=h, sbs=sbs, stf=stf, stb_=stb_):
                        gc = g * 4 + c
                        cb = c % 2
                        csl = slice(c * 128, (c + 1) * 128)
                        has_inter = gc > 0
                        S.op("pe", lambda e, cb=cb, has_inter=has_inter: e.matmul(pbank(OB), lhsT=sT[cb], rhs=v_sb[cb], start=True, stop=(not has_inter)), reads=[AB("sT%d" % cb), AB("v_sb%d" % cb)], writes=[*PB(OB)])
                        if has_inter:
                            for dt in range(2):
                                S.op("pe", lambda e, dt=dt, csl=csl, sbs=sbs: e.matmul(pbank(OB), lhsT=qdT[:, dt, csl], rhs=state_b[:, sbs, dt, :], start=False, stop=(dt == 1)), reads=[AB("qdT"), stb_], writes=[*PB(OB)])
                        if gc < last_gc:
                            for dt in range(2):
                                S.op("pe", lambda e, dt=dt, cb=cb: e.matmul(pbank(4 + dt), lhsT=k_tm[cb][:, dt * 128:(dt + 1) * 128], rhs=v_sb[cb], start=True, stop=True), reads=[AB("k_tm%d" % cb), AB("v_sb%d" % cb)], writes=[*PB(4 + dt)])
                                S.op("dve", lambda e, dt=dt, L=L, h=h: e.scalar_tensor_tensor(out=state_f[:, L, h, dt, :], in0=state_f[:, L, h, dt, :], scalar=CHUNK_DECAY[h], in1=pbank(4 + dt), op0=ALU.mult, op1=ALU.add), reads=[stf, *PB(4 + dt)], writes=[stf])
                            S.op("pool", lambda e, L=L, h=h, sbs=sbs: e.tensor_copy(out=state_b[:, sbs, :, :], in_=state_f[:, L, h, :, :]), reads=[stf], writes=[stb_])
                        S.op("act", lambda e, cb=cb: e.activation(out=junk[:, 0:512], in_=pbank(OB), func=AF.Square, accum_out=small[:, SM_SSO + cb:SM_SSO + cb + 1]), reads=[*PB(OB)], writes=[AB("junk"), B("sso%d" % cb)])
                        S.op("act", lambda e, cb=cb: e.activation(out=small[:, SM_LNO + cb:SM_LNO + cb + 1], in_=small[:, SM_SSO + cb:SM_SSO + cb + 1], func=AF.Ln, scale=1.0 / 512, bias=cst[:, C_EPS:C_EPS + 1]), reads=[B("sso%d" % cb), B("cst")], writes=[B("lno%d" % cb)])
                        S.op("act", lambda e, cb=cb: e.activation(out=small[:, SM_RSO + cb:SM_RSO + cb + 1], in_=small[:, SM_LNO + cb:SM_LNO + cb + 1], func=AF.Exp, scale=-0.5), reads=[B("lno%d" % cb)], writes=[B("rso%d" % cb)])
                        S.op("dve", lambda e, cb=cb, c=c: e.scalar_tensor_tensor(out=og[cb], in0=pbank(OB), scalar=small[:, SM_RSO + cb:SM_RSO + cb + 1], in1=sg[:, c, :], op0=ALU.mult, op1=ALU.mult), reads=[*PB(OB), B("rso%d" % cb), AB("sg%d" % c)], writes=[AB("og%d" % cb)])

                    def stageC(c, L=L, h=h, Wout=Wout, wb=wb):
                        cb = c % 2
                        for et in range(4):
                            S.op("pe", lambda e, et=et, cb=cb: e.transpose(out=ogtr[:, et * 128:(et + 1) * 128], in_=og[cb][:, et * 128:(et + 1) * 128], identity=ident[:]), reads=[AB("og%d" % cb), B("ident")], writes=[*PB(2)])
                        S.op("act", lambda e, cb=cb: e.activation(out=ogT[cb], in_=ogtr.rearrange("p (e t) -> p e t", e=4), func=AF.Copy), reads=[*PB(2)], writes=[AB("ogT%d" % cb)])
                        for nh in range(2):
                            for et in range(4):
                                S.op("pe", lambda e, nh=nh, et=et, cb=cb, Wout=Wout: e.matmul(pbank(6 + nh), lhsT=ogT[cb][:, et, :], rhs=Wout[:, et, nh * 512:(nh + 1) * 512], start=(et == 0), stop=(et == 3)), reads=[AB("ogT%d" % cb), wb], writes=[*PB(6 + nh)])
                            S.op("dve", lambda e, nh=nh, c=c: e.tensor_tensor(out=xg[:, c, nh * 512:(nh + 1) * 512], in0=pbank(6 + nh), in1=xg[:, c, nh * 512:(nh + 1) * 512], op=ALU.add), reads=[*PB(6 + nh), B("x%d" % c)], writes=[B("x%d" % c)])
                        if n_layers <= 2 and L == n_layers - 1 and h == 3:
                            store_x_chunk(g, c)
                            if g + 1 < NG:
                                load_x_chunk(g + 1, c)

                    for c in range(4):
                        stageA(c)
                        if c >= 1:
                            stageB(c - 1)
                        if c >= 2:
                            stageC(c - 2)
                    stageB(3)
                    stageC(2)
                    stageC(3)

            if n_layers <= 2:
                continue
            S.barrier(arena_bufs)

            cv = Carver()
            xnS = [cv.bf(1024), cv.bf(1024)]
            xnTS = cv.bf(4096).rearrange("p (k t) -> p k t", k=8)
            qTs = cv.bf(4096).rearrange("p (t n) -> p t n", t=8)
            sq = cv.bf(512)
            lnr = cv.f32(512)
            rstd = cv.f32(512)
            sgT = cv.bf(4096).rearrange("p (t n) -> p t n", t=8)
            e0_bf = cv.bf(2048)
            e1_bf = cv.bf(2048)
            ebufs = [e0_bf.bitcast(F32), e1_bf.bitcast(F32)]
            junkS = e0_bf[:, 0:1024]
            pT = [cv.bf(2048).rearrange("p (b n) -> p b n", b=2) for _ in range(2)]
            dens = [cv.f32(512) for _ in range(2)]
            ogTs2 = [cv.bf(1024).rearrange("p (t i) -> p t i", t=8) for _ in range(2)]
            r = g % 2

            def qknorm(bk, nm):
                S.op("act", lambda e: e.activation(out=sq, in_=pbank(bk), func=AF.Square), reads=[*PB(bk)], writes=[AB("sq")])
                b2 = next_bank()
                if b2 == bk:
                    b2 = next_bank()
                S.op("pe", lambda e, b2=b2: e.matmul(pbank(b2), lhsT=bones[:], rhs=sq, start=True, stop=True), reads=[AB("sq"), B("bones")], writes=[*PB(b2)])
                S.op("act", lambda e, b2=b2: e.activation(out=lnr, in_=pbank(b2), func=AF.Ln, scale=1.0 / 64, bias=cst[:, C_EPS:C_EPS + 1]), reads=[*PB(b2), B("cst")], writes=[AB("lnr")])
                S.op("act", lambda e: e.activation(out=rstd, in_=lnr, func=AF.Exp, scale=-0.5), reads=[AB("lnr")], writes=[AB("rstd")])

            for j in range(n_layers - 2):
                siA = take_unit()
                prefetch_upto(siA + 1)
                siB = take_unit()
                slotA = siA % 2
                slotB = siB % 2
                WA = wslot[slotA]
                WBt = wslot[slotB]
                wa = B("wslot%d" % slotA)
                wbb = B("wslot%d" % slotB)
                Wq = WA[:, :].rearrange("p (k c) -> p k c", k=8)
                Wo = WBt[:, 0:8192].rearrange("p (k n) -> p k n", k=8)
                norm_phase(xnTS, xnS, junkS, "ebuf0")
                if j == 0:
                    Wkv = WBt[:, 8192:11264].rearrange("p (k c) -> p k c", k=8)
                    for gg in range(2):
                        bk = next_bank()
                        for kt in range(8):
                            S.op("pe", lambda e, gg=gg, kt=kt, bk=bk, Wkv=Wkv: e.matmul(pbank(bk), lhsT=Wkv[:, kt, gg * 128:(gg + 1) * 128], rhs=xnTS[:, kt, :], start=(kt == 0), stop=(kt == 7)), reads=[wbb, AB("xnT")], writes=[*PB(bk)])
                        qknorm(bk, "k")
                        for par in range(2):
                            psl = slice(par * 64, par * 64 + 64)
                            S.op("dve", lambda e, gg=gg, par=par, psl=psl, bk=bk, r=r: e.scalar_tensor_tensor(
                                out=kz[psl, par, gg, r * 512:(r + 1) * 512], in0=pbank(bk)[psl, :], scalar=cst[psl, C_GK:C_GK + 1], in1=rstd[psl, :],
                                op0=ALU.mult, op1=ALU.mult), reads=[*PB(bk), AB("rstd"), B("cst")], writes=[B("kz")])
                    for c in range(4):
                        bk = next_bank()
                        vslot = r * 4 + c
                        for kt in range(8):
                            S.op("pe", lambda e, c=c, kt=kt, bk=bk, Wkv=Wkv: e.matmul(pbank(bk)[:, 0:128], lhsT=xnTS[:, kt, c * 128:(c + 1) * 128], rhs=Wkv[:, kt, 256:384], start=(kt == 0), stop=(kt == 7)), reads=[wbb, AB("xnT")], writes=[*PB(bk)])
                        S.op("act", lambda e, bk=bk, vslot=vslot: e.activation(out=vz[:, vslot, :, 0, 0:64], in_=pbank(bk)[:, 0:128].rearrange("p (g d) -> p g d", g=2), func=AF.Copy), reads=[*PB(bk)], writes=[B("vz")])
                        S.op("dve", lambda e, bk=bk, vslot=vslot: e.tensor_copy(out=vz[:, vslot, :, 1, 64:128], in_=pbank(bk)[:, 0:128].rearrange("p (g d) -> p g d", g=2)), reads=[*PB(bk)], writes=[B("vz")])
                def q_finish(t, bk, j=j):
                    b2 = next_bank()
                    if b2 == bk:
                        b2 = next_bank()
                    S.op("pe", lambda e, b2=b2: e.matmul(pbank(b2), lhsT=bones[:], rhs=sq, start=True, stop=True), reads=[AB("sq"), B("bones")], writes=[*PB(b2)])
                    S.op("act", lambda e, b2=b2: e.activation(out=lnr, in_=pbank(b2), func=AF.Ln, scale=1.0 / 64, bias=cst[:, C_EPS:C_EPS + 1]), reads=[*PB(b2), B("cst")], writes=[AB("lnr")])
                    S.op("act", lambda e: e.activation(out=rstd, in_=lnr, func=AF.Exp, scale=-0.5), reads=[AB("lnr")], writes=[AB("rstd")])
                    S.op("dve", lambda e, t=t, bk=bk, j=j: e.scalar_tensor_tensor(out=qTs[:, t, :], in0=pbank(bk), scalar=cst[:, C_GQ + j:C_GQ + j + 1], in1=rstd, op0=ALU.mult, op1=ALU.mult), reads=[*PB(bk), AB("rstd"), B("cst")], writes=[AB("qTs")])

                prevq = None
                for t in range(8):
                    bk = next_bank()
                    if prevq is not None and bk == prevq[1]:
                        bk = next_bank()
                    for kt in range(8):
                        S.op("pe", lambda e, t=t, kt=kt, bk=bk, Wq=Wq: e.matmul(pbank(bk), lhsT=Wq[:, kt, t * 128:(t + 1) * 128], rhs=xnTS[:, kt, :], start=(kt == 0), stop=(kt == 7)), reads=[wa, AB("xnT")], writes=[*PB(bk)])
                    if prevq is not None:
                        q_finish(*prevq)
                    S.op("act", lambda e, bk=bk: e.activation(out=sq, in_=pbank(bk), func=AF.Square), reads=[*PB(bk)], writes=[AB("sq")])
                    prevq = (t, bk)
                q_finish(*prevq)
                for t in range(8):
                    bk = next_bank()
                    for kt in range(8):
                        S.op("pe", lambda e, t=t, kt=kt, bk=bk, Wq=Wq: e.matmul(pbank(bk), lhsT=Wq[:, kt, 1024 + t * 128:1024 + (t + 1) * 128], rhs=xnTS[:, kt, :], start=(kt == 0), stop=(kt == 7)), reads=[wa, AB("xnT")], writes=[*PB(bk)])
                    S.op("act", lambda e, t=t, bk=bk: e.activation(out=sgT[:, t, :], in_=pbank(bk), func=AF.Silu), reads=[*PB(bk)], writes=[AB("sgT")])
                prefetch_upto(siA + 2)
                scnt = [0]

                def unit_info(c):
                    gc = g * 4 + c
                    cur_tok = r * 512 + c * 128
                    prev_tok = (cur_tok - 128) % 1024
                    cur_slot = r * 4 + c
                    prev_slot = (cur_slot - 1) % 8
                    return [(0, prev_tok, prev_slot), (1, cur_tok, cur_slot)] if gc > 0 else [(1, cur_tok, cur_slot)]

                def SA(c, gg, j=j):
                    csl = slice(c * 128, (c + 1) * 128)
                    for bi, (blk, btok, bslot) in enumerate(unit_info(c)):
                        dbl = scnt[0] % 2
                        scnt[0] += 1
                        eb = ebufs[dbl]
                        for par in range(2):
                            bk = par
                            S.op("pe", lambda e, par=par, gg=gg, btok=btok, bk=bk, csl=csl: e.matmul(pbank(bk), lhsT=kz[:, par, gg, btok:btok + 128], rhs=qTs[:, 4 * gg:4 * gg + 4, csl], start=True, stop=True), reads=[B("kz"), AB("qTs")], writes=[*PB(bk)])
                        for par in range(2):
                            S.op("act", lambda e, dbl=dbl, j=j, par=par, eb=eb: e.activation(out=eb[:, par * 512:(par + 1) * 512], in_=pbank(par), func=AF.Exp, scale=0.125, bias=small[:, SM_NMQ + j:SM_NMQ + j + 1]), reads=[*PB(par), B("nmq")], writes=[AB("ebuf%d" % dbl)])
                        eng = "dve" if (scnt[0] % 2 == 0) else "pool"
                        S.op(eng, lambda e, gg=gg, bi=bi, blk=blk, eb=eb: e.tensor_tensor(out=pT[gg][:, bi, :], in0=eb, in1=expB[:, blk, gg, :], op=ALU.mult), reads=[AB("ebuf%d" % dbl), B("expB")], writes=[AB("pT%d" % gg)])

                def SB(c, gg, j=j):
                    csl = slice(c * 128, (c + 1) * 128)
                    blks = unit_info(c)
                    og_ = ogTs2[c % 2]
                    ob = 2 + 2 * ((2 * c + gg) % 2)
                    nmm = 2 * len(blks)
                    k_ = 0
                    for par in range(2):
                        for bi, (blk, btok, bslot) in enumerate(blks):
                            S.op("pe", lambda e, par=par, bi=bi, bslot=bslot, gg=gg, k_=k_, nmm=nmm, ob=ob: e.matmul(pbank(ob), lhsT=vz[:, bslot, gg, par, :], rhs=pT[gg][:, bi, par * 512:(par + 1) * 512], start=(k_ == 0), stop=(k_ == nmm - 1)), reads=[B("vz"), AB("pT%d" % gg)], writes=[*PB(ob)])
                            k_ += 1
                    k_ = 0
                    for par in range(2):
                        for bi, (blk, btok, bslot) in enumerate(blks):
                            S.op("pe", lambda e, par=par, bi=bi, gg=gg, k_=k_, nmm=nmm, ob=ob: e.matmul(pbank(ob + 1), lhsT=onesz[:, par, :], rhs=pT[gg][:, bi, par * 512:(par + 1) * 512], start=(k_ == 0), stop=(k_ == nmm - 1)), reads=[B("onesz"), AB("pT%d" % gg)], writes=[*PB(ob + 1)])
                            k_ += 1
                    S.op("dve", lambda e, gg=gg, j=j, ob=ob: e.tensor_tensor(
                        out=dens[gg].rearrange("p (t i) -> p t i", t=4), in0=pbank(ob + 1).rearrange("p (t i) -> p t i", t=4),
                        in1=small[:, SM_SE + 8 * j + 4 * gg:SM_SE + 8 * j + 4 * gg + 4].unsqueeze(2).to_broadcast([128, 4, 128]), op=ALU.add),
                        reads=[*PB(ob + 1), B("sinkexp")], writes=[AB("dens%d" % gg)])
                    S.op("dve", lambda e, gg=gg: e.reciprocal(out=dens[gg], in_=dens[gg]), reads=[AB("dens%d" % gg)], writes=[AB("dens%d" % gg)])
                    S.op("pool", lambda e, gg=gg, csl=csl: e.tensor_tensor(out=dens[gg].rearrange("p (t i) -> p t i", t=4), in0=dens[gg].rearrange("p (t i) -> p t i", t=4), in1=sgT[:, 4 * gg:4 * gg + 4, csl], op=ALU.mult), reads=[AB("dens%d" % gg), AB("sgT")], writes=[AB("dens%d" % gg)])
                    S.op("dve", lambda e, gg=gg, og_=og_, ob=ob: e.tensor_tensor(out=og_[:, 4 * gg:4 * gg + 4, :], in0=pbank(ob).rearrange("p (t i) -> p t i", t=4), in1=dens[gg].rearrange("p (t i) -> p t i", t=4), op=ALU.mult), reads=[*PB(ob), AB("dens%d" % gg)], writes=[AB("ogTs%d" % (c % 2))])

                def SC(c, j=j, Wo=Wo, wbb=wbb):
                    og_ = ogTs2[c % 2]
                    for nh in range(2):
                        for t in range(8):
                            S.op("pe", lambda e, nh=nh, t=t, Wo=Wo, og_=og_: e.matmul(pbank(6 + nh), lhsT=og_[:, t, :], rhs=Wo[:, t, nh * 512:(nh + 1) * 512], start=(t == 0), stop=(t == 7)), reads=[AB("ogTs%d" % (c % 2)), wbb], writes=[*PB(6 + nh)])
                        S.op("dve", lambda e, nh=nh, c=c: e.tensor_tensor(out=xg[:, c, nh * 512:(nh + 1) * 512], in0=pbank(6 + nh), in1=xg[:, c, nh * 512:(nh + 1) * 512], op=ALU.add), reads=[*PB(6 + nh), B("x%d" % c)], writes=[B("x%d" % c)])
                    if j == n_layers - 3:
                        store_x_chunk(g, c)
                        if g + 1 < NG:
                            load_x_chunk(g + 1, c)

                units = [(c, gg) for c in range(4) for gg in range(2)]
                pendC = []
                for i, (c, gg) in enumerate(units):
                    SA(c, gg)
                    if i >= 1:
                        pc, pg = units[i - 1]
                        if pendC:
                            SC(pendC.pop(0))
                        SB(pc, pg)
                        if pg == 1:
                            pendC.append(pc)
                if pendC:
                    SC(pendC.pop(0))
                SB(*units[-1])
                SC(units[-1][0])
            S.barrier(arena_bufs)

        if debug:
            S.dma("sync", lambda e: e.dma_start(out=dbg_kz[:, :], in_=kz[:].rearrange("p a b c -> p (a b c)")), reads=[B("kz")], writes=[B("dbg1")], sem="dbg")
            S.dma("sync", lambda e: e.dma_start(out=dbg_vz[:, :], in_=vz[:].rearrange("p a b c d -> p (a b c d)")), reads=[B("vz")], writes=[B("dbg2")], sem="dbg")
            S.dma("sync", lambda e: e.dma_start(out=dbg_q[:, :], in_=qTs.rearrange("p a b -> p (a b)")), reads=[AB("qTs")], writes=[B("dbg3")], sem="dbg")
            S.dma("sync", lambda e: e.dma_start(out=dbg_sg[:, :], in_=sgT.rearrange("p a b -> p (a b)")), reads=[AB("sgT")], writes=[B("dbg4")], sem="dbg")
            S.wait_for("sync", [B("dbg1"), B("dbg2"), B("dbg3"), B("dbg4")])
        S.wait_for("sync", [B("ydram%d" % c) for c in range(4)])
        S.emit()
    return nc


def _t5_bucket(dist):
    max_exact = 16
    dist_f = np.maximum(dist, 1).astype(np.float32)
    large = max_exact + (np.log(dist_f / np.float32(max_exact)) / np.float32(math.log(128 / max_exact)) * np.float32(32 - max_exact)).astype(np.int32)
    large = np.minimum(large, 31)
    return np.where(dist < max_exact, dist, large)


def host_prep(inputs, S_len):
    f32 = np.float32
    a_w_in = np.asarray(inputs["a_w_in"], f32)
    a_w_out = np.asarray(inputs["a_w_out"], f32)
    w_kv = np.asarray(inputs["w_kv"], f32)
    b_w_in = np.asarray(inputs["b_w_in"], f32)
    b_w_out = np.asarray(inputs["b_w_out"], f32)
    wall = np.empty((128, WTOT), f32)

    def pk(w):
        n = w.shape[1]
        return w.reshape(8, 128, n).transpose(1, 0, 2).reshape(128, 8 * n)

    for L in range(2):
        for h in range(4):
            u = L * 4 + h
            W = a_w_in[L]
            q = W[:, h * 256:(h + 1) * 256]
            k = W[:, 1024 + h * 256:1024 + (h + 1) * 256]
            v = W[:, 2048 + h * 512:2048 + (h + 1) * 512]
            gt = W[:, 4096 + h * 512:4096 + (h + 1) * 512]
            blk = np.concatenate([q[:, 0::2], q[:, 1::2], k[:, 0::2], k[:, 1::2], v, gt], axis=1)
            wall[:, UNIT_OFF[u]:UNIT_OFF[u] + 12288] = pk(blk)
            wo = a_w_out[L][h * 512:(h + 1) * 512, :]
            wall[:, UNIT_OFF[u] + 12288:UNIT_OFF[u] + 16384] = wo.reshape(4, 128, 1024).transpose(1, 0, 2).reshape(128, 4096)
    for j in range(2):
        ua = 8 + 2 * j
        ub = 9 + 2 * j
        wall[:, UNIT_OFF[ua]:UNIT_OFF[ua] + 16384] = pk(b_w_in[j])
        wall[:, UNIT_OFF[ub]:UNIT_OFF[ub] + 8192] = pk(b_w_out[j])
        if j == 0:
            kk = w_kv[:, 0:128]
            vv = w_kv[:, 128:256]
            blk = np.concatenate([kk[:, 0:64], kk[:, 0:64], kk[:, 64:128], kk[:, 64:128], vv], axis=1)
            wall[:, UNIT_OFF[ub] + 8192:UNIT_OFF[ub] + 11264] = pk(blk)

    cst = np.zeros((128, NA), f32)
    cst[:, C_IDENT:C_IDENT + 128] = np.eye(128, dtype=f32)
    bo = np.zeros((128, 128), f32)
    bo[0:64, 0:64] = 1.0
    bo[64:128, 64:128] = 1.0
    cst[:, C_BONES:C_BONES + 128] = bo
    idx = np.arange(128)
    for h in range(4):
        lg = math.log(GAMMA[h])
        diff = idx[None, :] - idx[:, None]
        dt_ = np.where(diff >= 0, np.exp(np.maximum(diff, 0) * lg), 0.0) / 16.0
        cst[:, C_DT + h * 128:C_DT + (h + 1) * 128] = dt_.astype(f32)
        cst[:, C_QD + h * 128:C_QD + (h + 1) * 128] = np.exp((idx + 1.0) * lg).astype(f32)[None, :]
        cst[:, C_KD + h] = (np.exp((127.0 - idx) * lg) / 16.0).astype(f32)
    jj = idx[:, None]
    ii = idx[None, :]
    cst[:, C_MASK:C_MASK + 128] = (jj > ii).astype(f32)
    cst[:, C_MASK + 128:C_MASK + 256] = (jj <= ii).astype(f32)
    gains = [inputs["a_norm_g"][0], inputs["a_norm_g"][1], inputs["kv_norm_g"], inputs["b_norm_g"][0], inputs["b_norm_g"][1]]
    for n, gv in enumerate(gains):
        cst[:, C_G + n * 8:C_G + n * 8 + 8] = np.asarray(gv, f32).reshape(8, 128).T
    p64 = idx % 64
    par = idx // 64
    for j in range(2):
        gq = np.asarray(inputs["b_q_norm_g"][j], f32)
        cst[:, C_GQ + j] = gq[p64]
        cst[:, C_GQR + j * 64:C_GQR + (j + 1) * 64] = gq[None, :]
        sk = np.asarray(inputs["b_sinks"][j], f32)
        for gg in range(2):
            for t in range(4):
                cst[:, C_SINK + j * 8 + gg * 4 + t] = sk[8 * gg + 2 * t + par]
    gk = np.asarray(inputs["k_norm_g"], f32)
    cst[:, C_GK] = gk[p64]
    cst[:, C_GKR:C_GKR + 64] = gk[None, :]
    cst[:, C_EPS] = EPS

    rel_bias = np.asarray(inputs["rel_bias"], f32)
    biasT = np.zeros((128, 2, 2, 2, 4, 128), f32)
    for blk in range(2):
        dist = (ii + 128 - jj) if blk == 0 else (ii - jj)
        bucket = _t5_bucket(np.maximum(dist, 0))
        for gg in range(2):
            for pr in range(2):
                for t in range(4):
                    hh = 8 * gg + 2 * t + pr
                    biasT[:, blk, gg, pr, t, :] = rel_bias[bucket, hh]
    biasT = biasT.reshape(128, 4096)

    angle = (1.0 / (np.float32(10000.0) ** np.linspace(0.0, 1.0, 128, dtype=f32))).astype(f32)
    pos = np.arange(S_len, dtype=f32)
    ang = (angle[:, None] * pos[None, :]).astype(f32)
    cosT = np.cos(ang).astype(f32)
    sinT = np.sin(ang).astype(f32)
    return {"wall": wall, "cst": cst, "biasT": biasT, "cosT": cosT, "sinT": sinT}


_CACHE = {}


def kernel(**inputs):
    x = np.asarray(inputs["x"], np.float32)
    nb, S_len, _ = x.shape
    shared = host_prep(inputs, S_len)
    key = (S_len, 4)
    if key not in _CACHE:
        _CACHE[key] = build_program(S_len, 4)
    nc = _CACHE[key]
    in_maps = []
    for b in range(nb):
        m = dict(shared)
        m["x"] = np.ascontiguousarray(x[b])
        in_maps.append(m)
    res = run_bass_kernel_spmd(nc, in_maps, core_ids=list(range(nb)))
    out = np.stack([np.asarray(r["y"], np.float32) for r in res.results], axis=0)
    return out
```

```python
import math
from contextlib import ExitStack

import numpy as np
import concourse.bass as bass
import concourse.mybir as mybir
from concourse.bass_utils import run_bass_kernel_spmd

F32 = mybir.dt.float32
BF16 = mybir.dt.bfloat16
AF = mybir.ActivationFunctionType
ALU = mybir.AluOpType
AX = mybir.AxisListType

D = 1024
EPS = 1e-6
UNIT_SIZES = [16384] * 8 + [16384, 11264, 16384, 8192]
UNIT_OFF = [int(v) for v in np.cumsum([0] + UNIT_SIZES)]
WTOT = UNIT_OFF[-1]
GAMMA = [1.0 - 2.0 ** (-5.0 - h) for h in range(4)]
CHUNK_DECAY = [g ** 128 for g in GAMMA]

C_IDENT = 0
C_BONES = 128
C_DT = 256
C_QD = 768
C_KD = 1280
C_MASK = 1284
C_G = 1540
C_GQ = 1580
C_GK = 1582
C_SINK = 1583
C_GQR = 1599
C_GKR = 1727
C_EPS = 1791
NA = 1792

COMPUTE = ("pe", "act", "dve", "pool")
QUEUES = ("sync",)


class Buf:
    __slots__ = ("name", "w", "r")

    def __init__(self, name):
        self.name = name
        self.w = None
        self.r = {}


class Sched:
    def __init__(self, nc):
        self.nc = nc
        self.streams = {e: [] for e in COMPUTE + QUEUES}
        self.known = {e: {} for e in COMPUTE + QUEUES}
        self.dmacnt = {}
        self.needed = {e: set() for e in COMPUTE}

    def _deps(self, eng, reads, writes):
        waits = {}
        kn = self.known[eng]

        def need(ev, raw):
            if ev is None:
                return
            k, v = ev
            if k == eng and not raw:
                return
            if kn.get(k, 0) >= v:
                return
            if waits.get(k, 0) < v:
                waits[k] = v

        for b in reads:
            need(b.w, True)
        for b in writes:
            need(b.w, False)
            for k, v in b.r.items():
                need((k, v), False)
        for k, v in waits.items():
            kn[k] = v
            if k in self.needed:
                self.needed[k].add(v)
        return list(waits.items())

    def _record(self, ev, reads, writes):
        k, v = ev
        for b in reads:
            if b.r.get(k, 0) < v:
                b.r[k] = v
        for b in writes:
            b.w = ev
            b.r = {}

    def op(self, eng, fn, reads=(), writes=()):
        pr = [b for b in reads if b.name.startswith("psum")]
        if pr:
            reads = [b for b in reads if not b.name.startswith("psum")]
            writes = list(writes) + pr
        waits = self._deps(eng, reads, writes)
        idx = len(self.streams[eng]) + 1
        ev = (eng, idx)
        self.streams[eng].append((fn, waits, ev, 1))
        self._record(ev, reads, writes)

    def dma(self, eng, fns, reads=(), writes=(), sem="dma"):
        if callable(fns):
            fns = [fns]
        waits = self._deps(eng, reads, writes)
        self.dmacnt[sem] = self.dmacnt.get(sem, 0) + 16 * len(fns)
        ev = (sem, self.dmacnt[sem])
        for i, fn in enumerate(fns):
            self.streams[eng].append((fn, waits if i == 0 else [], (sem, 16), 0))
        self._record(ev, reads, writes)

    def wait_for(self, eng, bufs):
        waits = self._deps(eng, bufs, bufs)
        self.streams[eng].append((None, waits, None, 0))

    def barrier(self, bufs_to_reset=()):
        pos = {e: self._last_idx(e) for e in COMPUTE}
        for e in COMPUTE + QUEUES:
            waits = []
            for k, v in pos.items():
                if k == e or v == 0:
                    continue
                if self.known[e].get(k, 0) >= v:
                    continue
                self.known[e][k] = v
                self.needed[k].add(v)
                waits.append((k, v))
            if waits:
                self.streams[e].append((None, waits, None, 0))
        for b in bufs_to_reset:
            b.w = None
            b.r = {}

    def _last_idx(self, e):
        for fn, waits, ev, kind in reversed(self.streams[e]):
            if kind == 1:
                return ev[1]
        return 0

    def op_index_fix(self):
        pass

    def emit(self):
        nc = self.nc
        rank = {}
        for e in COMPUTE:
            ids = sorted(self.needed[e])
            rank[e] = {v: i + 1 for i, v in enumerate(ids)}
        with ExitStack() as es:
            sems = {}
            for k in list(COMPUTE) + list(self.dmacnt.keys()):
                sems[k] = es.enter_context(nc.semaphore("s_" + k))
            block = es.enter_context(nc.Block())

            def run(eng, e):
                for fn, waits, ev, kind in self.streams[eng]:
                    for k, v in waits:
                        val = rank[k][v] if k in rank else v
                        e.wait_ge(sems[k], val)
                    if fn is None:
                        continue
                    ins = fn(e)
                    if kind == 1:
                        if ev[1] in rank[ev[0]]:
                            ins.then_inc(sems[ev[0]], 1)
                    else:
                        ins.then_inc(sems[ev[0]], 16)

            @block.tensor
            def _(e):
                run("pe", e)

            @block.scalar
            def _(e):
                run("act", e)

            @block.vector
            def _(e):
                run("dve", e)

            @block.gpsimd
            def _(e):
                run("pool", e)

            @block.sync
            def _(e):
                run("sync", e)


def build_program(S_len, n_layers=4, debug=False):
    NG = S_len // 512
    nc = bass.Bass("TRN2", target_bir_lowering=False)
    x_d = nc.dram_tensor("x", [S_len, D], F32, kind="ExternalInput").ap()
    y_d = nc.dram_tensor("y", [S_len, D], F32, kind="ExternalOutput").ap()
    wall_d = nc.dram_tensor("wall", [128, WTOT], F32, kind="ExternalInput").ap()
    cst_d = nc.dram_tensor("cst", [128, NA], F32, kind="ExternalInput").ap()
    bias_d = nc.dram_tensor("biasT", [128, 4096], F32, kind="ExternalInput").ap()
    cos_d = nc.dram_tensor("cosT", [128, S_len], F32, kind="ExternalInput").ap()
    sin_d = nc.dram_tensor("sinT", [128, S_len], F32, kind="ExternalInput").ap()
    scr_d = nc.dram_tensor("scr", [128, WTOT], BF16, kind="Internal").ap()
    if debug:
        dbg_kz = nc.dram_tensor("dbg_kz", [128, 4096], BF16, kind="ExternalOutput").ap()
        dbg_vz = nc.dram_tensor("dbg_vz", [128, 4096], BF16, kind="ExternalOutput").ap()
        dbg_q = nc.dram_tensor("dbg_q", [128, 4096], BF16, kind="ExternalOutput").ap()
        dbg_sg = nc.dram_tensor("dbg_sg", [128, 4096], BF16, kind="ExternalOutput").ap()

    es = ExitStack()
    with es:
        def sb(name, shape, dt):
            return es.enter_context(nc.sbuf_tensor("sb_" + name, shape, dt))

        def psum(name, shape, dt):
            return es.enter_context(nc.psum_tensor(name, shape, dt))

        S = Sched(nc)
        bufs = {}

        def B(name):
            if name not in bufs:
                bufs[name] = Buf(name)
            return bufs[name]

        cst = sb("cst", [128, NA], F32)
        ident = sb("ident", [128, 128], BF16)
        bones = sb("bones", [128, 128], BF16)
        onesz = sb("onesz", [128, 2, 128], BF16)
        expB = sb("expB", [128, 2, 2, 1024], BF16)
        state_f = sb("state_f", [128, 2, 4, 2, 512], F32)
        state_b = sb("state_b", [128, 2, 2, 512], BF16)
        kz = sb("kz", [128, 2, 2, 1024], BF16)
        vz = sb("vz", [128, 8, 2, 2, 128], BF16)
        xg = sb("xg", [128, 4, 1024], F32)
        wslot = [sb("wslot0", [128, 16384], BF16), sb("wslot1", [128, 16384], BF16)]
        small = sb("small", [128, 64], F32)
        ARENA = 29184
        arena = sb("arena", [128, ARENA], BF16)
        PS = [psum("ps%d" % i, [128, 1024], F32) for i in range(4)]

        def pbank(i):
            return PS[i // 2][:, (i % 2) * 512:(i % 2) * 512 + 512]

        def pbank_bf(i):
            return pbank(i).bitcast(BF16)

        def PB(i):
            return [B("psum%d" % i)]

        rr = [0]

        def next_bank():
            rr[0] = (rr[0] + 1) % 8
            return rr[0]

        class Carver:
            def __init__(self):
                self.pos = 0

            def bf(self, n):
                a = self.pos
                self.pos += n
                assert self.pos <= ARENA, self.pos
                return arena[:, a:a + n]

            def f32(self, n):
                a = self.pos
                self.pos += 2 * n
                assert self.pos <= ARENA, self.pos
                return arena[:, a:a + 2 * n].bitcast(F32)

        arena_bufs = []

        def AB(name):
            b = B(name)
            if b not in arena_bufs:
                arena_bufs.append(b)
            return b

        SM_SS = 0
        SM_LN = 4
        SM_RS = 8
        SM_SSO = 12
        SM_LNO = 14
        SM_RSO = 16
        SM_MQ2 = 20
        SM_MK2 = 22
        SM_NMQ = 24
        SM_SE = 32

        S.dma("sync", lambda e: e.dma_start(out=cst[:], in_=cst_d[:, :]), writes=[B("cst")], sem="cst")
        cv = Carver()
        btmp = cv.f32(4096)
        S.dma("sync", lambda e: e.dma_start(out=btmp, in_=bias_d[:, :]), writes=[AB("btmp")], sem="btmp")
        S.op("dve", lambda e: e.tensor_copy(out=ident[:], in_=cst[:, C_IDENT:C_IDENT + 128]), reads=[B("cst")], writes=[B("ident")])
        S.op("dve", lambda e: e.tensor_copy(out=bones[:], in_=cst[:, C_BONES:C_BONES + 128]), reads=[B("cst")], writes=[B("bones")])
        S.op("pool", lambda e: e.memset(onesz[:], 0.0), writes=[B("onesz")])
        S.op("pool", lambda e: e.memset(onesz[:, 0, 0:64], 1.0), writes=[B("onesz")])
        S.op("pool", lambda e: e.memset(onesz[:, 1, 64:128], 1.0), writes=[B("onesz")])
        S.op("pool", lambda e: e.memset(kz[:], 0.0), writes=[B("kz")])
        S.op("pool", lambda e: e.memset(vz[:], 0.0), writes=[B("vz")])
        S.op("pool", lambda e: e.memset(state_f[:], 0.0), writes=[B("state_f%d_%d" % (L, h)) for L in range(2) for h in range(4)])
        gq2 = cv.f32(128)
        gk2 = cv.f32(64)
        S.op("dve", lambda e: e.tensor_tensor(out=gq2, in0=cst[:, C_GQR:C_GQR + 128], in1=cst[:, C_GQR:C_GQR + 128], op=ALU.mult), reads=[B("cst")], writes=[AB("gq2")])
        S.op("dve", lambda e: e.tensor_tensor(out=gk2, in0=cst[:, C_GKR:C_GKR + 64], in1=cst[:, C_GKR:C_GKR + 64], op=ALU.mult), reads=[B("cst")], writes=[AB("gk2")])
        S.op("dve", lambda e: e.tensor_reduce(out=small[:, SM_MQ2:SM_MQ2 + 2], in_=gq2.rearrange("p (a b) -> p a b", a=2), axis=AX.X, op=ALU.max), reads=[AB("gq2")], writes=[B("mq2")])
        S.op("dve", lambda e: e.tensor_reduce(out=small[:, SM_MK2:SM_MK2 + 1], in_=gk2, axis=AX.X, op=ALU.max), reads=[AB("gk2")], writes=[B("mk2")])
        S.op("dve", lambda e: e.tensor_scalar(out=small[:, SM_NMQ:SM_NMQ + 2], in0=small[:, SM_MQ2:SM_MQ2 + 2], scalar1=small[:, SM_MK2:SM_MK2 + 1], scalar2=-4.0, op0=ALU.add, op1=ALU.mult), reads=[B("mq2"), B("mk2")], writes=[B("nmq")])
        for j in range(2):
            S.op("act", lambda e, j=j: e.activation(out=small[:, SM_SE + 8 * j:SM_SE + 8 * j + 8], in_=cst[:, C_SINK + 8 * j:C_SINK + 8 * j + 8], func=AF.Exp, bias=small[:, SM_NMQ + j:SM_NMQ + j + 1]), reads=[B("cst"), B("nmq")], writes=[B("sinkexp")])
        mneg = cv.f32(256)
        S.op("dve", lambda e: e.tensor_scalar(out=mneg, in0=cst[:, C_MASK:C_MASK + 256], scalar1=-1.0, scalar2=240000.0, op0=ALU.add, op1=ALU.mult), reads=[B("cst")], writes=[AB("mneg")])
        for blk in range(2):
            S.op("dve", lambda e, blk=blk: e.scalar_tensor_tensor(
                out=btmp[:, blk * 2048:(blk + 1) * 2048].rearrange("p (c i) -> p c i", i=128),
                in0=btmp[:, blk * 2048:(blk + 1) * 2048].rearrange("p (c i) -> p c i", i=128),
                scalar=8.0,
                in1=cst[:, C_MASK + blk * 128:C_MASK + blk * 128 + 128].unsqueeze(1).to_broadcast([128, 16, 128]),
                op0=ALU.mult, op1=ALU.mult), reads=[AB("btmp"), B("cst")], writes=[AB("btmp")])
            S.op("dve", lambda e, blk=blk: e.tensor_tensor(
                out=expB[:, blk, :, :].rearrange("p g (c i) -> p (g c) i", i=128),
                in0=btmp[:, blk * 2048:(blk + 1) * 2048].rearrange("p (c i) -> p c i", i=128),
                in1=mneg[:, blk * 128:blk * 128 + 128].unsqueeze(1).to_broadcast([128, 16, 128]),
                op=ALU.add), reads=[AB("btmp"), AB("mneg")], writes=[B("expB")])

        stage = [cv.f32(4096), cv.f32(4096)]
        pieces_of_unit = []
        for u in range(12):
            pcs = []
            if u < 8:
                L = u // 4
                for pp in range(4):
                    pcs.append((pp * 3072, 3072, [(k * 1536, 1536, L, 2 * pp + k) for k in range(2)]))
                pcs.append((12288, 4096, [(0, 4096, None, None)]))
            elif u in (8, 10):
                j = (u - 8) // 2
                for pp in range(4):
                    pcs.append((pp * 4096, 4096, [(k * 2048, 2048, 3 + j, 2 * pp + k) for k in range(2)]))
            else:
                pcs.append((0, 4096, [(0, 4096, None, None)]))
                pcs.append((4096, 4096, [(0, 4096, None, None)]))
                if u == 9:
                    pcs.append((8192, 3072, [(k * 384, 384, 2, k) for k in range(8)]))
            pieces_of_unit.append(pcs)
        pcnt = 0
        ccnt = 0
        for u in range(12):
            slot = u % 2
            wb = B("wslot%d" % slot)
            for (c0, n, subs) in pieces_of_unit[u]:
                sidx = pcnt % 2
                pcnt += 1
                st = stage[sidx]
                stb = AB("stage%d" % sidx)
                S.dma("sync", lambda e, st=st, c0=c0, n=n, u=u: e.dma_start(out=st[:, 0:n], in_=wall_d[:, UNIT_OFF[u] + c0:UNIT_OFF[u] + c0 + n]), writes=[stb], sem="pp%d" % sidx)
                for (o, m, gi, kt) in subs:
                    dst = wslot[slot][:, c0 + o:c0 + o + m]
                    src = st[:, o:o + m]
                    if gi is None:
                        half = m // 2
                        S.op("pool", lambda e, dst=dst, src=src, half=half: e.tensor_copy(out=dst[:, 0:half], in_=src[:, 0:half]), reads=[stb], writes=[wb])
                        S.op("dve", lambda e, dst=dst, src=src, half=half, m=m: e.tensor_copy(out=dst[:, half:m], in_=src[:, half:m]), reads=[stb], writes=[wb])
                    else:
                        gcol = C_G + gi * 8 + kt
                        if ccnt % 2 == 0:
                            S.op("dve", lambda e, dst=dst, src=src, gcol=gcol: e.tensor_scalar(out=dst, in0=src, scalar1=cst[:, gcol:gcol + 1], scalar2=None, op0=ALU.mult), reads=[stb, B("cst")], writes=[wb])
                        else:
                            S.op("act", lambda e, dst=dst, src=src, gcol=gcol: e.activation(out=dst, in_=src, func=AF.Copy, scale=cst[:, gcol:gcol + 1]), reads=[stb, B("cst")], writes=[wb])
                        ccnt += 1
            sz = UNIT_SIZES[u]
            S.dma("sync", lambda e, u=u, sz=sz, slot=slot: e.dma_start(out=scr_d[:, UNIT_OFF[u]:UNIT_OFF[u] + sz], in_=wslot[slot][:, 0:sz]), reads=[wb], writes=[B("scr%d" % u)], sem="scr%d" % u)

        S.barrier(arena_bufs)

        units_per_group = list(range(0, 4 * min(n_layers, 2))) + ([8, 9] if n_layers >= 3 else []) + ([10, 11] if n_layers >= 4 else [])
        useq = [(g, u) for g in range(NG) for u in units_per_group]
        loaded = [0]

        def load_unit(si):
            if si >= len(useq):
                return
            g, u = useq[si]
            slot = si % 2
            sz = UNIT_SIZES[u]
            half = sz // 2
            S.dma("sync", [lambda e, u=u, slot=slot, half=half: e.dma_start(out=wslot[slot][:, 0:half], in_=scr_d[:, UNIT_OFF[u]:UNIT_OFF[u] + half]),
                           lambda e, u=u, slot=slot, half=half, sz=sz: e.dma_start(out=wslot[slot][:, half:sz], in_=scr_d[:, UNIT_OFF[u] + half:UNIT_OFF[u] + sz])],
                  reads=[B("scr%d" % u)], writes=[B("wslot%d" % slot)], sem="w%d" % slot)

        seqpos = [0]
        issued = [0]

        def prefetch_upto(si):
            while issued[0] <= si:
                load_unit(issued[0])
                issued[0] += 1

        def take_unit():
            si = seqpos[0]
            prefetch_upto(si)
            seqpos[0] += 1
            return si

        def begin_unit():
            si = take_unit()
            prefetch_upto(si + 1)
            return si % 2

        def norm_phase(xnT, xn, junk, junkname="junk"):
            for c in range(4):
                xb = B("x%d" % c)
                S.op("act", lambda e, c=c: e.activation(out=junk, in_=xg[:, c, :], func=AF.Square, accum_out=small[:, SM_SS + c:SM_SS + c + 1]), reads=[xb], writes=[AB(junkname), B("ss%d" % c)])
            for c in range(4):
                S.op("act", lambda e, c=c: e.activation(out=small[:, SM_LN + c:SM_LN + c + 1], in_=small[:, SM_SS + c:SM_SS + c + 1], func=AF.Ln, scale=1.0 / D, bias=cst[:, C_EPS:C_EPS + 1]), reads=[B("ss%d" % c), B("cst")], writes=[B("ln%d" % c)])
            for c in range(4):
                S.op("act", lambda e, c=c: e.activation(out=small[:, SM_RS + c:SM_RS + c + 1], in_=small[:, SM_LN + c:SM_LN + c + 1], func=AF.Exp, scale=-0.5), reads=[B("ln%d" % c)], writes=[B("rs%d" % c)])
            for c in range(4):
                xb = B("x%d" % c)
                xs = xn[c % 2]
                S.op("dve", lambda e, c=c, xs=xs: e.tensor_scalar(out=xs, in0=xg[:, c, :], scalar1=small[:, SM_RS + c:SM_RS + c + 1], scalar2=None, op0=ALU.mult), reads=[xb, B("rs%d" % c)], writes=[AB("xn%d" % (c % 2))])
                bk = next_bank()
                pv = pbank_bf(bk)
                for kt in range(8):
                    S.op("pe", lambda e, kt=kt, pv=pv, xs=xs: e.transpose(out=pv[:, kt * 128:(kt + 1) * 128], in_=xs[:, kt * 128:(kt + 1) * 128], identity=ident[:]), reads=[AB("xn%d" % (c % 2)), B("ident")], writes=[*PB(bk)])
                S.op("act", lambda e, c=c, pv=pv: e.activation(out=xnT[:, :, c * 128:(c + 1) * 128], in_=pv.rearrange("p (k t) -> p k t", k=8), func=AF.Copy), reads=[*PB(bk)], writes=[AB("xnT")])

        def load_x_chunk(g, c):
            r0 = (g * 4 + c) * 128
            S.dma("sync", lambda e, c=c, r0=r0: e.dma_start(out=xg[:, c, :], in_=x_d[r0:r0 + 128, :]), writes=[B("x%d" % c)], sem="x%d" % c)

        def store_x_chunk(g, c):
            r0 = (g * 4 + c) * 128
            S.dma("sync", lambda e, c=c, r0=r0: e.dma_start(out=y_d[r0:r0 + 128, :], in_=xg[:, c, :]), reads=[B("x%d" % c)], writes=[B("ydram%d" % c)], sem="st%d" % c)

        for c in range(4):
            load_x_chunk(0, c)

        lh_counter = [0]

        for g in range(NG):
            cv = Carver()
            xn = [cv.bf(1024), cv.bf(1024)]
            junk = cv.bf(1024)
            xnT = cv.bf(4096).rearrange("p (k t) -> p k t", k=8)
            qT = cv.bf(1024).rearrange("p (d t) -> p d t", d=2)
            qdT = cv.bf(1024).rearrange("p (d t) -> p d t", d=2)
            kT = cv.bf(1024).rearrange("p (d t) -> p d t", d=2)
            rt = [cv.f32(512) for _ in range(4)]
            cs = cv.f32(1024).rearrange("p (a t) -> p a t", a=2)
            sg = cv.f32(2048).rearrange("p (c n) -> p c n", c=4)
            k_tm = [cv.bf(256) for _ in range(2)]
            v_sb = [cv.bf(512) for _ in range(2)]
            sT = [cv.bf(128) for _ in range(2)]
            og = [cv.bf(512) for _ in range(2)]
            ogT = [cv.bf(512).rearrange("p (e t) -> p e t", e=4) for _ in range(2)]

            S.dma("sync", [lambda e, g=g: e.dma_start(out=cs[:, 0, :], in_=cos_d[:, g * 512:(g + 1) * 512]),
                           lambda e, g=g: e.dma_start(out=cs[:, 1, :], in_=sin_d[:, g * 512:(g + 1) * 512])], writes=[AB("cs")], sem="cs")

            for L in range(min(n_layers, 2)):
                norm_phase(xnT, xn, junk)
                for h in range(4):
                    slot = begin_unit()
                    W = wslot[slot]
                    wb = B("wslot%d" % slot)
                    Win = W[:, 0:12288].rearrange("p (k c) -> p k c", k=8)
                    Wout = W[:, 12288:16384].rearrange("p (e n) -> p e n", e=4)
                    sbs = lh_counter[0] % 2
                    lh_counter[0] += 1
                    stf = B("state_f%d_%d" % (L, h))
                    stb_ = B("state_b%d" % sbs)
                    if g > 0:
                        S.op("pool", lambda e, L=L, h=h, sbs=sbs: e.tensor_copy(out=state_b[:, sbs, :, :], in_=state_f[:, L, h, :, :]), reads=[stf], writes=[stb_])
                    for tile in range(4):
                        for kt in range(8):
                            S.op("pe", lambda e, tile=tile, kt=kt, Win=Win: e.matmul(pbank(tile), lhsT=Win[:, kt, tile * 128:(tile + 1) * 128], rhs=xnT[:, kt, :], start=(kt == 0), stop=(kt == 7)), reads=[wb, AB("xnT")], writes=[*PB(tile)])
                    for c in range(4):
                        for kt in range(8):
                            S.op("pe", lambda e, c=c, kt=kt, Win=Win: e.matmul(pbank(4 + c), lhsT=xnT[:, kt, c * 128:(c + 1) * 128], rhs=Win[:, kt, 1024:1536], start=(kt == 0), stop=(kt == 7)), reads=[wb, AB("xnT")], writes=[*PB(4 + c)])
                        S.op("act", lambda e, c=c: e.activation(out=sg[:, c, :], in_=pbank(4 + c), func=AF.Silu), reads=[*PB(4 + c)], writes=[AB("sg%d" % c)])
                    for (pe_, po_, dst, nm) in ((0, 1, qT, "qT"), (2, 3, kT, "kT")):
                        S.op("dve", lambda e, pe_=pe_: e.tensor_tensor(out=rt[0], in0=pbank(pe_), in1=cs[:, 0, :], op=ALU.mult), reads=[*PB(pe_), AB("cs")], writes=[AB("rt0")])
                        S.op("dve", lambda e, po_=po_: e.tensor_tensor(out=rt[1], in0=pbank(po_), in1=cs[:, 1, :], op=ALU.mult), reads=[*PB(po_), AB("cs")], writes=[AB("rt1")])
                        S.op("pool", lambda e, dst=dst: e.tensor_tensor(out=dst[:, 0, :], in0=rt[0], in1=rt[1], op=ALU.subtract), reads=[AB("rt0"), AB("rt1")], writes=[AB(nm)])
                        S.op("dve", lambda e, po_=po_: e.tensor_tensor(out=rt[2], in0=pbank(po_), in1=cs[:, 0, :], op=ALU.mult), reads=[*PB(po_), AB("cs")], writes=[AB("rt2")])
                        S.op("dve", lambda e, pe_=pe_: e.tensor_tensor(out=rt[3], in0=pbank(pe_), in1=cs[:, 1, :], op=ALU.mult), reads=[*PB(pe_), AB("cs")], writes=[AB("rt3")])
                        S.op("pool", lambda e, dst=dst: e.tensor_tensor(out=dst[:, 1, :], in0=rt[2], in1=rt[3], op=ALU.add), reads=[AB("rt2"), AB("rt3")], writes=[AB(nm)])
                    for dt in range(2):
                        S.op("pool", lambda e, dt=dt, h=h: e.tensor_tensor(
                            out=qdT[:, dt, :].rearrange("p (c i) -> p c i", c=4),
                            in0=qT[:, dt, :].rearrange("p (c i) -> p c i", c=4),
                            in1=cst[:, C_QD + h * 128:C_QD + h * 128 + 128].unsqueeze(1).to_broadcast([128, 4, 128]),
                            op=ALU.mult), reads=[AB("qT"), B("cst")], writes=[AB("qdT")])
                    ktr = pbank_bf(3)[:, 0:256]
                    scps = pbank(2)[:, 128:256]
                    ogtr = pbank_bf(2)[:, 512:1024]
                    OB = 1
                    last_gc = S_len // 128 - 1

                    def stageA(c, L=L, h=h, Win=Win, wb=wb):
                        gc = g * 4 + c
                        cb = c % 2
                        csl = slice(c * 128, (c + 1) * 128)
                        if gc < last_gc:
                            for dt in range(2):
                                S.op("pe", lambda e, dt=dt, csl=csl: e.transpose(out=ktr[:, dt * 128:(dt + 1) * 128], in_=kT[:, dt, csl], identity=ident[:]), reads=[AB("kT"), B("ident")], writes=[*PB(3)])
                            S.op("act", lambda e, cb=cb, h=h: e.activation(out=k_tm[cb], in_=ktr, func=AF.Copy, scale=cst[:, C_KD + h:C_KD + h + 1]), reads=[*PB(3), B("cst")], writes=[AB("k_tm%d" % cb)])
                        for kt in range(8):
                            S.op("pe", lambda e, kt=kt, csl=csl, Win=Win: e.matmul(pbank(0), lhsT=xnT[:, kt, csl], rhs=Win[:, kt, 512:1024], start=(kt == 0), stop=(kt == 7)), reads=[wb, AB("xnT")], writes=[*PB(0)])
                        S.op("act", lambda e, cb=cb: e.activation(out=v_sb[cb], in_=pbank(0), func=AF.Copy), reads=[*PB(0)], writes=[AB("v_sb%d" % cb)])
                        for dt in range(2):
                            S.op("pe", lambda e, dt=dt, csl=csl: e.matmul(scps, lhsT=kT[:, dt, csl], rhs=qT[:, dt, csl], start=(dt == 0), stop=(dt == 1)), reads=[AB("kT"), AB("qT")], writes=[*PB(2)])
                        S.op("dve", lambda e, cb=cb, h=h: e.tensor_tensor(out=sT[cb], in0=scps, in1=cst[:, C_DT + h * 128:C_DT + h * 128 + 128], op=ALU.mult), reads=[*PB(2), B("cst")], writes=[AB("sT%d" % cb)])

                    def stageB(c, L=L, h=h, sbs=sbs, stf=stf, stb_=stb_):
                        gc = g * 4 + c
                        cb = c % 2
                        csl = slice(c * 128, (c + 1) * 128)
                        has_inter = gc > 0
                        S.op("pe", lambda e, cb=cb, has_inter=has_inter: e.matmul(pbank(OB), lhsT=sT[cb], rhs=v_sb[cb], start=True, stop=(not has_inter)), reads=[AB("sT%d" % cb), AB("v_sb%d" % cb)], writes=[*PB(OB)])
                        if has_inter:
                            for dt in range(2):
                                S.op("pe", lambda e, dt=dt, csl=csl, sbs=sbs: e.matmul(pbank(OB), lhsT=qdT[:, dt, csl], rhs=state_b[:, sbs, dt, :], start=False, stop=(dt == 1)), reads=[AB("qdT"), stb_], writes=[*PB(OB)])
                        if gc < last_gc:
                            for dt in range(2):
                                S.op("pe", lambda e, dt=dt, cb=cb: e.matmul(pbank(4 + dt), lhsT=k_tm[cb][:, dt * 128:(dt + 1) * 128], rhs=v_sb[cb], start=True, stop=True), reads=[AB("k_tm%d" % cb), AB("v_sb%d" % cb)], writes=[*PB(4 + dt)])
                                S.op("dve", lambda e, dt=dt, L=L, h=h: e.scalar_tensor_tensor(out=state_f[:, L, h, dt, :], in0=state_f[:, L, h, dt, :], scalar=CHUNK_DECAY[h], in1=pbank(4 + dt), op0=ALU.mult, op1=ALU.add), reads=[stf, *PB(4 + dt)], writes=[stf])
                            S.op("pool", lambda e, L=L, h=h, sbs=sbs: e.tensor_copy(out=state_b[:, sbs, :, :], in_=state_f[:, L, h, :, :]), reads=[stf], writes=[stb_])
                        S.op("act", lambda e, cb=cb: e.activation(out=junk[:, 0:512], in_=pbank(OB), func=AF.Square, accum_out=small[:, SM_SSO + cb:SM_SSO + cb + 1]), reads=[*PB(OB)], writes=[AB("junk"), B("sso%d" % cb)])
                        S.op("act", lambda e, cb=cb: e.activation(out=small[:, SM_LNO + cb:SM_LNO + cb + 1], in_=small[:, SM_SSO + cb:SM_SSO + cb + 1], func=AF.Ln, scale=1.0 / 512, bias=cst[:, C_EPS:C_EPS + 1]), reads=[B("sso%d" % cb), B("cst")], writes=[B("lno%d" % cb)])
                        S.op("act", lambda e, cb=cb: e.activation(out=small[:, SM_RSO + cb:SM_RSO + cb + 1], in_=small[:, SM_LNO + cb:SM_LNO + cb + 1], func=AF.Exp, scale=-0.5), reads=[B("lno%d" % cb)], writes=[B("rso%d" % cb)])
                        S.op("dve", lambda e, cb=cb, c=c: e.scalar_tensor_tensor(out=og[cb], in0=pbank(OB), scalar=small[:, SM_RSO + cb:SM_RSO + cb + 1], in1=sg[:, c, :], op0=ALU.mult, op1=ALU.mult), reads=[*PB(OB), B("rso%d" % cb), AB("sg%d" % c)], writes=[AB("og%d" % cb)])

                    def stageC(c, L=L, h=h, Wout=Wout, wb=wb):
                        cb = c % 2
                        for et in range(4):
                            S.op("pe", lambda e, et=et, cb=cb: e.transpose(out=ogtr[:, et * 128:(et + 1) * 128], in_=og[cb][:, et * 128:(et + 1) * 128], identity=ident[:]), reads=[AB("og%d" % cb), B("ident")], writes=[*PB(2)])
                        S.op("act", lambda e, cb=cb: e.activation(out=ogT[cb], in_=ogtr.rearrange("p (e t) -> p e t", e=4), func=AF.Copy), reads=[*PB(2)], writes=[AB("ogT%d" % cb)])
                        for nh in range(2):
                            for et in range(4):
                                S.op("pe", lambda e, nh=nh, et=et, cb=cb, Wout=Wout: e.matmul(pbank(6 + nh), lhsT=ogT[cb][:, et, :], rhs=Wout[:, et, nh * 512:(nh + 1) * 512], start=(et == 0), stop=(et == 3)), reads=[AB("ogT%d" % cb), wb], writes=[*PB(6 + nh)])
                            S.op("dve", lambda e, nh=nh, c=c: e.tensor_tensor(out=xg[:, c, nh * 512:(nh + 1) * 512], in0=pbank(6 + nh), in1=xg[:, c, nh * 512:(nh + 1) * 512], op=ALU.add), reads=[*PB(6 + nh), B("x%d" % c)], writes=[B("x%d" % c)])
                        if n_layers <= 2 and L == n_layers - 1 and h == 3:
                            store_x_chunk(g, c)
                            if g + 1 < NG:
                                load_x_chunk(g + 1, c)

                    for c in range(4):
                        stageA(c)
                        if c >= 1:
                            stageB(c - 1)
                        if c >= 2:
                            stageC(c - 2)
                    stageB(3)
                    stageC(2)
                    stageC(3)

            if n_layers <= 2:
                continue
            S.barrier(arena_bufs)

            cv = Carver()
            xnS = [cv.bf(1024), cv.bf(1024)]
            xnTS = cv.bf(4096).rearrange("p (k t) -> p k t", k=8)
            qTs = cv.bf(4096).rearrange("p (t n) -> p t n", t=8)
            sq = cv.bf(512)
            lnr = cv.f32(512)
            rstd = cv.f32(512)
            sgT = cv.bf(4096).rearrange("p (t n) -> p t n", t=8)
            e0_bf = cv.bf(2048)
            e1_bf = cv.bf(2048)
            ebufs = [e0_bf.bitcast(F32), e1_bf.bitcast(F32)]
            junkS = e0_bf[:, 0:1024]
            pT = [cv.bf(2048).rearrange("p (b n) -> p b n", b=2) for _ in range(2)]
            dens = [cv.f32(512) for _ in range(2)]
            ogTs2 = [cv.bf(1024).rearrange("p (t i) -> p t i", t=8) for _ in range(2)]
            r = g % 2

            def qknorm(bk, nm):
                S.op("act", lambda e: e.activation(out=sq, in_=pbank(bk), func=AF.Square), reads=[*PB(bk)], writes=[AB("sq")])
                b2 = next_bank()
                if b2 == bk:
                    b2 = next_bank()
                S.op("pe", lambda e, b2=b2: e.matmul(pbank(b2), lhsT=bones[:], rhs=sq, start=True, stop=True), reads=[AB("sq"), B("bones")], writes=[*PB(b2)])
                S.op("act", lambda e, b2=b2: e.activation(out=lnr, in_=pbank(b2), func=AF.Ln, scale=1.0 / 64, bias=cst[:, C_EPS:C_EPS + 1]), reads=[*PB(b2), B("cst")], writes=[AB("lnr")])
                S.op("act", lambda e: e.activation(out=rstd, in_=lnr, func=AF.Exp, scale=-0.5), reads=[AB("lnr")], writes=[AB("rstd")])

            for j in range(n_layers - 2):
                siA = take_unit()
                prefetch_upto(siA + 1)
                siB = take_unit()
                slotA = siA % 2
                slotB = siB % 2
                WA = wslot[slotA]
                WBt = wslot[slotB]
                wa = B("wslot%d" % slotA)
                wbb = B("wslot%d" % slotB)
                Wq = WA[:, :].rearrange("p (k c) -> p k c", k=8)
                Wo = WBt[:, 0:8192].rearrange("p (k n) -> p k n", k=8)
                norm_phase(xnTS, xnS, junkS, "ebuf0")
                if j == 0:
                    Wkv = WBt[:, 8192:11264].rearrange("p (k c) -> p k c", k=8)
                    for gg in range(2):
                        bk = next_bank()
                        for kt in range(8):
                            S.op("pe", lambda e, gg=gg, kt=kt, bk=bk, Wkv=Wkv: e.matmul(pbank(bk), lhsT=Wkv[:, kt, gg * 128:(gg + 1) * 128], rhs=xnTS[:, kt, :], start=(kt == 0), stop=(kt == 7)), reads=[wbb, AB("xnT")], writes=[*PB(bk)])
                        qknorm(bk, "k")
                        for par in range(2):
                            psl = slice(par * 64, par * 64 + 64)
                            S.op("dve", lambda e, gg=gg, par=par, psl=psl, bk=bk, r=r: e.scalar_tensor_tensor(
                                out=kz[psl, par, gg, r * 512:(r + 1) * 512], in0=pbank(bk)[psl, :], scalar=cst[psl, C_GK:C_GK + 1], in1=rstd[psl, :],
                                op0=ALU.mult, op1=ALU.mult), reads=[*PB(bk), AB("rstd"), B("cst")], writes=[B("kz")])
                    for c in range(4):
                        bk = next_bank()
                        vslot = r * 4 + c
                        for kt in range(8):
                            S.op("pe", lambda e, c=c, kt=kt, bk=bk, Wkv=Wkv: e.matmul(pbank(bk)[:, 0:128], lhsT=xnTS[:, kt, c * 128:(c + 1) * 128], rhs=Wkv[:, kt, 256:384], start=(kt == 0), stop=(kt == 7)), reads=[wbb, AB("xnT")], writes=[*PB(bk)])
                        S.op("act", lambda e, bk=bk, vslot=vslot: e.activation(out=vz[:, vslot, :, 0, 0:64], in_=pbank(bk)[:, 0:128].rearrange("p (g d) -> p g d", g=2), func=AF.Copy), reads=[*PB(bk)], writes=[B("vz")])
                        S.op("dve", lambda e, bk=bk, vslot=vslot: e.tensor_copy(out=vz[:, vslot, :, 1, 64:128], in_=pbank(bk)[:, 0:128].rearrange("p (g d) -> p g d", g=2)), reads=[*PB(bk)], writes=[B("vz")])
                def q_finish(t, bk, j=j):
                    b2 = next_bank()
                    if b2 == bk:
                        b2 = next_bank()
                    S.op("pe", lambda e, b2=b2: e.matmul(pbank(b2), lhsT=bones[:], rhs=sq, start=True, stop=True), reads=[AB("sq"), B("bones")], writes=[*PB(b2)])
                    S.op("act", lambda e, b2=b2: e.activation(out=lnr, in_=pbank(b2), func=AF.Ln, scale=1.0 / 64, bias=cst[:, C_EPS:C_EPS + 1]), reads=[*PB(b2), B("cst")], writes=[AB("lnr")])
                    S.op("act", lambda e: e.activation(out=rstd, in_=lnr, func=AF.Exp, scale=-0.5), reads=[AB("lnr")], writes=[AB("rstd")])
                    S.op("dve", lambda e, t=t, bk=bk, j=j: e.scalar_tensor_tensor(out=qTs[:, t, :], in0=pbank(bk), scalar=cst[:, C_GQ + j:C_GQ + j + 1], in1=rstd, op0=ALU.mult, op1=ALU.mult), reads=[*PB(bk), AB("rstd"), B("cst")], writes=[AB("qTs")])

                prevq = None
                for t in range(8):
                    bk = next_bank()
                    if prevq is not None and bk == prevq[1]:
                        bk = next_bank()
                    for kt in range(8):
                        S.op("pe", lambda e, t=t, kt=kt, bk=bk, Wq=Wq: e.matmul(pbank(bk), lhsT=Wq[:, kt, t * 128:(t + 1) * 128], rhs=xnTS[:, kt, :], start=(kt == 0), stop=(kt == 7)), reads=[wa, AB("xnT")], writes=[*PB(bk)])
                    if prevq is not None:
                        q_finish(*prevq)
                    S.op("act", lambda e, bk=bk: e.activation(out=sq, in_=pbank(bk), func=AF.Square), reads=[*PB(bk)], writes=[AB("sq")])
                    prevq = (t, bk)
                q_finish(*prevq)
                for t in range(8):
                    bk = next_bank()
                    for kt in range(8):
                        S.op("pe", lambda e, t=t, kt=kt, bk=bk, Wq=Wq: e.matmul(pbank(bk), lhsT=Wq[:, kt, 1024 + t * 128:1024 + (t + 1) * 128], rhs=xnTS[:, kt, :], start=(kt == 0), stop=(kt == 7)), reads=[wa, AB("xnT")], writes=[*PB(bk)])
                    S.op("act", lambda e, t=t, bk=bk: e.activation(out=sgT[:, t, :], in_=pbank(bk), func=AF.Silu), reads=[*PB(bk)], writes=[AB("sgT")])
                prefetch_upto(siA + 2)
                scnt = [0]

                def unit_info(c):
                    gc = g * 4 + c
                    cur_tok = r * 512 + c * 128
                    prev_tok = (cur_tok - 128) % 1024
                    cur_slot = r * 4 + c
                    prev_slot = (cur_slot - 1) % 8
                    return [(0, prev_tok, prev_slot), (1, cur_tok, cur_slot)] if gc > 0 else [(1, cur_tok, cur_slot)]

                def SA(c, gg, j=j):
                    csl = slice(c * 128, (c + 1) * 128)
                    for bi, (blk, btok, bslot) in enumerate(unit_info(c)):
                        for par in range(2):
                            bk = par
                            S.op("pe", lambda e, par=par, gg=gg, btok=btok, bk=bk, csl=csl: e.matmul(pbank(bk), lhsT=kz[:, par, gg, btok:btok + 128], rhs=qTs[:, 4 * gg:4 * gg + 4, csl], start=True, stop=False), reads=[B("kz"), AB("qTs")], writes=[*PB(bk)])
                            S.op("pe", lambda e, par=par, gg=gg, blk=blk, bk=bk: e.matmul(pbank(bk), lhsT=ident[:], rhs=expB[:, blk, gg, par * 512:(par + 1) * 512], start=False, stop=True), reads=[B("ident"), B("expB")], writes=[*PB(bk)])
                        for par in range(2):
                            S.op("act", lambda e, j=j, par=par, gg=gg, bi=bi: e.activation(out=pT[gg][:, bi, par * 512:(par + 1) * 512], in_=pbank(par), func=AF.Exp, scale=0.125, bias=small[:, SM_NMQ + j:SM_NMQ + j + 1]), reads=[*PB(par), B("nmq")], writes=[AB("pT%d" % gg)])

                def SB(c, gg, j=j):
                    csl = slice(c * 128, (c + 1) * 128)
                    blks = unit_info(c)
                    og_ = ogTs2[c % 2]
                    ob = 2 + 2 * ((2 * c + gg) % 2)
                    nmm = 2 * len(blks)
                    k_ = 0
                    for par in range(2):
                        for bi, (blk, btok, bslot) in enumerate(blks):
                            S.op("pe", lambda e, par=par, bi=bi, bslot=bslot, gg=gg, k_=k_, nmm=nmm, ob=ob: e.matmul(pbank(ob), lhsT=vz[:, bslot, gg, par, :], rhs=pT[gg][:, bi, par * 512:(par + 1) * 512], start=(k_ == 0), stop=(k_ == nmm - 1)), reads=[B("vz"), AB("pT%d" % gg)], writes=[*PB(ob)])
                            k_ += 1
                    k_ = 0
                    for par in range(2):
                        for bi, (blk, btok, bslot) in enumerate(blks):
                            S.op("pe", lambda e, par=par, bi=bi, gg=gg, k_=k_, nmm=nmm, ob=ob: e.matmul(pbank(ob + 1), lhsT=onesz[:, par, :], rhs=pT[gg][:, bi, par * 512:(par + 1) * 512], start=(k_ == 0), stop=(k_ == nmm - 1)), reads=[B("onesz"), AB("pT%d" % gg)], writes=[*PB(ob + 1)])
                            k_ += 1
                    S.op("dve", lambda e, gg=gg, j=j, ob=ob: e.tensor_tensor(
                        out=dens[gg].rearrange("p (t i) -> p t i", t=4), in0=pbank(ob + 1).rearrange("p (t i) -> p t i", t=4),
                        in1=small[:, SM_SE + 8 * j + 4 * gg:SM_SE + 8 * j + 4 * gg + 4].unsqueeze(2).to_broadcast([128, 4, 128]), op=ALU.add),
                        reads=[*PB(ob + 1), B("sinkexp")], writes=[AB("dens%d" % gg)])
                    S.op("act", lambda e, gg=gg: e.activation(out=dens[gg], in_=dens[gg], func=AF.Ln), reads=[AB("dens%d" % gg)], writes=[AB("dens%d" % gg)])
                    S.op("act", lambda e, gg=gg: e.activation(out=dens[gg], in_=dens[gg], func=AF.Exp, scale=-1.0), reads=[AB("dens%d" % gg)], writes=[AB("dens%d" % gg)])
                    S.op("pool", lambda e, gg=gg, csl=csl: e.tensor_tensor(out=dens[gg].rearrange("p (t i) -> p t i", t=4), in0=dens[gg].rearrange("p (t i) -> p t i", t=4), in1=sgT[:, 4 * gg:4 * gg + 4, csl], op=ALU.mult), reads=[AB("dens%d" % gg), AB("sgT")], writes=[AB("dens%d" % gg)])
                    S.op("dve", lambda e, gg=gg, og_=og_, ob=ob: e.tensor_tensor(out=og_[:, 4 * gg:4 * gg + 4, :], in0=pbank(ob).rearrange("p (t i) -> p t i", t=4), in1=dens[gg].rearrange("p (t i) -> p t i", t=4), op=ALU.mult), reads=[*PB(ob), AB("dens%d" % gg)], writes=[AB("ogTs%d" % (c % 2))])

                def SC(c, j=j, Wo=Wo, wbb=wbb):
                    og_ = ogTs2[c % 2]
                    for nh in range(2):
                        for t in range(8):
                            S.op("pe", lambda e, nh=nh, t=t, Wo=Wo, og_=og_: e.matmul(pbank(6 + nh), lhsT=og_[:, t, :], rhs=Wo[:, t, nh * 512:(nh + 1) * 512], start=(t == 0), stop=(t == 7)), reads=[AB("ogTs%d" % (c % 2)), wbb], writes=[*PB(6 + nh)])
                        S.op("dve", lambda e, nh=nh, c=c: e.tensor_tensor(out=xg[:, c, nh * 512:(nh + 1) * 512], in0=pbank(6 + nh), in1=xg[:, c, nh * 512:(nh + 1) * 512], op=ALU.add), reads=[*PB(6 + nh), B("x%d" % c)], writes=[B("x%d" % c)])
                    if j == n_layers - 3:
                        store_x_chunk(g, c)
                        if g + 1 < NG:
                            load_x_chunk(g + 1, c)

                units = [(c, gg) for c in range(4) for gg in range(2)]
                pendC = []
                for i, (c, gg) in enumerate(units):
                    SA(c, gg)
                    if i >= 1:
                        pc, pg = units[i - 1]
                        if pendC:
                            SC(pendC.pop(0))
                        SB(pc, pg)
                        if pg == 1:
                            pendC.append(pc)
                if pendC:
                    SC(pendC.pop(0))
                SB(*units[-1])
                SC(units[-1][0])
            S.barrier(arena_bufs)

        if debug:
            S.dma("sync", lambda e: e.dma_start(out=dbg_kz[:, :], in_=kz[:].rearrange("p a b c -> p (a b c)")), reads=[B("kz")], writes=[B("dbg1")], sem="dbg")
            S.dma("sync", lambda e: e.dma_start(out=dbg_vz[:, :], in_=vz[:].rearrange("p a b c d -> p (a b c d)")), reads=[B("vz")], writes=[B("dbg2")], sem="dbg")
            S.dma("sync", lambda e: e.dma_start(out=dbg_q[:, :], in_=qTs.rearrange("p a b -> p (a b)")), reads=[AB("qTs")], writes=[B("dbg3")], sem="dbg")
            S.dma("sync", lambda e: e.dma_start(out=dbg_sg[:, :], in_=sgT.rearrange("p a b -> p (a b)")), reads=[AB("sgT")], writes=[B("dbg4")], sem="dbg")
            S.wait_for("sync", [B("dbg1"), B("dbg2"), B("dbg3"), B("dbg4")])
        S.wait_for("sync", [B("ydram%d" % c) for c in range(4)])
        S.emit()
    return nc


def _t5_bucket(dist):
    max_exact = 16
    dist_f = np.maximum(dist, 1).astype(np.float32)
    large = max_exact + (np.log(dist_f / np.float32(max_exact)) / np.float32(math.log(128 / max_exact)) * np.float32(32 - max_exact)).astype(np.int32)
    large = np.minimum(large, 31)
    return np.where(dist < max_exact, dist, large)


def host_prep(inputs, S_len):
    f32 = np.float32
    a_w_in = np.asarray(inputs["a_w_in"], f32)
    a_w_out = np.asarray(inputs["a_w_out"], f32)
    w_kv = np.asarray(inputs["w_kv"], f32)
    b_w_in = np.asarray(inputs["b_w_in"], f32)
    b_w_out = np.asarray(inputs["b_w_out"], f32)
    wall = np.empty((128, WTOT), f32)

    def pk(w):
        n = w.shape[1]
        return w.reshape(8, 128, n).transpose(1, 0, 2).reshape(128, 8 * n)

    for L in range(2):
        for h in range(4):
            u = L * 4 + h
            W = a_w_in[L]
            q = W[:, h * 256:(h + 1) * 256]
            k = W[:, 1024 + h * 256:1024 + (h + 1) * 256]
            v = W[:, 2048 + h * 512:2048 + (h + 1) * 512]
            gt = W[:, 4096 + h * 512:4096 + (h + 1) * 512]
            blk = np.concatenate([q[:, 0::2], q[:, 1::2], k[:, 0::2], k[:, 1::2], v, gt], axis=1)
            wall[:, UNIT_OFF[u]:UNIT_OFF[u] + 12288] = pk(blk)
            wo = a_w_out[L][h * 512:(h + 1) * 512, :]
            wall[:, UNIT_OFF[u] + 12288:UNIT_OFF[u] + 16384] = wo.reshape(4, 128, 1024).transpose(1, 0, 2).reshape(128, 4096)
    for j in range(2):
        ua = 8 + 2 * j
        ub = 9 + 2 * j
        wall[:, UNIT_OFF[ua]:UNIT_OFF[ua] + 16384] = pk(b_w_in[j])
        wall[:, UNIT_OFF[ub]:UNIT_OFF[ub] + 8192] = pk(b_w_out[j])
        if j == 0:
            kk = w_kv[:, 0:128]
            vv = w_kv[:, 128:256]
            blk = np.concatenate([kk[:, 0:64], kk[:, 0:64], kk[:, 64:128], kk[:, 64:128], vv], axis=1)
            wall[:, UNIT_OFF[ub] + 8192:UNIT_OFF[ub] + 11264] = pk(blk)

    cst = np.zeros((128, NA), f32)
    cst[:, C_IDENT:C_IDENT + 128] = np.eye(128, dtype=f32)
    bo = np.zeros((128, 128), f32)
    bo[0:64, 0:64] = 1.0
    bo[64:128, 64:128] = 1.0
    cst[:, C_BONES:C_BONES + 128] = bo
    idx = np.arange(128)
    for h in range(4):
        lg = math.log(GAMMA[h])
        diff = idx[None, :] - idx[:, None]
        dt_ = np.where(diff >= 0, np.exp(np.maximum(diff, 0) * lg), 0.0) / 16.0
        cst[:, C_DT + h * 128:C_DT + (h + 1) * 128] = dt_.astype(f32)
        cst[:, C_QD + h * 128:C_QD + (h + 1) * 128] = np.exp((idx + 1.0) * lg).astype(f32)[None, :]
        cst[:, C_KD + h] = (np.exp((127.0 - idx) * lg) / 16.0).astype(f32)
    jj = idx[:, None]
    ii = idx[None, :]
    cst[:, C_MASK:C_MASK + 128] = (jj > ii).astype(f32)
    cst[:, C_MASK + 128:C_MASK + 256] = (jj <= ii).astype(f32)
    gains = [inputs["a_norm_g"][0], inputs["a_norm_g"][1], inputs["kv_norm_g"], inputs["b_norm_g"][0], inputs["b_norm_g"][1]]
    for n, gv in enumerate(gains):
        cst[:, C_G + n * 8:C_G + n * 8 + 8] = np.asarray(gv, f32).reshape(8, 128).T
    p64 = idx % 64
    par = idx // 64
    for j in range(2):
        gq = np.asarray(inputs["b_q_norm_g"][j], f32)
        cst[:, C_GQ + j] = gq[p64]
        cst[:, C_GQR + j * 64:C_GQR + (j + 1) * 64] = gq[None, :]
        sk = np.asarray(inputs["b_sinks"][j], f32)
        for gg in range(2):
            for t in range(4):
                cst[:, C_SINK + j * 8 + gg * 4 + t] = sk[8 * gg + 2 * t + par]
    gk = np.asarray(inputs["k_norm_g"], f32)
    cst[:, C_GK] = gk[p64]
    cst[:, C_GKR:C_GKR + 64] = gk[None, :]
    cst[:, C_EPS] = EPS

    rel_bias = np.asarray(inputs["rel_bias"], f32)
    biasT = np.zeros((128, 2, 2, 2, 4, 128), f32)
    for blk in range(2):
        dist = (ii + 128 - jj) if blk == 0 else (ii - jj)
        bucket = _t5_bucket(np.maximum(dist, 0))
        for gg in range(2):
            for pr in range(2):
                for t in range(4):
                    hh = 8 * gg + 2 * t + pr
                    biasT[:, blk, gg, pr, t, :] = rel_bias[bucket, hh]
    biasT = biasT.reshape(128, 4096)

    angle = (1.0 / (np.float32(10000.0) ** np.linspace(0.0, 1.0, 128, dtype=f32))).astype(f32)
    pos = np.arange(S_len, dtype=f32)
    ang = (angle[:, None] * pos[None, :]).astype(f32)
    cosT = np.cos(ang).astype(f32)
    sinT = np.sin(ang).astype(f32)
    return {"wall": wall, "cst": cst, "biasT": biasT, "cosT": cosT, "sinT": sinT}


_CACHE = {}


def kernel(**inputs):
    x = np.asarray(inputs["x"], np.float32)
    nb, S_len, _ = x.shape
    shared = host_prep(inputs, S_len)
    key = (S_len, 4)
    if key not in _CACHE:
        _CACHE[key] = build_program(S_len, 4)
    nc = _CACHE[key]
    in_maps = []
    for b in range(nb):
        m = dict(shared)
        m["x"] = np.ascontiguousarray(x[b])
        in_maps.append(m)
    res = run_bass_kernel_spmd(nc, in_maps, core_ids=list(range(nb)))
    out = np.stack([np.asarray(r["y"], np.float32) for r in res.results], axis=0)
    return out
```

```python
import math
from contextlib import ExitStack

import numpy as np
import concourse.bass as bass
import concourse.mybir as mybir
from concourse.bass_utils import run_bass_kernel_spmd

F32 = mybir.dt.float32
BF16 = mybir.dt.bfloat16
AF = mybir.ActivationFunctionType
ALU = mybir.AluOpType
AX = mybir.AxisListType

D = 1024
EPS = 1e-6
UNIT_SIZES = [16384] * 8 + [16384, 11264, 16384, 8192]
UNIT_OFF = [int(v) for v in np.cumsum([0] + UNIT_SIZES)]
WTOT = UNIT_OFF[-1]
GAMMA = [1.0 - 2.0 ** (-5.0 - h) for h in range(4)]
CHUNK_DECAY = [g ** 128 for g in GAMMA]

C_IDENT = 0
C_BONES = 128
C_DT = 256
C_QD = 768
C_KD = 1280
C_MASK = 1284
C_G = 1540
C_GQ = 1580
C_GK = 1582
C_SINK = 1583
C_GQR = 1599
C_GKR = 1727
C_EPS = 1791
NA = 1792

COMPUTE = ("pe", "act", "dve", "pool")
QUEUES = ("sync",)


class Buf:
    __slots__ = ("name", "w", "r")

    def __init__(self, name):
        self.name = name
        self.w = None
        self.r = {}


class Sched:
    def __init__(self, nc):
        self.nc = nc
        self.streams = {e: [] for e in COMPUTE + QUEUES}
        self.known = {e: {} for e in COMPUTE + QUEUES}
        self.dmacnt = {}
        self.needed = {e: set() for e in COMPUTE}

    def _deps(self, eng, reads, writes):
        waits = {}
        kn = self.known[eng]

        def need(ev, raw):
            if ev is None:
                return
            k, v = ev
            if k == eng and not raw:
                return
            if kn.get(k, 0) >= v:
                return
            if waits.get(k, 0) < v:
                waits[k] = v

        for b in reads:
            need(b.w, True)
        for b in writes:
            need(b.w, False)
            for k, v in b.r.items():
                need((k, v), False)
        for k, v in waits.items():
            kn[k] = v
            if k in self.needed:
                self.needed[k].add(v)
        return list(waits.items())

    def _record(self, ev, reads, writes):
        k, v = ev
        for b in reads:
            if b.r.get(k, 0) < v:
                b.r[k] = v
        for b in writes:
            b.w = ev
            b.r = {}

    def op(self, eng, fn, reads=(), writes=()):
        pr = [b for b in reads if b.name.startswith("psum")]
        if pr:
            reads = [b for b in reads if not b.name.startswith("psum")]
            writes = list(writes) + pr
        waits = self._deps(eng, reads, writes)
        idx = len(self.streams[eng]) + 1
        ev = (eng, idx)
        self.streams[eng].append((fn, waits, ev, 1))
        self._record(ev, reads, writes)

    def dma(self, eng, fns, reads=(), writes=(), sem="dma"):
        if callable(fns):
            fns = [fns]
        waits = self._deps(eng, reads, writes)
        self.dmacnt[sem] = self.dmacnt.get(sem, 0) + 16 * len(fns)
        ev = (sem, self.dmacnt[sem])
        for i, fn in enumerate(fns):
            self.streams[eng].append((fn, waits if i == 0 else [], (sem, 16), 0))
        self._record(ev, reads, writes)

    def wait_for(self, eng, bufs):
        waits = self._deps(eng, bufs, bufs)
        self.streams[eng].append((None, waits, None, 0))

    def barrier(self, bufs_to_reset=()):
        pos = {e: self._last_idx(e) for e in COMPUTE}
        for e in COMPUTE + QUEUES:
            waits = []
            for k, v in pos.items():
                if k == e or v == 0:
                    continue
                if self.known[e].get(k, 0) >= v:
                    continue
                self.known[e][k] = v
                self.needed[k].add(v)
                waits.append((k, v))
            if waits:
                self.streams[e].append((None, waits, None, 0))
        for b in bufs_to_reset:
            b.w = None
            b.r = {}

    def _last_idx(self, e):
        for fn, waits, ev, kind in reversed(self.streams[e]):
            if kind == 1:
                return ev[1]
        return 0

    def op_index_fix(self):
        pass

    def emit(self):
        nc = self.nc
        rank = {}
        for e in COMPUTE:
            ids = sorted(self.needed[e])
            rank[e] = {v: i + 1 for i, v in enumerate(ids)}
        with ExitStack() as es:
            sems = {}
            for k in list(COMPUTE) + list(self.dmacnt.keys()):
                sems[k] = es.enter_context(nc.semaphore("s_" + k))
            block = es.enter_context(nc.Block())

            def run(eng, e):
                for fn, waits, ev, kind in self.streams[eng]:
                    for k, v in waits:
                        val = rank[k][v] if k in rank else v
                        e.wait_ge(sems[k], val)
                    if fn is None:
                        continue
                    ins = fn(e)
                    if kind == 1:
                        if ev[1] in rank[ev[0]]:
                            ins.then_inc(sems[ev[0]], 1)
                    else:
                        ins.then_inc(sems[ev[0]], 16)

            @block.tensor
            def _(e):
                run("pe", e)

            @block.scalar
            def _(e):
                run("act", e)

            @block.vector
            def _(e):
                run("dve", e)

            @block.gpsimd
            def _(e):
                run("pool", e)

            @block.sync
            def _(e):
                run("sync", e)


def build_program(S_len, n_layers=4, debug=False):
    NG = S_len // 512
    nc = bass.Bass("TRN2", target_bir_lowering=False)
    x_d = nc.dram_tensor("x", [S_len, D], F32, kind="ExternalInput").ap()
    y_d = nc.dram_tensor("y", [S_len, D], F32, kind="ExternalOutput").ap()
    wall_d = nc.dram_tensor("wall", [128, WTOT], F32, kind="ExternalInput").ap()
    cst_d = nc.dram_tensor("cst", [128, NA], F32, kind="ExternalInput").ap()
    bias_d = nc.dram_tensor("biasT", [128, 4096], F32, kind="ExternalInput").ap()
    cos_d = nc.dram_tensor("cosT", [128, S_len], F32, kind="ExternalInput").ap()
    sin_d = nc.dram_tensor("sinT", [128, S_len], F32, kind="ExternalInput").ap()
    scr_d = nc.dram_tensor("scr", [128, WTOT], BF16, kind="Internal").ap()
    if debug:
        dbg_kz = nc.dram_tensor("dbg_kz", [128, 4096], BF16, kind="ExternalOutput").ap()
        dbg_vz = nc.dram_tensor("dbg_vz", [128, 4096], BF16, kind="ExternalOutput").ap()
        dbg_q = nc.dram_tensor("dbg_q", [128, 4096], BF16, kind="ExternalOutput").ap()
        dbg_sg = nc.dram_tensor("dbg_sg", [128, 4096], BF16, kind="ExternalOutput").ap()

    es = ExitStack()
    with es:
        def sb(name, shape, dt):
            return es.enter_context(nc.sbuf_tensor("sb_" + name, shape, dt))

        def psum(name, shape, dt):
            return es.enter_context(nc.psum_tensor(name, shape, dt))

        S = Sched(nc)
        bufs = {}

        def B(name):
            if name not in bufs:
                bufs[name] = Buf(name)
            return bufs[name]

        cst = sb("cst", [128, NA], F32)
        ident = sb("ident", [128, 128], BF16)
        bones = sb("bones", [128, 128], BF16)
        onesz = sb("onesz", [128, 2, 128], BF16)
        expB = sb("expB", [128, 2, 2, 1024], BF16)
        state_f = sb("state_f", [128, 2, 4, 2, 512], F32)
        state_b = sb("state_b", [128, 2, 2, 512], BF16)
        kz = sb("kz", [128, 2, 2, 1024], BF16)
        vz = sb("vz", [128, 8, 2, 2, 128], BF16)
        xg = sb("xg", [128, 4, 1024], F32)
        wslot = [sb("wslot0", [128, 16384], BF16), sb("wslot1", [128, 16384], BF16)]
        small = sb("small", [128, 64], F32)
        ARENA = 29184
        arena = sb("arena", [128, ARENA], BF16)
        PS = [psum("ps%d" % i, [128, 1024], F32) for i in range(4)]

        def pbank(i):
            return PS[i // 2][:, (i % 2) * 512:(i % 2) * 512 + 512]

        def pbank_bf(i):
            return pbank(i).bitcast(BF16)

        def PB(i):
            return [B("psum%d" % i)]

        rr = [0]

        def next_bank():
            rr[0] = (rr[0] + 1) % 8
            return rr[0]

        class Carver:
            def __init__(self):
                self.pos = 0

            def bf(self, n):
                a = self.pos
                self.pos += n
                assert self.pos <= 26112, self.pos
                return arena[:, a:a + n]

            def f32(self, n):
                a = self.pos
                self.pos += 2 * n
                assert self.pos <= 26112, self.pos
                return arena[:, a:a + 2 * n].bitcast(F32)

        arena_bufs = []

        def AB(name):
            b = B(name)
            if b not in arena_bufs:
                arena_bufs.append(b)
            return b

        SM_SS = 0
        SM_LN = 4
        SM_RS = 8
        SM_SSO = 12
        SM_LNO = 14
        SM_RSO = 16
        SM_MQ2 = 20
        SM_MK2 = 22
        SM_NMQ = 24
        SM_SE = 32

        S.dma("sync", lambda e: e.dma_start(out=cst[:], in_=cst_d[:, :]), writes=[B("cst")], sem="cst")
        cv = Carver()
        btmp = cv.f32(4096)
        S.dma("sync", lambda e: e.dma_start(out=btmp, in_=bias_d[:, :]), writes=[AB("btmp")], sem="btmp")
        S.op("dve", lambda e: e.tensor_copy(out=ident[:], in_=cst[:, C_IDENT:C_IDENT + 128]), reads=[B("cst")], writes=[B("ident")])
        S.op("dve", lambda e: e.tensor_copy(out=bones[:], in_=cst[:, C_BONES:C_BONES + 128]), reads=[B("cst")], writes=[B("bones")])
        S.op("pool", lambda e: e.memset(onesz[:], 0.0), writes=[B("onesz")])
        S.op("pool", lambda e: e.memset(onesz[:, 0, 0:64], 1.0), writes=[B("onesz")])
        S.op("pool", lambda e: e.memset(onesz[:, 1, 64:128], 1.0), writes=[B("onesz")])
        S.op("pool", lambda e: e.memset(kz[:], 0.0), writes=[B("kz")])
        S.op("pool", lambda e: e.memset(vz[:], 0.0), writes=[B("vz")])
        S.op("pool", lambda e: e.memset(state_f[:], 0.0), writes=[B("state_f%d_%d" % (L, h)) for L in range(2) for h in range(4)])
        gq2 = cv.f32(128)
        gk2 = cv.f32(64)
        S.op("dve", lambda e: e.tensor_tensor(out=gq2, in0=cst[:, C_GQR:C_GQR + 128], in1=cst[:, C_GQR:C_GQR + 128], op=ALU.mult), reads=[B("cst")], writes=[AB("gq2")])
        S.op("dve", lambda e: e.tensor_tensor(out=gk2, in0=cst[:, C_GKR:C_GKR + 64], in1=cst[:, C_GKR:C_GKR + 64], op=ALU.mult), reads=[B("cst")], writes=[AB("gk2")])
        S.op("dve", lambda e: e.tensor_reduce(out=small[:, SM_MQ2:SM_MQ2 + 2], in_=gq2.rearrange("p (a b) -> p a b", a=2), axis=AX.X, op=ALU.max), reads=[AB("gq2")], writes=[B("mq2")])
        S.op("dve", lambda e: e.tensor_reduce(out=small[:, SM_MK2:SM_MK2 + 1], in_=gk2, axis=AX.X, op=ALU.max), reads=[AB("gk2")], writes=[B("mk2")])
        S.op("dve", lambda e: e.tensor_scalar(out=small[:, SM_NMQ:SM_NMQ + 2], in0=small[:, SM_MQ2:SM_MQ2 + 2], scalar1=small[:, SM_MK2:SM_MK2 + 1], scalar2=-4.0, op0=ALU.add, op1=ALU.mult), reads=[B("mq2"), B("mk2")], writes=[B("nmq")])
        for j in range(2):
            S.op("act", lambda e, j=j: e.activation(out=small[:, SM_SE + 8 * j:SM_SE + 8 * j + 8], in_=cst[:, C_SINK + 8 * j:C_SINK + 8 * j + 8], func=AF.Exp, bias=small[:, SM_NMQ + j:SM_NMQ + j + 1]), reads=[B("cst"), B("nmq")], writes=[B("sinkexp")])
        mneg = cv.f32(256)
        S.op("dve", lambda e: e.tensor_scalar(out=mneg, in0=cst[:, C_MASK:C_MASK + 256], scalar1=-1.0, scalar2=240000.0, op0=ALU.add, op1=ALU.mult), reads=[B("cst")], writes=[AB("mneg")])
        for blk in range(2):
            S.op("dve", lambda e, blk=blk: e.scalar_tensor_tensor(
                out=btmp[:, blk * 2048:(blk + 1) * 2048].rearrange("p (c i) -> p c i", i=128),
                in0=btmp[:, blk * 2048:(blk + 1) * 2048].rearrange("p (c i) -> p c i", i=128),
                scalar=8.0,
                in1=cst[:, C_MASK + blk * 128:C_MASK + blk * 128 + 128].unsqueeze(1).to_broadcast([128, 16, 128]),
                op0=ALU.mult, op1=ALU.mult), reads=[AB("btmp"), B("cst")], writes=[AB("btmp")])
            S.op("dve", lambda e, blk=blk: e.tensor_tensor(
                out=expB[:, blk, :, :].rearrange("p g (c i) -> p (g c) i", i=128),
                in0=btmp[:, blk * 2048:(blk + 1) * 2048].rearrange("p (c i) -> p c i", i=128),
                in1=mneg[:, blk * 128:blk * 128 + 128].unsqueeze(1).to_broadcast([128, 16, 128]),
                op=ALU.add), reads=[AB("btmp"), AB("mneg")], writes=[B("expB")])

        STG0 = 26112
        stg = [arena[:, STG0:STG0 + 1536].bitcast(F32), arena[:, STG0 + 1536:STG0 + 3072].bitcast(F32)]
        PIECE = 768
        segs_of_unit = []
        for u in range(12):
            sg_ = []
            if u < 8:
                L = u // 4
                for k in range(8):
                    sg_.append((k * 1536, 1536, L, k))
                sg_.append((12288, 4096, None, None))
            elif u in (8, 10):
                j = (u - 8) // 2
                for k in range(8):
                    sg_.append((k * 2048, 2048, 3 + j, k))
            else:
                sg_.append((0, 8192, None, None))
                if u == 9:
                    for k in range(8):
                        sg_.append((8192 + k * 384, 384, 2, k))
            pcs = []
            for (o, n, gi, kt) in sg_:
                p = 0
                while p < n:
                    m = min(PIECE, n - p)
                    pcs.append((o + p, m, gi, kt))
                    p += m
            segs_of_unit.append(pcs)
        conv_queue = []
        conv_issued = []
        conv_limit = [-1]
        conv_done_units = set()
        conv_cnt = [0, 0]

        def conv_issue():
            if not conv_queue or len(conv_issued) >= 2:
                return False
            pc = conv_queue[0]
            if pc[0] > conv_limit[0]:
                return False
            conv_queue.pop(0)
            k = conv_cnt[0] % 2
            conv_cnt[0] += 1
            (si_, u, off, m, gi, kt, last) = pc
            st = stg[k]
            S.dma("sync", lambda e, st=st, u=u, off=off, m=m: e.dma_start(out=st[:, 0:m], in_=wall_d[:, UNIT_OFF[u] + off:UNIT_OFF[u] + off + m]), writes=[B("stg%d" % k)], sem="pp%d" % k)
            conv_issued.append((k, pc))
            return True

        def conv_convert():
            k, (si_, u, off, m, gi, kt, last) = conv_issued.pop(0)
            slot = si_ % 2
            wb_ = B("wslot%d" % slot)
            dst = wslot[slot][:, off:off + m]
            src = stg[k][:, 0:m]
            stb = B("stg%d" % k)
            c_ = conv_cnt[1]
            conv_cnt[1] += 1
            if gi is None:
                eng = ("pool", "dve", "act")[c_ % 3]
                if eng == "act":
                    S.op("act", lambda e, dst=dst, src=src: e.activation(out=dst, in_=src, func=AF.Copy), reads=[stb], writes=[wb_])
                else:
                    S.op(eng, lambda e, dst=dst, src=src: e.tensor_copy(out=dst, in_=src), reads=[stb], writes=[wb_])
            else:
                gcol = C_G + gi * 8 + kt
                if c_ % 2 == 0:
                    S.op("dve", lambda e, dst=dst, src=src, gcol=gcol: e.tensor_scalar(out=dst, in0=src, scalar1=cst[:, gcol:gcol + 1], scalar2=None, op0=ALU.mult), reads=[stb, B("cst")], writes=[wb_])
                else:
                    S.op("act", lambda e, dst=dst, src=src, gcol=gcol: e.activation(out=dst, in_=src, func=AF.Copy, scale=cst[:, gcol:gcol + 1]), reads=[stb, B("cst")], writes=[wb_])
            if last:
                sz = UNIT_SIZES[u]
                S.dma("sync", lambda e, u=u, sz=sz, slot=slot: e.dma_start(out=scr_d[:, UNIT_OFF[u]:UNIT_OFF[u] + sz], in_=wslot[slot][:, 0:sz]), reads=[wb_], writes=[B("scr%d" % u)], sem="scr%d" % u)
                conv_done_units.add(si_)

        def pump():
            while conv_issued:
                conv_convert()
            conv_issue()
            conv_issue()

        def conv_flush(si_):
            guard = 0
            while si_ not in conv_done_units:
                pump()
                guard += 1
                assert guard < 10000

        S.barrier(arena_bufs)

        units_per_group = list(range(0, 4 * min(n_layers, 2))) + ([8, 9] if n_layers >= 3 else []) + ([10, 11] if n_layers >= 4 else [])
        useq = [(g, u) for g in range(NG) for u in units_per_group]
        loaded = [0]

        for si_, (g_, u_) in enumerate(useq):
            if g_ == 0:
                pcs = segs_of_unit[u_]
                for i_, (off, m, gi, kt) in enumerate(pcs):
                    conv_queue.append((si_, u_, off, m, gi, kt, i_ == len(pcs) - 1))

        def load_unit(si):
            if si >= len(useq):
                return
            g, u = useq[si]
            if g == 0:
                return
            slot = si % 2
            sz = UNIT_SIZES[u]
            half = sz // 2
            S.dma("sync", [lambda e, u=u, slot=slot, half=half: e.dma_start(out=wslot[slot][:, 0:half], in_=scr_d[:, UNIT_OFF[u]:UNIT_OFF[u] + half]),
                           lambda e, u=u, slot=slot, half=half, sz=sz: e.dma_start(out=wslot[slot][:, half:sz], in_=scr_d[:, UNIT_OFF[u] + half:UNIT_OFF[u] + sz])],
                  reads=[B("scr%d" % u)], writes=[B("wslot%d" % slot)], sem="w%d" % slot)

        seqpos = [0]
        issued = [0]

        def prefetch_upto(si):
            while issued[0] <= si:
                load_unit(issued[0])
                issued[0] += 1

        def take_unit():
            si = seqpos[0]
            prefetch_upto(si)
            if si < len(useq) and useq[si][0] == 0:
                conv_limit[0] = max(conv_limit[0], si)
                conv_flush(si)
            seqpos[0] += 1
            return si

        def begin_unit():
            si = take_unit()
            prefetch_upto(si + 1)
            conv_limit[0] = max(conv_limit[0], si + 1)
            return si % 2

        def norm_phase(xnT, xn, junk, junkname="junk"):
            for c in range(4):
                xb = B("x%d" % c)
                S.op("act", lambda e, c=c: e.activation(out=junk, in_=xg[:, c, :], func=AF.Square, accum_out=small[:, SM_SS + c:SM_SS + c + 1]), reads=[xb], writes=[AB(junkname), B("ss%d" % c)])
            for c in range(4):
                S.op("act", lambda e, c=c: e.activation(out=small[:, SM_LN + c:SM_LN + c + 1], in_=small[:, SM_SS + c:SM_SS + c + 1], func=AF.Ln, scale=1.0 / D, bias=cst[:, C_EPS:C_EPS + 1]), reads=[B("ss%d" % c), B("cst")], writes=[B("ln%d" % c)])
            for c in range(4):
                S.op("act", lambda e, c=c: e.activation(out=small[:, SM_RS + c:SM_RS + c + 1], in_=small[:, SM_LN + c:SM_LN + c + 1], func=AF.Exp, scale=-0.5), reads=[B("ln%d" % c)], writes=[B("rs%d" % c)])
            for c in range(4):
                xb = B("x%d" % c)
                xs = xn[c % 2]
                S.op("dve", lambda e, c=c, xs=xs: e.tensor_scalar(out=xs, in0=xg[:, c, :], scalar1=small[:, SM_RS + c:SM_RS + c + 1], scalar2=None, op0=ALU.mult), reads=[xb, B("rs%d" % c)], writes=[AB("xn%d" % (c % 2))])
                bk = next_bank()
                pv = pbank_bf(bk)
                for kt in range(8):
                    S.op("pe", lambda e, kt=kt, pv=pv, xs=xs: e.transpose(out=pv[:, kt * 128:(kt + 1) * 128], in_=xs[:, kt * 128:(kt + 1) * 128], identity=ident[:]), reads=[AB("xn%d" % (c % 2)), B("ident")], writes=[*PB(bk)])
                S.op("act", lambda e, c=c, pv=pv: e.activation(out=xnT[:, :, c * 128:(c + 1) * 128], in_=pv.rearrange("p (k t) -> p k t", k=8), func=AF.Copy), reads=[*PB(bk)], writes=[AB("xnT")])

        def load_x_chunk(g, c):
            r0 = (g * 4 + c) * 128
            S.dma("sync", lambda e, c=c, r0=r0: e.dma_start(out=xg[:, c, :], in_=x_d[r0:r0 + 128, :]), writes=[B("x%d" % c)], sem="x%d" % c)

        def store_x_chunk(g, c):
            r0 = (g * 4 + c) * 128
            S.dma("sync", lambda e, c=c, r0=r0: e.dma_start(out=y_d[r0:r0 + 128, :], in_=xg[:, c, :]), reads=[B("x%d" % c)], writes=[B("ydram%d" % c)], sem="st%d" % c)

        for c in range(4):
            load_x_chunk(0, c)

        lh_counter = [0]

        for g in range(NG):
            cv = Carver()
            xn = [cv.bf(1024), cv.bf(1024)]
            junk = cv.bf(1024)
            xnT = cv.bf(4096).rearrange("p (k t) -> p k t", k=8)
            qT = cv.bf(1024).rearrange("p (d t) -> p d t", d=2)
            qdT = cv.bf(1024).rearrange("p (d t) -> p d t", d=2)
            kT = cv.bf(1024).rearrange("p (d t) -> p d t", d=2)
            rt = [cv.f32(512) for _ in range(4)]
            cs = cv.f32(1024).rearrange("p (a t) -> p a t", a=2)
            sg = cv.f32(2048).rearrange("p (c n) -> p c n", c=4)
            k_tm = [cv.bf(256) for _ in range(2)]
            v_sb = [cv.bf(512) for _ in range(2)]
            sT = [cv.bf(128) for _ in range(2)]
            og = [cv.bf(512) for _ in range(2)]
            ogT = [cv.bf(512).rearrange("p (e t) -> p e t", e=4) for _ in range(2)]

            S.dma("sync", [lambda e, g=g: e.dma_start(out=cs[:, 0, :], in_=cos_d[:, g * 512:(g + 1) * 512]),
                           lambda e, g=g: e.dma_start(out=cs[:, 1, :], in_=sin_d[:, g * 512:(g + 1) * 512])], writes=[AB("cs")], sem="cs")

            for L in range(min(n_layers, 2)):
                norm_phase(xnT, xn, junk)
                for h in range(4):
                    slot = begin_unit()
                    W = wslot[slot]
                    wb = B("wslot%d" % slot)
                    Win = W[:, 0:12288].rearrange("p (k c) -> p k c", k=8)
                    Wout = W[:, 12288:16384].rearrange("p (e n) -> p e n", e=4)
                    sbs = lh_counter[0] % 2
                    lh_counter[0] += 1
                    stf = B("state_f%d_%d" % (L, h))
                    stb_ = B("state_b%d" % sbs)
                    if g > 0:
                        S.op("pool", lambda e, L=L, h=h, sbs=sbs: e.tensor_copy(out=state_b[:, sbs, :, :], in_=state_f[:, L, h, :, :]), reads=[stf], writes=[stb_])
                    for tile in range(4):
                        for kt in range(8):
                            S.op("pe", lambda e, tile=tile, kt=kt, Win=Win: e.matmul(pbank(tile), lhsT=Win[:, kt, tile * 128:(tile + 1) * 128], rhs=xnT[:, kt, :], start=(kt == 0), stop=(kt == 7)), reads=[wb, AB("xnT")], writes=[*PB(tile)])
                    for c in range(4):
                        for kt in range(8):
                            S.op("pe", lambda e, c=c, kt=kt, Win=Win: e.matmul(pbank(4 + c), lhsT=xnT[:, kt, c * 128:(c + 1) * 128], rhs=Win[:, kt, 1024:1536], start=(kt == 0), stop=(kt == 7)), reads=[wb, AB("xnT")], writes=[*PB(4 + c)])
                        S.op("act", lambda e, c=c: e.activation(out=sg[:, c, :], in_=pbank(4 + c), func=AF.Silu), reads=[*PB(4 + c)], writes=[AB("sg%d" % c)])
                    for (pe_, po_, dst, nm) in ((0, 1, qT, "qT"), (2, 3, kT, "kT")):
                        S.op("dve", lambda e, pe_=pe_: e.tensor_tensor(out=rt[0], in0=pbank(pe_), in1=cs[:, 0, :], op=ALU.mult), reads=[*PB(pe_), AB("cs")], writes=[AB("rt0")])
                        S.op("dve", lambda e, po_=po_: e.tensor_tensor(out=rt[1], in0=pbank(po_), in1=cs[:, 1, :], op=ALU.mult), reads=[*PB(po_), AB("cs")], writes=[AB("rt1")])
                        S.op("pool", lambda e, dst=dst: e.tensor_tensor(out=dst[:, 0, :], in0=rt[0], in1=rt[1], op=ALU.subtract), reads=[AB("rt0"), AB("rt1")], writes=[AB(nm)])
                        S.op("dve", lambda e, po_=po_: e.tensor_tensor(out=rt[2], in0=pbank(po_), in1=cs[:, 0, :], op=ALU.mult), reads=[*PB(po_), AB("cs")], writes=[AB("rt2")])
                        S.op("dve", lambda e, pe_=pe_: e.tensor_tensor(out=rt[3], in0=pbank(pe_), in1=cs[:, 1, :], op=ALU.mult), reads=[*PB(pe_), AB("cs")], writes=[AB("rt3")])
                        S.op("pool", lambda e, dst=dst: e.tensor_tensor(out=dst[:, 1, :], in0=rt[2], in1=rt[3], op=ALU.add), reads=[AB("rt2"), AB("rt3")], writes=[AB(nm)])
                    for dt in range(2):
                        S.op("pool", lambda e, dt=dt, h=h: e.tensor_tensor(
                            out=qdT[:, dt, :].rearrange("p (c i) -> p c i", c=4),
                            in0=qT[:, dt, :].rearrange("p (c i) -> p c i", c=4),
                            in1=cst[:, C_QD + h * 128:C_QD + h * 128 + 128].unsqueeze(1).to_broadcast([128, 4, 128]),
                            op=ALU.mult), reads=[AB("qT"), B("cst")], writes=[AB("qdT")])
                    ktr = pbank_bf(3)[:, 0:256]
                    scps = pbank(2)[:, 128:256]
                    ogtr = pbank_bf(2)[:, 512:1024]
                    OB = 1
                    last_gc = S_len // 128 - 1

                    def stageA(c, L=L, h=h, Win=Win, wb=wb):
                        gc = g * 4 + c
                        cb = c % 2
                        csl = slice(c * 128, (c + 1) * 128)
                        if gc < last_gc:
                            for dt in range(2):
                                S.op("pe", lambda e, dt=dt, csl=csl: e.transpose(out=ktr[:, dt * 128:(dt + 1) * 128], in_=kT[:, dt, csl], identity=ident[:]), reads=[AB("kT"), B("ident")], writes=[*PB(3)])
                            S.op("act", lambda e, cb=cb, h=h: e.activation(out=k_tm[cb], in_=ktr, func=AF.Copy, scale=cst[:, C_KD + h:C_KD + h + 1]), reads=[*PB(3), B("cst")], writes=[AB("k_tm%d" % cb)])
                        for kt in range(8):
                            S.op("pe", lambda e, kt=kt, csl=csl, Win=Win: e.matmul(pbank(0), lhsT=xnT[:, kt, csl], rhs=Win[:, kt, 512:1024], start=(kt == 0), stop=(kt == 7)), reads=[wb, AB("xnT")], writes=[*PB(0)])
                        S.op("act", lambda e, cb=cb: e.activation(out=v_sb[cb], in_=pbank(0), func=AF.Copy), reads=[*PB(0)], writes=[AB("v_sb%d" % cb)])
                        for dt in range(2):
                            S.op("pe", lambda e, dt=dt, csl=csl: e.matmul(scps, lhsT=kT[:, dt, csl], rhs=qT[:, dt, csl], start=(dt == 0), stop=(dt == 1)), reads=[AB("kT"), AB("qT")], writes=[*PB(2)])
                        S.op("dve", lambda e, cb=cb, h=h: e.tensor_tensor(out=sT[cb], in0=scps, in1=cst[:, C_DT + h * 128:C_DT + h * 128 + 128], op=ALU.mult), reads=[*PB(2), B("cst")], writes=[AB("sT%d" % cb)])

                    def stageB(c, L=L, h=h, sbs=sbs, stf=stf, stb_=stb_):
                        gc = g * 4 + c
                        cb = c % 2
                        csl = slice(c * 128, (c + 1) * 128)
                        has_inter = gc > 0
                        S.op("pe", lambda e, cb=cb, has_inter=has_inter: e.matmul(pbank(OB), lhsT=sT[cb], rhs=v_sb[cb], start=True, stop=(not has_inter)), reads=[AB("sT%d" % cb), AB("v_sb%d" % cb)], writes=[*PB(OB)])
                        if has_inter:
                            for dt in range(2):
                                S.op("pe", lambda e, dt=dt, csl=csl, sbs=sbs: e.matmul(pbank(OB), lhsT=qdT[:, dt, csl], rhs=state_b[:, sbs, dt, :], start=False, stop=(dt == 1)), reads=[AB("qdT"), stb_], writes=[*PB(OB)])
                        if gc < last_gc:
                            for dt in range(2):
                                S.op("pe", lambda e, dt=dt, cb=cb: e.matmul(pbank(4 + dt), lhsT=k_tm[cb][:, dt * 128:(dt + 1) * 128], rhs=v_sb[cb], start=True, stop=True), reads=[AB("k_tm%d" % cb), AB("v_sb%d" % cb)], writes=[*PB(4 + dt)])
                                S.op("dve", lambda e, dt=dt, L=L, h=h: e.scalar_tensor_tensor(out=state_f[:, L, h, dt, :], in0=state_f[:, L, h, dt, :], scalar=CHUNK_DECAY[h], in1=pbank(4 + dt), op0=ALU.mult, op1=ALU.add), reads=[stf, *PB(4 + dt)], writes=[stf])
                            S.op("pool", lambda e, L=L, h=h, sbs=sbs: e.tensor_copy(out=state_b[:, sbs, :, :], in_=state_f[:, L, h, :, :]), reads=[stf], writes=[stb_])
                        S.op("act", lambda e, cb=cb: e.activation(out=junk[:, 0:512], in_=pbank(OB), func=AF.Square, accum_out=small[:, SM_SSO + cb:SM_SSO + cb + 1]), reads=[*PB(OB)], writes=[AB("junk"), B("sso%d" % cb)])
                        S.op("act", lambda e, cb=cb: e.activation(out=small[:, SM_LNO + cb:SM_LNO + cb + 1], in_=small[:, SM_SSO + cb:SM_SSO + cb + 1], func=AF.Ln, scale=1.0 / 512, bias=cst[:, C_EPS:C_EPS + 1]), reads=[B("sso%d" % cb), B("cst")], writes=[B("lno%d" % cb)])
                        S.op("act", lambda e, cb=cb: e.activation(out=small[:, SM_RSO + cb:SM_RSO + cb + 1], in_=small[:, SM_LNO + cb:SM_LNO + cb + 1], func=AF.Exp, scale=-0.5), reads=[B("lno%d" % cb)], writes=[B("rso%d" % cb)])
                        S.op("dve", lambda e, cb=cb, c=c: e.scalar_tensor_tensor(out=og[cb], in0=pbank(OB), scalar=small[:, SM_RSO + cb:SM_RSO + cb + 1], in1=sg[:, c, :], op0=ALU.mult, op1=ALU.mult), reads=[*PB(OB), B("rso%d" % cb), AB("sg%d" % c)], writes=[AB("og%d" % cb)])

                    def stageC(c, L=L, h=h, Wout=Wout, wb=wb):
                        cb = c % 2
                        for et in range(4):
                            S.op("pe", lambda e, et=et, cb=cb: e.transpose(out=ogtr[:, et * 128:(et + 1) * 128], in_=og[cb][:, et * 128:(et + 1) * 128], identity=ident[:]), reads=[AB("og%d" % cb), B("ident")], writes=[*PB(2)])
                        S.op("act", lambda e, cb=cb: e.activation(out=ogT[cb], in_=ogtr.rearrange("p (e t) -> p e t", e=4), func=AF.Copy), reads=[*PB(2)], writes=[AB("ogT%d" % cb)])
                        for nh in range(2):
                            for et in range(4):
                                S.op("pe", lambda e, nh=nh, et=et, cb=cb, Wout=Wout: e.matmul(pbank(6 + nh), lhsT=ogT[cb][:, et, :], rhs=Wout[:, et, nh * 512:(nh + 1) * 512], start=(et == 0), stop=(et == 3)), reads=[AB("ogT%d" % cb), wb], writes=[*PB(6 + nh)])
                            S.op("dve", lambda e, nh=nh, c=c: e.tensor_tensor(out=xg[:, c, nh * 512:(nh + 1) * 512], in0=pbank(6 + nh), in1=xg[:, c, nh * 512:(nh + 1) * 512], op=ALU.add), reads=[*PB(6 + nh), B("x%d" % c)], writes=[B("x%d" % c)])
                        if n_layers <= 2 and L == n_layers - 1 and h == 3:
                            store_x_chunk(g, c)
                            if g + 1 < NG:
                                load_x_chunk(g + 1, c)

                    for c in range(4):
                        stageA(c)
                        pump()
                        if c >= 1:
                            stageB(c - 1)
                            pump()
                        if c >= 2:
                            stageC(c - 2)
                            pump()
                    stageB(3)
                    pump()
                    stageC(2)
                    pump()
                    stageC(3)
                    pump()

            if n_layers <= 2:
                continue
            S.barrier(arena_bufs)

            cv = Carver()
            xnS = [cv.bf(1024), cv.bf(1024)]
            xnTS = cv.bf(4096).rearrange("p (k t) -> p k t", k=8)
            qTs = cv.bf(4096).rearrange("p (t n) -> p t n", t=8)
            sq = cv.bf(512)
            lnr = cv.f32(512)
            rstd = cv.f32(512)
            sgT = cv.bf(4096).rearrange("p (t n) -> p t n", t=8)
            junkS = cv.bf(1024)
            pT = [cv.bf(2048).rearrange("p (b n) -> p b n", b=2) for _ in range(2)]
            dens = [cv.f32(512) for _ in range(2)]
            ogTs2 = [cv.bf(1024).rearrange("p (t i) -> p t i", t=8) for _ in range(2)]
            r = g % 2

            def qknorm(bk, nm):
                S.op("act", lambda e: e.activation(out=sq, in_=pbank(bk), func=AF.Square), reads=[*PB(bk)], writes=[AB("sq")])
                b2 = next_bank()
                if b2 == bk:
                    b2 = next_bank()
                S.op("pe", lambda e, b2=b2: e.matmul(pbank(b2), lhsT=bones[:], rhs=sq, start=True, stop=True), reads=[AB("sq"), B("bones")], writes=[*PB(b2)])
                S.op("act", lambda e, b2=b2: e.activation(out=lnr, in_=pbank(b2), func=AF.Ln, scale=1.0 / 64, bias=cst[:, C_EPS:C_EPS + 1]), reads=[*PB(b2), B("cst")], writes=[AB("lnr")])
                S.op("act", lambda e: e.activation(out=rstd, in_=lnr, func=AF.Exp, scale=-0.5), reads=[AB("lnr")], writes=[AB("rstd")])

            for j in range(n_layers - 2):
                siA = take_unit()
                prefetch_upto(siA + 1)
                siB = take_unit()
                slotA = siA % 2
                slotB = siB % 2
                WA = wslot[slotA]
                WBt = wslot[slotB]
                wa = B("wslot%d" % slotA)
                wbb = B("wslot%d" % slotB)
                Wq = WA[:, :].rearrange("p (k c) -> p k c", k=8)
                Wo = WBt[:, 0:8192].rearrange("p (k n) -> p k n", k=8)
                norm_phase(xnTS, xnS, junkS, "junkS")
                if j == 0:
                    Wkv = WBt[:, 8192:11264].rearrange("p (k c) -> p k c", k=8)
                    for gg in range(2):
                        bk = next_bank()
                        for kt in range(8):
                            S.op("pe", lambda e, gg=gg, kt=kt, bk=bk, Wkv=Wkv: e.matmul(pbank(bk), lhsT=Wkv[:, kt, gg * 128:(gg + 1) * 128], rhs=xnTS[:, kt, :], start=(kt == 0), stop=(kt == 7)), reads=[wbb, AB("xnT")], writes=[*PB(bk)])
                        qknorm(bk, "k")
                        for par in range(2):
                            psl = slice(par * 64, par * 64 + 64)
                            S.op("dve", lambda e, gg=gg, par=par, psl=psl, bk=bk, r=r: e.scalar_tensor_tensor(
                                out=kz[psl, par, gg, r * 512:(r + 1) * 512], in0=pbank(bk)[psl, :], scalar=cst[psl, C_GK:C_GK + 1], in1=rstd[psl, :],
                                op0=ALU.mult, op1=ALU.mult), reads=[*PB(bk), AB("rstd"), B("cst")], writes=[B("kz")])
                    for c in range(4):
                        bk = next_bank()
                        vslot = r * 4 + c
                        for kt in range(8):
                            S.op("pe", lambda e, c=c, kt=kt, bk=bk, Wkv=Wkv: e.matmul(pbank(bk)[:, 0:128], lhsT=xnTS[:, kt, c * 128:(c + 1) * 128], rhs=Wkv[:, kt, 256:384], start=(kt == 0), stop=(kt == 7)), reads=[wbb, AB("xnT")], writes=[*PB(bk)])
                        S.op("act", lambda e, bk=bk, vslot=vslot: e.activation(out=vz[:, vslot, :, 0, 0:64], in_=pbank(bk)[:, 0:128].rearrange("p (g d) -> p g d", g=2), func=AF.Copy), reads=[*PB(bk)], writes=[B("vz")])
                        S.op("dve", lambda e, bk=bk, vslot=vslot: e.tensor_copy(out=vz[:, vslot, :, 1, 64:128], in_=pbank(bk)[:, 0:128].rearrange("p (g d) -> p g d", g=2)), reads=[*PB(bk)], writes=[B("vz")])
                def q_finish(t, bk, j=j):
                    b2 = next_bank()
                    if b2 == bk:
                        b2 = next_bank()
                    S.op("pe", lambda e, b2=b2: e.matmul(pbank(b2), lhsT=bones[:], rhs=sq, start=True, stop=True), reads=[AB("sq"), B("bones")], writes=[*PB(b2)])
                    S.op("act", lambda e, b2=b2: e.activation(out=lnr, in_=pbank(b2), func=AF.Ln, scale=1.0 / 64, bias=cst[:, C_EPS:C_EPS + 1]), reads=[*PB(b2), B("cst")], writes=[AB("lnr")])
                    S.op("act", lambda e: e.activation(out=rstd, in_=lnr, func=AF.Exp, scale=-0.5), reads=[AB("lnr")], writes=[AB("rstd")])
                    S.op("dve", lambda e, t=t, bk=bk, j=j: e.scalar_tensor_tensor(out=qTs[:, t, :], in0=pbank(bk), scalar=cst[:, C_GQ + j:C_GQ + j + 1], in1=rstd, op0=ALU.mult, op1=ALU.mult), reads=[*PB(bk), AB("rstd"), B("cst")], writes=[AB("qTs")])

                prevq = None
                for t in range(8):
                    bk = next_bank()
                    if prevq is not None and bk == prevq[1]:
                        bk = next_bank()
                    for kt in range(8):
                        S.op("pe", lambda e, t=t, kt=kt, bk=bk, Wq=Wq: e.matmul(pbank(bk), lhsT=Wq[:, kt, t * 128:(t + 1) * 128], rhs=xnTS[:, kt, :], start=(kt == 0), stop=(kt == 7)), reads=[wa, AB("xnT")], writes=[*PB(bk)])
                    if prevq is not None:
                        q_finish(*prevq)
                    pump()
                    S.op("act", lambda e, bk=bk: e.activation(out=sq, in_=pbank(bk), func=AF.Square), reads=[*PB(bk)], writes=[AB("sq")])
                    prevq = (t, bk)
                q_finish(*prevq)
                for t in range(8):
                    bk = next_bank()
                    for kt in range(8):
                        S.op("pe", lambda e, t=t, kt=kt, bk=bk, Wq=Wq: e.matmul(pbank(bk), lhsT=Wq[:, kt, 1024 + t * 128:1024 + (t + 1) * 128], rhs=xnTS[:, kt, :], start=(kt == 0), stop=(kt == 7)), reads=[wa, AB("xnT")], writes=[*PB(bk)])
                    S.op("act", lambda e, t=t, bk=bk: e.activation(out=sgT[:, t, :], in_=pbank(bk), func=AF.Silu), reads=[*PB(bk)], writes=[AB("sgT")])
                    pump()
                prefetch_upto(siA + 2)
                conv_limit[0] = max(conv_limit[0], siA + 2)
                scnt = [0]

                def unit_info(c):
                    gc = g * 4 + c
                    cur_tok = r * 512 + c * 128
                    prev_tok = (cur_tok - 128) % 1024
                    cur_slot = r * 4 + c
                    prev_slot = (cur_slot - 1) % 8
                    return [(0, prev_tok, prev_slot), (1, cur_tok, cur_slot)] if gc > 0 else [(1, cur_tok, cur_slot)]

                def SA(c, gg, j=j):
                    csl = slice(c * 128, (c + 1) * 128)
                    for bi, (blk, btok, bslot) in enumerate(unit_info(c)):
                        for par in range(2):
                            bk = par
                            S.op("pe", lambda e, par=par, gg=gg, btok=btok, bk=bk, csl=csl: e.matmul(pbank(bk), lhsT=kz[:, par, gg, btok:btok + 128], rhs=qTs[:, 4 * gg:4 * gg + 4, csl], start=True, stop=False), reads=[B("kz"), AB("qTs")], writes=[*PB(bk)])
                            S.op("pe", lambda e, par=par, gg=gg, blk=blk, bk=bk: e.matmul(pbank(bk), lhsT=ident[:], rhs=expB[:, blk, gg, par * 512:(par + 1) * 512], start=False, stop=True), reads=[B("ident"), B("expB")], writes=[*PB(bk)])
                        for par in range(2):
                            S.op("act", lambda e, j=j, par=par, gg=gg, bi=bi: e.activation(out=pT[gg][:, bi, par * 512:(par + 1) * 512], in_=pbank(par), func=AF.Exp, scale=0.125, bias=small[:, SM_NMQ + j:SM_NMQ + j + 1]), reads=[*PB(par), B("nmq")], writes=[AB("pT%d" % gg)])

                def SB(c, gg, j=j):
                    csl = slice(c * 128, (c + 1) * 128)
                    blks = unit_info(c)
                    og_ = ogTs2[c % 2]
                    ob = 2 + 2 * ((2 * c + gg) % 2)
                    nmm = 2 * len(blks)
                    k_ = 0
                    for par in range(2):
                        for bi, (blk, btok, bslot) in enumerate(blks):
                            S.op("pe", lambda e, par=par, bi=bi, bslot=bslot, gg=gg, k_=k_, nmm=nmm, ob=ob: e.matmul(pbank(ob), lhsT=vz[:, bslot, gg, par, :], rhs=pT[gg][:, bi, par * 512:(par + 1) * 512], start=(k_ == 0), stop=(k_ == nmm - 1)), reads=[B("vz"), AB("pT%d" % gg)], writes=[*PB(ob)])
                            k_ += 1
                    k_ = 0
                    for par in range(2):
                        for bi, (blk, btok, bslot) in enumerate(blks):
                            S.op("pe", lambda e, par=par, bi=bi, gg=gg, k_=k_, nmm=nmm, ob=ob: e.matmul(pbank(ob + 1), lhsT=onesz[:, par, :], rhs=pT[gg][:, bi, par * 512:(par + 1) * 512], start=(k_ == 0), stop=(k_ == nmm - 1)), reads=[B("onesz"), AB("pT%d" % gg)], writes=[*PB(ob + 1)])
                            k_ += 1
                    S.op("dve", lambda e, gg=gg, j=j, ob=ob: e.tensor_tensor(
                        out=dens[gg].rearrange("p (t i) -> p t i", t=4), in0=pbank(ob + 1).rearrange("p (t i) -> p t i", t=4),
                        in1=small[:, SM_SE + 8 * j + 4 * gg:SM_SE + 8 * j + 4 * gg + 4].unsqueeze(2).to_broadcast([128, 4, 128]), op=ALU.add),
                        reads=[*PB(ob + 1), B("sinkexp")], writes=[AB("dens%d" % gg)])
                    S.op("act", lambda e, gg=gg: e.activation(out=dens[gg], in_=dens[gg], func=AF.Ln), reads=[AB("dens%d" % gg)], writes=[AB("dens%d" % gg)])
                    S.op("act", lambda e, gg=gg: e.activation(out=dens[gg], in_=dens[gg], func=AF.Exp, scale=-1.0), reads=[AB("dens%d" % gg)], writes=[AB("dens%d" % gg)])
                    S.op("pool", lambda e, gg=gg, csl=csl: e.tensor_tensor(out=dens[gg].rearrange("p (t i) -> p t i", t=4), in0=dens[gg].rearrange("p (t i) -> p t i", t=4), in1=sgT[:, 4 * gg:4 * gg + 4, csl], op=ALU.mult), reads=[AB("dens%d" % gg), AB("sgT")], writes=[AB("dens%d" % gg)])
                    S.op("dve", lambda e, gg=gg, og_=og_, ob=ob: e.tensor_tensor(out=og_[:, 4 * gg:4 * gg + 4, :], in0=pbank(ob).rearrange("p (t i) -> p t i", t=4), in1=dens[gg].rearrange("p (t i) -> p t i", t=4), op=ALU.mult), reads=[*PB(ob), AB("dens%d" % gg)], writes=[AB("ogTs%d" % (c % 2))])

                def SC(c, j=j, Wo=Wo, wbb=wbb):
                    og_ = ogTs2[c % 2]
                    for nh in range(2):
                        for t in range(8):
                            S.op("pe", lambda e, nh=nh, t=t, Wo=Wo, og_=og_: e.matmul(pbank(6 + nh), lhsT=og_[:, t, :], rhs=Wo[:, t, nh * 512:(nh + 1) * 512], start=(t == 0), stop=(t == 7)), reads=[AB("ogTs%d" % (c % 2)), wbb], writes=[*PB(6 + nh)])
                        S.op("dve", lambda e, nh=nh, c=c: e.tensor_tensor(out=xg[:, c, nh * 512:(nh + 1) * 512], in0=pbank(6 + nh), in1=xg[:, c, nh * 512:(nh + 1) * 512], op=ALU.add), reads=[*PB(6 + nh), B("x%d" % c)], writes=[B("x%d" % c)])
                    if j == n_layers - 3:
                        store_x_chunk(g, c)
                        if g + 1 < NG:
                            load_x_chunk(g + 1, c)

                units = [(c, gg) for c in range(4) for gg in range(2)]
                pendC = []
                for i, (c, gg) in enumerate(units):
                    SA(c, gg)
                    pump()
                    if i >= 1:
                        pc, pg = units[i - 1]
                        if pendC:
                            SC(pendC.pop(0))
                            pump()
                        SB(pc, pg)
                        pump()
                        if pg == 1:
                            pendC.append(pc)
                if pendC:
                    SC(pendC.pop(0))
                SB(*units[-1])
                SC(units[-1][0])
            S.barrier(arena_bufs)

        if debug:
            S.dma("sync", lambda e: e.dma_start(out=dbg_kz[:, :], in_=kz[:].rearrange("p a b c -> p (a b c)")), reads=[B("kz")], writes=[B("dbg1")], sem="dbg")
            S.dma("sync", lambda e: e.dma_start(out=dbg_vz[:, :], in_=vz[:].rearrange("p a b c d -> p (a b c d)")), reads=[B("vz")], writes=[B("dbg2")], sem="dbg")
            S.dma("sync", lambda e: e.dma_start(out=dbg_q[:, :], in_=qTs.rearrange("p a b -> p (a b)")), reads=[AB("qTs")], writes=[B("dbg3")], sem="dbg")
            S.dma("sync", lambda e: e.dma_start(out=dbg_sg[:, :], in_=sgT.rearrange("p a b -> p (a b)")), reads=[AB("sgT")], writes=[B("dbg4")], sem="dbg")
            S.wait_for("sync", [B("dbg1"), B("dbg2"), B("dbg3"), B("dbg4")])
        S.wait_for("sync", [B("ydram%d" % c) for c in range(4)])
        S.emit()
    return nc


def _t5_bucket(dist):
    max_exact = 16
    dist_f = np.maximum(dist, 1).astype(np.float32)
    large = max_exact + (np.log(dist_f / np.float32(max_exact)) / np.float32(math.log(128 / max_exact)) * np.float32(32 - max_exact)).astype(np.int32)
    large = np.minimum(large, 31)
    return np.where(dist < max_exact, dist, large)


def host_prep(inputs, S_len):
    f32 = np.float32
    a_w_in = np.asarray(inputs["a_w_in"], f32)
    a_w_out = np.asarray(inputs["a_w_out"], f32)
    w_kv = np.asarray(inputs["w_kv"], f32)
    b_w_in = np.asarray(inputs["b_w_in"], f32)
    b_w_out = np.asarray(inputs["b_w_out"], f32)
    wall = np.empty((128, WTOT), f32)

    def pk(w):
        n = w.shape[1]
        return w.reshape(8, 128, n).transpose(1, 0, 2).reshape(128, 8 * n)

    for L in range(2):
        for h in range(4):
            u = L * 4 + h
            W = a_w_in[L]
            q = W[:, h * 256:(h + 1) * 256]
            k = W[:, 1024 + h * 256:1024 + (h + 1) * 256]
            v = W[:, 2048 + h * 512:2048 + (h + 1) * 512]
            gt = W[:, 4096 + h * 512:4096 + (h + 1) * 512]
            blk = np.concatenate([q[:, 0::2], q[:, 1::2], k[:, 0::2], k[:, 1::2], v, gt], axis=1)
            wall[:, UNIT_OFF[u]:UNIT_OFF[u] + 12288] = pk(blk)
            wo = a_w_out[L][h * 512:(h + 1) * 512, :]
            wall[:, UNIT_OFF[u] + 12288:UNIT_OFF[u] + 16384] = wo.reshape(4, 128, 1024).transpose(1, 0, 2).reshape(128, 4096)
    for j in range(2):
        ua = 8 + 2 * j
        ub = 9 + 2 * j
        wall[:, UNIT_OFF[ua]:UNIT_OFF[ua] + 16384] = pk(b_w_in[j])
        wall[:, UNIT_OFF[ub]:UNIT_OFF[ub] + 8192] = pk(b_w_out[j])
        if j == 0:
            kk = w_kv[:, 0:128]
            vv = w_kv[:, 128:256]
            blk = np.concatenate([kk[:, 0:64], kk[:, 0:64], kk[:, 64:128], kk[:, 64:128], vv], axis=1)
            wall[:, UNIT_OFF[ub] + 8192:UNIT_OFF[ub] + 11264] = pk(blk)

    cst = np.zeros((128, NA), f32)
    cst[:, C_IDENT:C_IDENT + 128] = np.eye(128, dtype=f32)
    bo = np.zeros((128, 128), f32)
    bo[0:64, 0:64] = 1.0
    bo[64:128, 64:128] = 1.0
    cst[:, C_BONES:C_BONES + 128] = bo
    idx = np.arange(128)
    for h in range(4):
        lg = math.log(GAMMA[h])
        diff = idx[None, :] - idx[:, None]
        dt_ = np.where(diff >= 0, np.exp(np.maximum(diff, 0) * lg), 0.0) / 16.0
        cst[:, C_DT + h * 128:C_DT + (h + 1) * 128] = dt_.astype(f32)
        cst[:, C_QD + h * 128:C_QD + (h + 1) * 128] = np.exp((idx + 1.0) * lg).astype(f32)[None, :]
        cst[:, C_KD + h] = (np.exp((127.0 - idx) * lg) / 16.0).astype(f32)
    jj = idx[:, None]
    ii = idx[None, :]
    cst[:, C_MASK:C_MASK + 128] = (jj > ii).astype(f32)
    cst[:, C_MASK + 128:C_MASK + 256] = (jj <= ii).astype(f32)
    gains = [inputs["a_norm_g"][0], inputs["a_norm_g"][1], inputs["kv_norm_g"], inputs["b_norm_g"][0], inputs["b_norm_g"][1]]
    for n, gv in enumerate(gains):
        cst[:, C_G + n * 8:C_G + n * 8 + 8] = np.asarray(gv, f32).reshape(8, 128).T
    p64 = idx % 64
    par = idx // 64
    for j in range(2):
        gq = np.asarray(inputs["b_q_norm_g"][j], f32)
        cst[:, C_GQ + j] = gq[p64]
        cst[:, C_GQR + j * 64:C_GQR + (j + 1) * 64] = gq[None, :]
        sk = np.asarray(inputs["b_sinks"][j], f32)
        for gg in range(2):
            for t in range(4):
                cst[:, C_SINK + j * 8 + gg * 4 + t] = sk[8 * gg + 2 * t + par]
    gk = np.asarray(inputs["k_norm_g"], f32)
    cst[:, C_GK] = gk[p64]
    cst[:, C_GKR:C_GKR + 64] = gk[None, :]
    cst[:, C_EPS] = EPS

    rel_bias = np.asarray(inputs["rel_bias"], f32)
    biasT = np.zeros((128, 2, 2, 2, 4, 128), f32)
    for blk in range(2):
        dist = (ii + 128 - jj) if blk == 0 else (ii - jj)
        bucket = _t5_bucket(np.maximum(dist, 0))
        for gg in range(2):
            for pr in range(2):
                for t in range(4):
                    hh = 8 * gg + 2 * t + pr
                    biasT[:, blk, gg, pr, t, :] = rel_bias[bucket, hh]
    biasT = biasT.reshape(128, 4096)

    angle = (1.0 / (np.float32(10000.0) ** np.linspace(0.0, 1.0, 128, dtype=f32))).astype(f32)
    pos = np.arange(S_len, dtype=f32)
    ang = (angle[:, None] * pos[None, :]).astype(f32)
    cosT = np.cos(ang).astype(f32)
    sinT = np.sin(ang).astype(f32)
    return {"wall": wall, "cst": cst, "biasT": biasT, "cosT": cosT, "sinT": sinT}


_CACHE = {}


def kernel(**inputs):
    x = np.asarray(inputs["x"], np.float32)
    nb, S_len, _ = x.shape
    shared = host_prep(inputs, S_len)
    key = (S_len, 4)
    if key not in _CACHE:
        _CACHE[key] = build_program(S_len, 4)
    nc = _CACHE[key]
    in_maps = []
    for b in range(nb):
        m = dict(shared)
        m["x"] = np.ascontiguousarray(x[b])
        in_maps.append(m)
    res = run_bass_kernel_spmd(nc, in_maps, core_ids=list(range(nb)))
    out = np.stack([np.asarray(r["y"], np.float32) for r in res.results], axis=0)
    return out
```
